# Optimizing a Trainium2 kernel written in Bass

```python
import jax, jax.numpy as jnp
from jax import lax
import numpy as np

D_MODEL = 1024
BATCH = 32
SEQ = 2048
DEPTH = 2
DEC_BATCH = 16
DEC_SEQ = 2048
PAST_LEN = 128

GRID_W = 64
ROPE_THETA = 10000.0
EPS = 1e-6
NEG = -1e30
Q_BLOCK = 128
HEAD_DIM = 64
N_BRANCH = 4
BRANCH_W = 256
A_HEADS = 4
A_NOPE = 64
A_ROPE = 32
A_V = 64
A_Q_LORA = 256
A_KV_LORA = 128
A_IN = A_Q_LORA + A_KV_LORA + A_ROPE
B_HEADS = 4
B_KV = 2
B_IN = (B_HEADS + 2 * B_KV) * HEAD_DIM
C_HEADS = 4
C_KV = 2
C_WINDOW = 128
C_BLOCK = 128
C_IN = (C_HEADS + 2 * C_KV) * HEAD_DIM
D_HEADS = 4
D_KV = 2
D_GROUPS = ((128, 1), (512, 4), (2048, 16))
D_BLOCK = 64
D_GROUP_IN = (D_HEADS + 2 * D_KV) * HEAD_DIM
D_IN = len(D_GROUPS) * D_GROUP_IN
GATE_IN = N_BRANCH * BRANCH_W
MERGE_IN = N_BRANCH * D_MODEL
N_IN = A_IN + B_IN + C_IN + D_IN + GATE_IN + MERGE_IN

kernel_name = "hybrid_gated_parallel_encoder"


def _split_cols(a, sizes):
    idx, acc = [], 0
    for s in sizes[:-1]:
        acc += s
        idx.append(acc)
    return jnp.split(a, idx, axis=-1)


def _rms_norm(x, g):
    xf = x.astype(jnp.float32)
    y = xf * lax.rsqrt(jnp.mean(xf * xf, axis=-1, keepdims=True) + EPS)
    return (y * g.astype(jnp.float32)).astype(x.dtype)


def _rope(x, pos):
    d = x.shape[-1]
    half = d // 2
    freqs = ROPE_THETA ** (-jnp.arange(half, dtype=jnp.float32) * 2.0 / d)
    ang = pos.astype(jnp.float32)[:, None] * freqs[None, :]
    ang = ang.reshape((ang.shape[0],) + (1,) * (x.ndim - 3) + (half,))
    cos, sin = jnp.cos(ang), jnp.sin(ang)
    xf = x.astype(jnp.float32)
    x1, x2 = xf[..., :half], xf[..., half:]
    return jnp.concatenate([x1 * cos - x2 * sin, x1 * sin + x2 * cos], axis=-1).astype(x.dtype)


def _axial_rope(x, rows, cols):
    half = x.shape[-1] // 2
    return jnp.concatenate([_rope(x[..., :half], rows), _rope(x[..., half:], cols)], axis=-1)


def _dense_attention(q, k, v, scale):
    bn, t = q.shape[:2]
    nb = t // Q_BLOCK
    qb = jnp.moveaxis(q.reshape((bn, nb, Q_BLOCK) + q.shape[2:]), 1, 0)

    def one(qblk):
        s = jnp.einsum('bqhgd,bkhd->bhgqk', qblk, k, preferred_element_type=jnp.float32) * scale
        p = jax.nn.softmax(s, axis=-1)
        return jnp.einsum('bhgqk,bkhd->bqhgd', p.astype(v.dtype), v)

    o = lax.map(one, qb)
    return jnp.moveaxis(o, 0, 1).reshape((bn, t) + o.shape[3:])


def _banded_attention(q, k, v, window, block, scale):
    bn, length = q.shape[:2]
    nb = -(-length // block)
    lp = nb * block
    qp = jnp.pad(q, ((0, 0), (0, lp - length)) + ((0, 0),) * (q.ndim - 2))
    kpad = ((0, 0), (block, lp - length + block), (0, 0), (0, 0))
    kb = jnp.pad(k, kpad).reshape((bn, nb + 2, block) + k.shape[2:])
    vb = jnp.pad(v, kpad).reshape((bn, nb + 2, block) + v.shape[2:])
    kw = jnp.concatenate([kb[:, :-2], kb[:, 1:-1], kb[:, 2:]], axis=2)
    vw = jnp.concatenate([vb[:, :-2], vb[:, 1:-1], vb[:, 2:]], axis=2)
    qb = qp.reshape((bn, nb, block) + q.shape[2:])

    def one(args):
        qblk, kblk, vblk, i = args
        qpos = i * block + jnp.arange(block)
        kpos = (i - 1) * block + jnp.arange(3 * block)
        valid = (jnp.abs(qpos[:, None] - kpos[None, :]) <= window) & (kpos[None, :] >= 0) & (kpos[None, :] < length)
        s = jnp.einsum('bqhgd,bkhd->bhgqk', qblk, kblk, preferred_element_type=jnp.float32) * scale
        s = jnp.where(valid, s, NEG)
        lse = jax.nn.logsumexp(s, axis=-1)
        p = jnp.exp(s - lse[..., None])
        o = jnp.einsum('bhgqk,bkhd->bqhgd', p.astype(vblk.dtype), vblk)
        return o, jnp.moveaxis(lse, 3, 1)

    o, lse = lax.map(one, (jnp.moveaxis(qb, 1, 0), jnp.moveaxis(kw, 1, 0), jnp.moveaxis(vw, 1, 0), jnp.arange(nb)))
    o = jnp.moveaxis(o, 0, 1).reshape((bn, lp) + o.shape[3:])[:, :length]
    lse = jnp.moveaxis(lse, 0, 1).reshape((bn, lp) + lse.shape[3:])[:, :length]
    return o, lse


def _mla_mixer(h, pos, q_a_norm, w_q_up, kv_a_norm, w_kv_up):
    bn, t = h.shape[:2]
    q_lat, kv_lat, k_rope = _split_cols(h, [A_Q_LORA, A_KV_LORA, A_ROPE])
    q = (_rms_norm(q_lat, q_a_norm) @ w_q_up).reshape(bn, t, A_HEADS, A_NOPE + A_ROPE)
    kv = (_rms_norm(kv_lat, kv_a_norm) @ w_kv_up).reshape(bn, t, A_HEADS, A_NOPE + A_V)
    q = jnp.concatenate([q[..., :A_NOPE], _rope(q[..., A_NOPE:], pos)], axis=-1)
    k_r = _rope(k_rope[:, :, None, :], pos)
    k = jnp.concatenate([kv[..., :A_NOPE], jnp.broadcast_to(k_r, (bn, t, A_HEADS, A_ROPE))], axis=-1)
    v = kv[..., A_NOPE:]
    o = _dense_attention(q[:, :, :, None, :], k, v, (A_NOPE + A_ROPE) ** -0.5)
    return o.reshape(bn, t, A_HEADS * A_V)


def _axial_gqa_mixer(h, rows, cols, q_norm, k_norm):
    bn, t = h.shape[:2]
    q, k, v = _split_cols(h, [B_HEADS * HEAD_DIM, B_KV * HEAD_DIM, B_KV * HEAD_DIM])
    q = _axial_rope(_rms_norm(q.reshape(bn, t, B_HEADS, HEAD_DIM), q_norm), rows, cols)
    k = _axial_rope(_rms_norm(k.reshape(bn, t, B_KV, HEAD_DIM), k_norm), rows, cols)
    q = q.reshape(bn, t, B_KV, B_HEADS // B_KV, HEAD_DIM)
    v = v.reshape(bn, t, B_KV, HEAD_DIM)
    o = _dense_attention(q, k, v, HEAD_DIM ** -0.5)
    return o.reshape(bn, t, B_HEADS * HEAD_DIM)


def _sink_window_mixer(h, pos, sink):
    bn, t = h.shape[:2]
    g = C_HEADS // C_KV
    q, k, v = _split_cols(h, [C_HEADS * HEAD_DIM, C_KV * HEAD_DIM, C_KV * HEAD_DIM])
    q = _rope(q.reshape(bn, t, C_HEADS, HEAD_DIM), pos).reshape(bn, t, C_KV, g, HEAD_DIM)
    k = _rope(k.reshape(bn, t, C_KV, HEAD_DIM), pos)
    v = v.reshape(bn, t, C_KV, HEAD_DIM)
    o, lse = _banded_attention(q, k, v, C_WINDOW, C_BLOCK, HEAD_DIM ** -0.5)
    lse_tot = jnp.logaddexp(lse, sink.astype(jnp.float32).reshape(C_KV, g))
    o = o * jnp.exp(lse - lse_tot)[..., None].astype(o.dtype)
    return o.reshape(bn, t, C_HEADS * HEAD_DIM)


def _dilated_mixer(h, pos):
    bn, t = h.shape[:2]
    g = D_HEADS // D_KV
    outs, lses = [], []
    for gi, (window, dil) in enumerate(D_GROUPS):
        hg = h[..., gi * D_GROUP_IN:(gi + 1) * D_GROUP_IN]
        q, k, v = _split_cols(hg, [D_HEADS * HEAD_DIM, D_KV * HEAD_DIM, D_KV * HEAD_DIM])
        q = _rope(q.reshape(bn, t, D_HEADS, HEAD_DIM), pos).reshape(bn, t, D_KV, g, HEAD_DIM)
        k = _rope(k.reshape(bn, t, D_KV, HEAD_DIM), pos)
        v = v.reshape(bn, t, D_KV, HEAD_DIM)

        def to_strided(a, dil=dil):
            rest = a.shape[2:]
            a = a.reshape((bn, t // dil, dil) + rest)
            return jnp.moveaxis(a, 2, 1).reshape((bn * dil, t // dil) + rest)

        def from_strided(a, dil=dil):
            rest = a.shape[2:]
            a = a.reshape((bn, dil, t // dil) + rest)
            return jnp.moveaxis(a, 1, 2).reshape((bn, t) + rest)

        o, lse = _banded_attention(to_strided(q), to_strided(k), to_strided(v), window // (2 * dil), D_BLOCK, HEAD_DIM ** -0.5)
        outs.append(from_strided(o))
        lses.append(from_strided(lse))
    w = jax.nn.softmax(jnp.stack(lses, axis=0), axis=0)
    o = jnp.einsum('nbthg,nbthgd->bthgd', w.astype(outs[0].dtype), jnp.stack(outs, axis=0))
    return o.reshape(bn, t, D_HEADS * HEAD_DIM)


def _layer(x, pos, rows, cols, norm_g, w_in, a_q_norm, w_q_up, a_kv_norm, w_kv_up,
           b_q_norm, b_k_norm, c_sink, w_branch, w_out):
    bn, t = x.shape[:2]
    xn = _rms_norm(x, norm_g)
    h = xn @ w_in
    h_a, h_b, h_c, h_d, z, mg = _split_cols(h, [A_IN, B_IN, C_IN, D_IN, GATE_IN, MERGE_IN])
    ys = [
        _mla_mixer(h_a, pos, a_q_norm, w_q_up, a_kv_norm, w_kv_up),
        _axial_gqa_mixer(h_b, rows, cols, b_q_norm, b_k_norm),
        _sink_window_mixer(h_c, pos, c_sink),
        _dilated_mixer(h_d, pos),
    ]
    merged = jnp.zeros_like(x)
    for i in range(N_BRANCH):
        yi = ys[i] * jax.nn.silu(z[..., i * BRANCH_W:(i + 1) * BRANCH_W])
        gi = jax.nn.sigmoid(mg[..., i * D_MODEL:(i + 1) * D_MODEL])
        merged = merged + gi * (yi @ w_branch[i])
    return x + merged @ w_out


def _trunk(x, norm_in, w_in, a_q_norm, w_q_up, a_kv_norm, w_kv_up,
           b_q_norm, b_k_norm, c_sink, w_branch, w_out, final_norm):
    t = x.shape[1]
    rows_n = t // GRID_W
    pos = jnp.arange(t)
    rows = jnp.repeat(jnp.arange(rows_n), GRID_W)
    cols = jnp.tile(jnp.arange(GRID_W), rows_n)
    for l in range(DEPTH):
        x = _layer(x, pos, rows, cols, norm_in[l], w_in[l], a_q_norm[l], w_q_up[l], a_kv_norm[l], w_kv_up[l],
                   b_q_norm[l], b_k_norm[l], c_sink[l], w_branch[l], w_out[l])
    return _rms_norm(x, final_norm)


def setup_inputs(seed: int = 0) -> dict:
    key = jax.random.key(seed)
    ks = jax.random.split(key, 16)
    f = jnp.float32

    def gain(k, shape):
        return jnp.ones(shape, f) + 0.02 * jax.random.normal(k, shape, f)

    return {
        'x_prompt': jax.random.normal(ks[0], (BATCH, SEQ, D_MODEL), f),
        'x_sample': jax.random.normal(ks[1], (DEC_BATCH, DEC_SEQ, D_MODEL), f),
        'norm_in': gain(ks[2], (DEPTH, D_MODEL)),
        'w_in': jax.random.normal(ks[3], (DEPTH, D_MODEL, N_IN), f) * D_MODEL ** -0.5,
        'a_q_norm': gain(ks[4], (DEPTH, A_Q_LORA)),
        'w_q_up': jax.random.normal(ks[5], (DEPTH, A_Q_LORA, A_HEADS * (A_NOPE + A_ROPE)), f) * A_Q_LORA ** -0.5,
        'a_kv_norm': gain(ks[6], (DEPTH, A_KV_LORA)),
        'w_kv_up': jax.random.normal(ks[7], (DEPTH, A_KV_LORA, A_HEADS * (A_NOPE + A_V)), f) * A_KV_LORA ** -0.5,
        'b_q_norm': gain(ks[8], (DEPTH, HEAD_DIM)),
        'b_k_norm': gain(ks[9], (DEPTH, HEAD_DIM)),
        'c_sink': 0.5 * jax.random.normal(ks[10], (DEPTH, C_HEADS), f),
        'w_branch': jax.random.normal(ks[11], (DEPTH, N_BRANCH, BRANCH_W, D_MODEL), f) * BRANCH_W ** -0.5,
        'w_out': jax.random.normal(ks[12], (DEPTH, D_MODEL, D_MODEL), f) * D_MODEL ** -0.5,
        'final_norm': gain(ks[13], (D_MODEL,)),
    }


def reference(x_prompt, x_sample, norm_in, w_in, a_q_norm, w_q_up, a_kv_norm, w_kv_up,
              b_q_norm, b_k_norm, c_sink, w_branch, w_out, final_norm):
    y_prompt = _trunk(x_prompt, norm_in, w_in, a_q_norm, w_q_up, a_kv_norm, w_kv_up,
                      b_q_norm, b_k_norm, c_sink, w_branch, w_out, final_norm)
    y_sample = _trunk(x_sample, norm_in, w_in, a_q_norm, w_q_up, a_kv_norm, w_kv_up,
                      b_q_norm, b_k_norm, c_sink, w_branch, w_out, final_norm)
    return (y_prompt, y_sample)
```

```python
import math
from contextlib import ExitStack

import numpy as np
import ml_dtypes
import concourse.bass as bass
import concourse.mybir as mybir
from concourse.bass_utils import run_bass_kernel_spmd

F32 = mybir.dt.float32
BF16 = mybir.dt.bfloat16
ALU = mybir.AluOpType
AF = mybir.ActivationFunctionType
AX = mybir.AxisListType

T = 2048
D = 1024
NT = 16
NIN = 8096
L = 2
OFF_A, OFF_B, OFF_C, OFF_D, OFF_Z, OFF_MG = 0, 416, 928, 1440, 2976, 4000
EPS = 1e-6
NCORES = 8


class Op:
    __slots__ = ("eng", "fn", "idx", "waits", "signal", "sem", "count", "is_dma", "epoch", "deps")

    def __init__(self, eng, fn, is_dma):
        self.eng = eng
        self.fn = fn
        self.is_dma = is_dma
        self.waits = {}
        self.signal = False
        self.sem = None
        self.count = 0
        self.deps = ()


class _Rec:
    def __init__(self):
        self.call = None

    def __getattr__(self, name):
        def f(*a, **k):
            self.call = (name, a, k)
            return self
        return f


class Sched:
    def __init__(self):
        self.q = {k: [] for k in ("pe", "act", "dve", "pool", "sp")}
        self.last_w = {}
        self.readers = {}
        self.epoch = 0
        self.dma_res_count = {}

    def _add(self, eng, fn, reads, writes, is_dma, dma_res=None):
        rec = _Rec()
        fn(rec)
        call = rec.call
        assert call is not None
        fn = (lambda c: (lambda e: getattr(e, c[0])(*c[1], **c[2])))(call)
        op = Op(eng, fn, is_dma)
        op.epoch = self.epoch
        op.idx = len(self.q[eng])
        deps = set()
        for r in reads:
            w = self.last_w.get(r)
            if w is not None:
                deps.add(w)
        for w_ in writes:
            w = self.last_w.get(w_)
            if w is not None:
                deps.add(w)
            rl = self.readers.get(w_)
            if rl:
                deps.update(rl)
        for r in reads:
            self.readers.setdefault(r, []).append(op)
        for w_ in writes:
            self.last_w[w_] = op
            self.readers[w_] = []
        if is_dma:
            op.sem = ("dma", dma_res)
            c = self.dma_res_count.get(dma_res, 0) + 16
            self.dma_res_count[dma_res] = c
            op.count = c
        best = {}
        keep = []
        for d in deps:
            if d.is_dma:
                keep.append(d)
            else:
                b = best.get(d.eng)
                if b is None or d.idx > b.idx:
                    best[d.eng] = d
        keep.extend(best.values())
        op.deps = keep
        self.q[eng].append(op)
        return op

    def op(self, eng, fn, reads=(), writes=()):
        return self._add(eng, fn, reads, writes, False)

    def dma(self, eng, fn, reads=(), writes=(), res=None):
        return self._add(eng, fn, reads, writes, True, res)

    @staticmethod
    def _skip(d, op):
        if d.is_dma or op.is_dma or d.eng != op.eng:
            return False
        return d.eng == "pe" or (op.idx - d.idx) > 2

    def finalize(self):
        for ops in self.q.values():
            for op in ops:
                for d in op.deps:
                    if not d.is_dma and not self._skip(d, op):
                        d.signal = True
        cnt = {}
        for eng, ops in self.q.items():
            for op in ops:
                if not op.is_dma and op.signal:
                    key = ("eng", eng, op.epoch)
                    cnt[key] = cnt.get(key, 0) + 1
                    op.sem = key
                    op.count = cnt[key]
        sems = set()
        for eng, ops in self.q.items():
            seen = {}
            for op in ops:
                w = {}
                for d in op.deps:
                    if self._skip(d, op):
                        continue
                    if w.get(d.sem, 0) < d.count:
                        w[d.sem] = d.count
                for s, c in list(w.items()):
                    if seen.get(s, 0) >= c:
                        del w[s]
                    else:
                        seen[s] = c
                op.waits = w
                sems.update(w.keys())
                if op.is_dma or op.signal:
                    sems.add(op.sem)
        return sorted(sems, key=str)

    def emit(self, nc, final_wait_res=()):
        sems = self.finalize()
        with ExitStack() as es:
            h = {}
            for i, s in enumerate(sems):
                h[s] = es.enter_context(nc.semaphore("s%d" % i))
            block = es.enter_context(nc.Block())
            finals = [(("dma", r), self.dma_res_count[r]) for r in final_wait_res if r in self.dma_res_count]

            def run(engobj, ops, extra=()):
                for op in ops:
                    for s, c in op.waits.items():
                        engobj.wait_ge(h[s], c)
                    ins = op.fn(engobj)
                    if op.is_dma:
                        ins.then_inc(h[op.sem], 16)
                    elif op.signal:
                        ins.then_inc(h[op.sem], 1)
                for s, c in extra:
                    engobj.wait_ge(h[s], c)

            @block.sync
            def _(e):
                run(e, self.q["sp"], finals)

            @block.tensor
            def _(e):
                run(e, self.q["pe"])

            @block.scalar
            def _(e):
                run(e, self.q["act"])

            @block.vector
            def _(e):
                run(e, self.q["dve"])

            @block.gpsimd
            def _(e):
                run(e, self.q["pool"])
        return len(sems)


def _host_consts():
    pos = (np.arange(NT)[None, :] * 128 + np.arange(128)[:, None]).astype(np.float64)
    f64 = np.power(np.float32(10000.0), -np.arange(32, dtype=np.float32) * np.float32(2.0) / np.float32(64)).astype(np.float32)
    f32_ = np.power(np.float32(10000.0), -np.arange(16, dtype=np.float32) * np.float32(2.0) / np.float32(32)).astype(np.float32)
    pos = pos.astype(np.float32)
    a64 = (pos[:, :, None] * f64[None, None, :]).astype(np.float64)
    cos64, sin64 = np.cos(a64), np.sin(a64)
    aA = (pos[:, :, None] * f32_[None, None, :]).astype(np.float64)
    cosA, sinA = np.cos(aA), np.sin(aA)
    rows = np.floor(pos / 64)
    cols = pos - rows * 64
    ar = (rows[:, :, None] * f32_[None, None, :]).astype(np.float64)
    ac = (cols[:, :, None] * f32_[None, None, :]).astype(np.float64)
    cosX = np.stack([np.cos(ar), np.cos(ac)], axis=2)
    sinX = np.stack([np.sin(ar), np.sin(ac)], axis=2)
    cf = np.concatenate([
        cos64.reshape(128, -1), sin64.reshape(128, -1),
        cosA.reshape(128, -1), sinA.reshape(128, -1),
        cosX.reshape(128, -1), sinX.reshape(128, -1),
        np.full((128, 1), EPS),
    ], axis=1).astype(np.float32)
    ident = np.eye(128)
    ki = np.arange(128)[:, None]
    qq = np.arange(384)[None, :] - 128
    maskC = np.where(np.abs(qq - ki) <= 128, 0.0, -30000.0)
    maskD = np.where(np.abs(qq - ki) <= 64, 0.0, -30000.0)
    cb = np.concatenate([ident, maskC, maskD], axis=1).astype(ml_dtypes.bfloat16)
    return np.ascontiguousarray(cf), np.ascontiguousarray(cb)


CF_COS64, CF_SIN64, CF_COSA, CF_SINA, CF_COSX, CF_SINX, CF_EPS = 0, 512, 1024, 1280, 1536, 2048, 2560
NCF = 2561
CB_ID, CB_MC, CB_MD = 0, 128, 512
NCB = 896
PF_GB, PF_SINK, PF_GIN, PF_GQ, PF_GKV = 0, 768, 776, 792, 796
NPF = 798


def _host_params(norm_in, a_q_norm, a_kv_norm, b_q_norm, b_k_norm, c_sink):
    pf = np.zeros((128, NPF), np.float32)
    for l in range(L):
        gb = np.concatenate([np.tile(b_q_norm[l][None, :], (4, 1)), np.tile(b_k_norm[l][None, :], (2, 1))], 0)
        pf[:, PF_GB + l * 384: PF_GB + (l + 1) * 384] = gb.reshape(1, 384)
        pf[:, PF_SINK + l * 4: PF_SINK + (l + 1) * 4] = c_sink[l][None, :]
        pf[:, PF_GIN + l * 8: PF_GIN + (l + 1) * 8] = norm_in[l].reshape(8, 128).T
        pf[:, PF_GQ + l * 2: PF_GQ + (l + 1) * 2] = a_q_norm[l].reshape(2, 128).T
        pf[:, PF_GKV + l] = a_kv_norm[l]
    return pf


def build(nseq, depth=L, dbg=False):
    nc = bass.Bass("TRN2", target_bir_lowering=False)
    dram = {}

    def din(name, shape, dt=F32):
        dram[name] = nc.dram_tensor(name, list(shape), dt, kind="ExternalInput")
        return dram[name].ap()

    x_d = din("x", [nseq, T, D])
    win_d = din("w_in", [L, D, NIN])
    wq_d = din("w_q_up", [L, 256, 384])
    wkv_d = din("w_kv_up", [L, 128, 512])
    wbr_d = din("w_branch", [L, 1024, 1024])
    wout_d = din("w_out", [L, D, D])
    gfin_d = din("gfin", [128, D])
    cf_d = din("cf", [128, NCF])
    cb_d = din("cb", [128, NCB], BF16)
    pf_d = din("pf", [128, NPF])
    y_d = nc.dram_tensor("y", [nseq, T, D], F32, kind="ExternalOutput").ap()
    s_in = nc.dram_tensor("s_in", [L, D, NIN], BF16).ap()
    s_q = nc.dram_tensor("s_q", [L, 256, 384], BF16).ap()
    s_kv = nc.dram_tensor("s_kv", [L, 128, 512], BF16).ap()
    s_br = nc.dram_tensor("s_br", [L, 1024, 1024], BF16).ap()
    s_out = nc.dram_tensor("s_out", [L, D, D], BF16).ap()
    if dbg:
        dbg_d = nc.dram_tensor("dbg", [128, 8 * T], BF16, kind="ExternalOutput").ap()

    S = Sched()
    UW = 21504

    with ExitStack() as es:
        def sb(name, shape, dt):
            return es.enter_context(nc.sbuf_tensor(name, list(shape), dt))

        def ps(name, shape, dt):
            return es.enter_context(nc.psum_tensor(name, list(shape), dt))

        X = sb("X", [128, NT, D], F32)
        XNT = sb("XNT", [128, 8, T], BF16)
        YT = sb("YT", [128, 8, T], BF16)
        WB = [sb("WB%d" % i, [128, 8, 512], BF16) for i in range(2)]
        WBR = [sb("WBR%d" % i, [128, 2, 512], BF16) for i in range(2)]
        WQ = sb("WQ", [128, 2, 384], BF16)
        WKV = sb("WKV", [128, 512], BF16)
        CF = sb("CF", [128, NCF], F32)
        CB = sb("CB", [128, NCB], BF16)
        PF = sb("PF", [128, NPF], F32)
        ST = sb("ST", [128, 64], F32)
        ESK = sb("ESK", [128, 8], F32)
        DUM = sb("DUM", [128, 2], F32)
        U = sb("U", [128, UW], BF16)
        PSALL = ps("PSALL", [128, 4096], F32)
        PSB = [PSALL[:, i * 512:(i + 1) * 512] for i in range(6)]
        PSR = [PSALL[:, r * 1024:(r + 1) * 1024] for r in range(2)]
        TPB = [PSALL[:, (6 + i) * 512:(7 + i) * 512].bitcast(BF16) for i in range(2)]
        STR = [(PSALL[:, 0:1024], [("ps", 0), ("ps", 1)]), (PSALL[:, 1024:2048], [("ps", 2), ("ps", 3)]),
               (PSALL[:, 3072:4096], [("tp", 0), ("tp", 1)])]

        ident = CB[:, CB_ID:CB_ID + 128]
        eps_t = CF[:, CF_EPS:CF_EPS + 1]

        def ub(off, n):
            assert off + n <= UW, (off, n)
            return U[:, off:off + n]

        def uf(off, n):
            assert off % 2 == 0 and off + 2 * n <= UW, (off, n)
            return U[:, off:off + 2 * n].bitcast(F32)

        lazy = {"pending": False, "early": False}

        def op(eng, fn, reads=(), writes=()):
            if lazy["early"]:
                return S.op(eng, fn, list(reads), writes)
            if lazy["pending"]:
                flush_fence()
            return S.op(eng, fn, list(reads) + ["Uown"], writes)

        def fence():
            lazy["pending"] = True

        def flush_fence():
            if lazy["pending"]:
                lazy["pending"] = False
                S.op("pool", lambda e: e.memset(DUM[:, 0:1], 0.0), reads=(), writes=["Uown", "DUM"])

        SCR = ["scr0", "scr1", "scr2"]
        rr = {"ev": 0, "ps": 0, "tp": 0, "st": 0, "ot": 0}

        def evac_eng():
            rr["ev"] ^= 1
            return "act" if rr["ev"] else "dve"

        def copy_op(eng, out, in_, reads, writes):
            if eng == "act":
                return op("act", lambda e: e.activation(out=out, in_=in_, func=AF.Copy), reads, writes)
            return op(eng, lambda e: e.tensor_copy(out=out, in_=in_), reads, writes)

        def next_ps():
            rr["ps"] ^= 1
            return rr["ps"]

        def next_tp():
            rr["tp"] ^= 1
            return rr["tp"]

        class WStream:
            def __init__(self, bufs, tag):
                self.bufs = bufs
                self.tag = tag
                self.descs = []
                self.issued = 0
                self.i = 0

            def _issue(self, j):
                slot = j % len(self.bufs)
                for (dst_fn, src) in self.descs[j]:
                    dst = dst_fn(self.bufs[slot])
                    S.dma("sp", (lambda d, s: (lambda e: e.dma_start(out=d, in_=s)))(dst, src),
                          reads=SCR, writes=[(self.tag, slot)], res=(self.tag, slot))

            def get(self):
                n = len(self.bufs)
                while self.issued < min(len(self.descs), self.i + n):
                    self._issue(self.issued)
                    self.issued += 1
                slot = self.i % n
                self.i += 1
                return self.bufs[slot], (self.tag, slot)

        wst = WStream(WB, "WB")
        bst = WStream(WBR, "WBR")

        def win_cols(l, ranges):
            out = []
            for (doff, c0, n) in ranges:
                src = s_in[l, :, c0:c0 + n].rearrange("(k p) c -> p k c", p=128)
                out.append(((lambda o, nn: (lambda buf: buf[:, :, o:o + nn]))(doff, n), src))
            return out

        def build_descs():
            for _s in range(nseq):
                for l in range(depth):
                    wst.descs.append(win_cols(l, [(0, OFF_Z, 512)]))
                    wst.descs.append(win_cols(l, [(0, OFF_Z + 512, 512)]))
                    wst.descs.append(win_cols(l, [(0, OFF_A, 416)]))
                    for off in (OFF_B, OFF_C):
                        for j in range(2):
                            wst.descs.append(win_cols(l, [(0, off + 128 * j, 128), (128, off + 256 + 64 * j, 64),
                                                          (192, off + 384 + 64 * j, 64)]))
                    for j in range(2):
                        for g in range(3):
                            off = OFF_D + 512 * g
                            wst.descs.append(win_cols(l, [(0, off + 128 * j, 128), (128, off + 256 + 64 * j, 64),
                                                          (192, off + 384 + 64 * j, 64)]))
                    for G in range(4):
                        for i in range(4):
                            for c in range(2):
                                wst.descs.append(win_cols(l, [(0, OFF_MG + i * 1024 + c * 512, 512)]))
                                src = s_br[l, i * 256:(i + 1) * 256, c * 512:(c + 1) * 512].rearrange("(k p) c -> p k c", p=128)
                                bst.descs.append([((lambda buf: buf[:, :, :]), src)])
                        for c in range(2):
                            src = s_out[l, :, c * 512:(c + 1) * 512].rearrange("(k p) c -> p k c", p=128)
                            wst.descs.append([((lambda buf: buf[:, :, :]), src)])

        build_descs()

        S.dma("sp", lambda e: e.dma_start(out=CF[:], in_=cf_d), writes=["CF"], res="CF")
        S.dma("sp", lambda e: e.dma_start(out=CB[:], in_=cb_d), writes=["CB"], res="CB")
        S.dma("sp", lambda e: e.dma_start(out=PF[:], in_=pf_d), writes=["PF"], res="PF")
        op("act", lambda e: e.activation(out=ESK[:, 0:8], in_=PF[:, PF_SINK:PF_SINK + 8], func=AF.Exp), ["PF"], ["ESK"])

        def convert():
            stg = [uf(0, 2048), uf(4096, 2048), uf(8192, 2048)]
            stb = [ub(12288, 2048), ub(14336, 2048), ub(16384, 2048)]
            pieces = []
            for l in range(L):
                for k in range(8):
                    for c0 in range(0, NIN, 2048):
                        n = min(2048, NIN - c0)
                        pieces.append((win_d[l, k * 128:(k + 1) * 128, c0:c0 + n], s_in[l, k * 128:(k + 1) * 128, c0:c0 + n], n,
                                       PF[:, PF_GIN + l * 8 + k: PF_GIN + l * 8 + k + 1]))
                for k in range(2):
                    pieces.append((wq_d[l, k * 128:(k + 1) * 128, :], s_q[l, k * 128:(k + 1) * 128, :], 384,
                                   PF[:, PF_GQ + l * 2 + k: PF_GQ + l * 2 + k + 1]))
                pieces.append((wkv_d[l], s_kv[l], 512, PF[:, PF_GKV + l: PF_GKV + l + 1]))
                for k in range(8):
                    pieces.append((wbr_d[l, k * 128:(k + 1) * 128, :], s_br[l, k * 128:(k + 1) * 128, :], 1024, None))
                for k in range(8):
                    pieces.append((wout_d[l, k * 128:(k + 1) * 128, :], s_out[l, k * 128:(k + 1) * 128, :], 1024, None))
            engs = ["dve", "act"]
            for i, (src, dst, n, g) in enumerate(pieces):
                sl = i % 3
                a, b = stg[sl], stb[sl]
                S.dma("sp", (lambda a_, s_, n_: (lambda e: e.dma_start(out=a_[:, 0:n_], in_=s_)))(a, src, n),
                      reads=["Uown"], writes=[("stg", sl)], res=("stg", sl))
                eng = engs[i % 2]
                if g is None:
                    copy_op(eng, b[:, 0:n], a[:, 0:n], [("stg", sl)], [("stb", sl)])
                elif eng == "act":
                    op("act", (lambda a_, b_, n_, g_: (lambda e: e.activation(out=b_[:, 0:n_], in_=a_[:, 0:n_], func=AF.Copy, scale=g_)))(a, b, n, g),
                       [("stg", sl), "PF"], [("stb", sl)])
                else:
                    op(eng, (lambda a_, b_, n_, g_: (lambda e: e.tensor_scalar_mul(out=b_[:, 0:n_], in0=a_[:, 0:n_], scalar1=g_)))(a, b, n, g),
                       [("stg", sl), "PF"], [("stb", sl)])
                S.dma("pool", (lambda b_, d_, n_: (lambda e: e.dma_start(out=d_, in_=b_[:, 0:n_])))(b, dst, n),
                      reads=[("stb", sl), "Uown"], writes=[("scr", i)], res=("scr", sl))
                S.last_w["scr%d" % sl] = S.q["pool"][-1]

        convert()

        def x_load(s):
            for t in range(NT):
                S.dma("sp", (lambda t_: (lambda e: e.dma_start(out=X[:, t_, :], in_=x_d[s, t_ * 128:(t_ + 1) * 128, :])))(t),
                      writes=[("X", t)], res=("X", t))

        def proj(wbuf, wkey, ncols, lhs_fn, xkeys, nk=8, wcol0=0):
            b = next_ps()
            P = PSB[b]
            for k in range(nk):
                op("pe", (lambda k_: (lambda e: e.matmul(P[:, 0:ncols], lhsT=lhs_fn(k_), rhs=wbuf[:, k_, wcol0:wcol0 + ncols],
                                                         start=(k_ == 0), stop=(k_ == nk - 1))))(k),
                   list(xkeys) + [wkey], [("ps", b)])
            return P, ("ps", b)

        def xnt_tile(t):
            return lambda k: XNT[:, k, t * 128:(t + 1) * 128]

        def transposes(srcs, reads):
            b = next_tp()
            TP = TPB[b]
            for i, (ap, w) in enumerate(srcs):
                op("pe", (lambda i_, ap_, w_: (lambda e: e.transpose(out=TP[0:w_, i_ * 128:(i_ + 1) * 128], in_=ap_, identity=ident)))(i, ap, w),
                   list(reads) + ["CB"], [("tp", b)])
            return TP, ("tp", b)

        def rope(src, dst, cos, sin, H, nh, w, tA, tB, reads, writes, eng="pool", sfx=0):
            sv = src.rearrange("p (h n x w) -> p h n x w", h=H, n=nh, x=2, w=w)
            dv = dst.rearrange("p (h n x w) -> p h n x w", h=H, n=nh, x=2, w=w)
            shp = [128, H, nh, w]
            cb_ = cos.rearrange("p (n w) -> p n w", n=nh).unsqueeze(1).to_broadcast(shp)
            sb_ = sin.rearrange("p (n w) -> p n w", n=nh).unsqueeze(1).to_broadcast(shp)
            x1, x2 = sv[:, :, :, 0, :], sv[:, :, :, 1, :]
            a = tA.rearrange("p (h n w) -> p h n w", h=H, n=nh)
            b = tB.rearrange("p (h n w) -> p h n w", h=H, n=nh)
            rd = list(reads) + ["CF"]
            kA, kB = ("tA", sfx), ("tB", sfx)
            op(eng, lambda e: e.tensor_tensor(out=a, in0=x1, in1=cb_, op=ALU.mult), rd, [kA])
            op(eng, lambda e: e.tensor_tensor(out=b, in0=x2, in1=sb_, op=ALU.mult), rd, [kB])
            op(eng, lambda e: e.tensor_tensor(out=dv[:, :, :, 0, :], in0=a, in1=b, op=ALU.subtract), [kA, kB], writes)
            op(eng, lambda e: e.tensor_tensor(out=a, in0=x1, in1=sb_, op=ALU.mult), rd, [kA])
            op(eng, lambda e: e.tensor_tensor(out=b, in0=x2, in1=cb_, op=ALU.mult), rd, [kB])
            op(eng, lambda e: e.tensor_tensor(out=dv[:, :, :, 1, :], in0=a, in1=b, op=ALU.add), [kA, kB], writes)

        def attention(heads, blocks_fn, scale, finish, PT, reads):
            items = []
            for hi in range(len(heads)):
                for g in range(4):
                    bl = blocks_fn(g)
                    for bi, (b, qlo, qhi, mask) in enumerate(bl):
                        items.append((hi, g, bi, len(bl), b, qlo, qhi, mask))
            groups = []
            i = 0
            while i < len(items):
                if i + 1 < len(items) and (items[i][6] - items[i][5]) == (items[i + 1][6] - items[i + 1][5]):
                    groups.append([items[i], items[i + 1]])
                    i += 2
                else:
                    groups.append([items[i]])
                    i += 1
            ng = len(groups)
            assert len(PT) == 3

            def issue_qk(gi):
                R, rkeys = STR[gi % 3]
                for ii, (hi, g, bi, nb, b, qlo, qhi, mask) in enumerate(groups[gi]):
                    hd = heads[hi]
                    n = qhi - qlo
                    kT_, qT_ = hd["kT"](b), hd["qT"](qlo, qhi)
                    ml = mask or []
                    op("pe", lambda e: e.matmul(R[:, ii * 512:ii * 512 + n], lhsT=kT_, rhs=qT_, start=True, stop=True), reads, rkeys)
                    for mi, (co, mo, mw) in enumerate(ml):
                        op("pe", lambda e: e.matmul(R[:, ii * 512 + co:ii * 512 + co + mw], lhsT=ident, rhs=CB[:, mo:mo + mw],
                                                    start=False, stop=True, skip_group_check=True), ["CB"], rkeys)

            issue_qk(0)
            if ng > 1:
                issue_qk(1)
            ob = None
            for gi, grp in enumerate(groups):
                if gi + 2 < ng:
                    issue_qk(gi + 2)
                R, rkeys = STR[gi % 3]
                slot = gi % 3
                PTt = PT[slot]
                ns = [it[6] - it[5] for it in grp]
                if len(grp) == 2 and ns[0] == ns[1]:
                    n = ns[0]
                    op("act", lambda e: e.activation(out=PTt.rearrange("p (g c) -> p g c", g=2)[:, :, 0:n],
                                                     in_=R.rearrange("p (g c) -> p g c", g=2)[:, :, 0:n], func=AF.Exp, scale=scale),
                       [], rkeys + [("PT", slot)])
                else:
                    for ii, n in enumerate(ns):
                        op("act", lambda e: e.activation(out=PTt[:, ii * 512:ii * 512 + n], in_=R[:, ii * 512:ii * 512 + n], func=AF.Exp, scale=scale),
                           [], rkeys + [("PT", slot)])
                for ii, (hi, g, bi, nb, b, qlo, qhi, mask) in enumerate(grp):
                    n = ns[ii]
                    hd = heads[hi]
                    if bi == 0:
                        rr["ot"] ^= 1
                        ob = 4 + rr["ot"]
                    OT = PSB[ob]
                    c0 = qlo - 512 * g
                    V_ = hd["V"](b)
                    op("pe", lambda e: e.matmul(OT[:, c0:c0 + n], lhsT=V_, rhs=PTt[:, ii * 512:ii * 512 + n], start=(bi == 0), stop=(bi == nb - 1),
                                                skip_group_check=True),
                       list(reads) + [("PT", slot)], [("ps", ob)])
                    if bi == nb - 1:
                        finish(hi, g, OT, ("ps", ob))

        def norm_finish(OT, okey, hp, g, Rt, ypair, esk_col=None, use_act=False):
            rk = "Rt"
            if isinstance(Rt, list):
                rr["rt"] = rr.get("rt", 0) ^ 1
                rk = ("Rt", rr["rt"])
                Rt = Rt[rr["rt"]]
            nr = slice(64 * hp, 64 * hp + 64)
            zr = slice(64 * (1 - hp), 64 * (1 - hp) + 64)
            cols = slice(512 * g, 512 * (g + 1))
            if use_act:
                if esk_col is not None:
                    op("act", lambda e: e.activation(out=Rt[zr, :], in_=OT[zr, :], func=AF.Ln, bias=ESK[zr, esk_col:esk_col + 1]), ["ESK"], [okey, rk])
                else:
                    op("act", lambda e: e.activation(out=Rt[zr, :], in_=OT[zr, :], func=AF.Ln), [], [okey, rk])
                op("act", lambda e: e.activation(out=Rt[nr, :], in_=Rt[zr, :], func=AF.Exp, scale=-1.0), [], [rk])
            elif esk_col is not None:
                op("dve", lambda e: e.tensor_scalar_add(out=Rt[zr, :], in0=OT[zr, :], scalar1=ESK[zr, esk_col:esk_col + 1]),
                   ["ESK"], [okey, rk])
                op("dve", lambda e: e.reciprocal(out=Rt[nr, :], in_=Rt[zr, :]), [], [rk])
            else:
                op("dve", lambda e: e.reciprocal(out=Rt[nr, :], in_=OT[zr, :]), [], [okey, rk])
            op("dve", lambda e: e.tensor_tensor(out=Rt[nr, :], in0=OT[nr, :], in1=Rt[nr, :], op=ALU.mult), [], [okey, rk])
            op("pool", lambda e: e.tensor_tensor(out=YT[nr, ypair, cols], in0=YT[nr, ypair, cols], in1=Rt[nr, :], op=ALU.mult),
               [rk], [("YT", ypair)])

        def dense_blocks(g):
            return [(b, 512 * g, 512 * (g + 1), None) for b in range(NT)]

        def band_blocks(maskoff, cs, need):
            def f(g):
                out = []
                for b in range(NT):
                    c = b // cs
                    alo = max(b - 1, c * cs, 4 * g)
                    ahi = min(b + 1, c * cs + cs - 1, 4 * g + 3)
                    if alo > ahi:
                        continue
                    ml = []
                    for a in range(alo, ahi + 1):
                        if (a - b) in need:
                            co, mo = (a - alo) * 128, maskoff + (a - b + 1) * 128
                            if ml and ml[-1][0] + ml[-1][2] == co and ml[-1][1] + ml[-1][2] == mo:
                                ml[-1] = (ml[-1][0], ml[-1][1], ml[-1][2] + 128)
                            else:
                                ml.append((co, mo, 128))
                    out.append((b, alo * 128, (ahi + 1) * 128, ml))
                return out
            return f

        def layer(s, l):
            junk = ub(0, 1024)
            xn = [ub(1024, 1024), ub(2048, 1024)]
            ss = ST[:, 0:16]
            rstd = ST[:, 16:32]
            op("pool", lambda e: e.memset(ss, 0.0), [], [("ss", g4) for g4 in range(4)])

            def stats(g4):
                for t in range(4 * g4, 4 * g4 + 4):
                    op("act", lambda e: e.activation(out=junk, in_=X[:, t, :], func=AF.Square, accum_out=ST[:, t:t + 1]),
                       [("X", t)], ["junk", ("ss", g4)])
                r4 = ST[:, 16 + 4 * g4:20 + 4 * g4]
                op("act", lambda e: e.activation(out=r4, in_=ST[:, 4 * g4:4 * g4 + 4], func=AF.Ln, scale=1.0 / D, bias=eps_t),
                   [("ss", g4), "CF"], [("rstd", g4)])
                op("act", lambda e: e.activation(out=r4, in_=r4, func=AF.Exp, scale=-0.5), [], [("rstd", g4)])

            def normalize(g4):
                for t in range(4 * g4, 4 * g4 + 4):
                    xs = xn[t % 2]
                    op("dve", lambda e: e.tensor_scalar_mul(out=xs, in0=X[:, t, :], scalar1=ST[:, 16 + t:17 + t]),
                       [("X", t), ("rstd", g4)], [("xn", t % 2)])
                    TP, tk = transposes([(xs[:, k * 128:(k + 1) * 128], 128) for k in range(8)], [("xn", t % 2)])
                    copy_op(evac_eng(), XNT[:, :, t * 128:(t + 1) * 128], TP[:, :].rearrange("p (k c) -> p k c", k=8), [], [tk, ("XNT", t)])

            stats(0)
            for g4 in range(4):
                if g4 + 1 < 4:
                    stats(g4 + 1)
                normalize(g4)
            allx = [("XNT", t) for t in range(NT)]
            S.dma("sp", lambda e: e.dma_start(out=WQ[:], in_=s_q[l].rearrange("(k p) c -> p k c", p=128)), reads=SCR + ["WQ"], writes=["WQ"], res="WQ")
            S.dma("sp", lambda e: e.dma_start(out=WKV[:], in_=s_kv[l]), reads=SCR, writes=["WKV"], res="WKV")
            fence()

            for half in range(2):
                wbuf, wkey = wst.get()
                for pp in range(4):
                    pair = half * 4 + pp
                    for g in range(4):
                        b = next_ps()
                        P = PSB[b]
                        for k in range(8):
                            op("pe", (lambda k_, P_=P, pp_=pp, g_=g: (lambda e: e.matmul(
                                P_[:, :], lhsT=wbuf[:, k_, pp_ * 128:(pp_ + 1) * 128], rhs=XNT[:, k_, g_ * 512:(g_ + 1) * 512],
                                start=(k_ == 0), stop=(k_ == 7))))(k), allx + [wkey], [("ps", b)])
                        op("act", (lambda P_=P, pair_=pair, g_=g: (lambda e: e.activation(
                            out=YT[:, pair_, g_ * 512:(g_ + 1) * 512], in_=P_[:, :], func=AF.Silu)))(), [], [("ps", b), ("YT", pair)])

            qlT = ub(0, 4096).rearrange("p (c t) -> p c t", c=2)
            kvlT = ub(4096, 2048)
            qTk = ub(6144, 2048)
            kTh = ub(8192, 2048)
            Vh = ub(10240, 2048).rearrange("p (t c) -> p t c", t=NT)
            tb = 12288
            krf = uf(tb + 512, 512)
            krb = ub(tb + 1536, 512)
            tA = uf(tb + 2048, 256)
            tB = uf(tb + 2560, 256)
            setsA = [dict(hAb=ub(tb, 416), qf=uf(tb + 3072, 96), qb=ub(tb + 3264, 96), kvb=ub(tb + 3392, 64),
                          ta=uf(21312, 16), tb=uf(21344, 16), sq=tA[:, 0:256]),
                     dict(hAb=ub(20480, 416), qf=uf(20896, 96), qb=ub(21088, 96), kvb=ub(21184, 64),
                          ta=uf(21248, 16), tb=uf(21280, 16), sq=tB[:, 0:256])]
            PT = [ub(tb + i * 1024, 1024) for i in range(3)]
            Rt = uf(tb + 5120, 512)
            krT = ub(18432, 2048)
            ssq, sskv = ST[:, 32:48], ST[:, 48:64]
            wbuf, wkey = wst.get()
            lazy["early"] = True
            pendA0 = proj(wbuf, wkey, 416, xnt_tile(0), [("XNT", 0)])
            lazy["early"] = False
            op("pool", lambda e: e.memset(ST[:, 32:64], 0.0), [], ["ssA"])

            def bodyA(t, P, pk):
                x = t % 2
                T_ = setsA[x]
                hAb, sq = T_["hAb"], T_["sq"]
                op("act", lambda e: e.activation(out=sq[:, 0:256], in_=P[:, 0:256], func=AF.Square, accum_out=ST[:, 32 + t:33 + t]),
                   [], [pk, ("sqA", x), "ssA"])
                op("act", lambda e: e.activation(out=sq[:, 0:128], in_=P[:, 256:384], func=AF.Square, accum_out=ST[:, 48 + t:49 + t]),
                   [], [pk, ("sqA", x), "ssA"])
                op("dve", lambda e: e.tensor_copy(out=hAb[:, 0:384], in_=P[:, 0:384]), [], [pk, ("hAb", x)])
                op("dve", lambda e: e.tensor_copy(out=krf[:, t * 32:(t + 1) * 32], in_=P[:, 384:416]), [], [pk, "krf"])
                TP, tk = transposes([(hAb[:, 0:128], 128), (hAb[:, 128:256], 128), (hAb[:, 256:384], 128)], [("hAb", x)])
                copy_op(evac_eng(), qlT[:, :, t * 128:(t + 1) * 128], TP[:, 0:256].rearrange("p (c k) -> p c k", c=2), [], [tk, "qlT"])
                copy_op(evac_eng(), kvlT[:, t * 128:(t + 1) * 128], TP[:, 256:384], [], [tk, "kvlT"])

            pend = {0: pendA0}
            for t in range(NT):
                if t + 1 < NT:
                    pend[t + 1] = proj(wbuf, wkey, 416, xnt_tile(t + 1), [("XNT", t + 1)])
                P, pk = pend.pop(t)
                bodyA(t, P, pk)
            op("act", lambda e: e.activation(out=ssq, in_=ssq, func=AF.Ln, scale=1.0 / 256, bias=eps_t), ["ssA", "CF"], ["ssA"])
            op("act", lambda e: e.activation(out=ssq, in_=ssq, func=AF.Exp, scale=-0.5), [], ["ssA"])
            op("act", lambda e: e.activation(out=sskv, in_=sskv, func=AF.Ln, scale=1.0 / 128, bias=eps_t), ["CF"], ["ssA"])
            op("act", lambda e: e.activation(out=sskv, in_=sskv, func=AF.Exp, scale=-0.5), [], ["ssA"])
            kv_ = krf.rearrange("p (t x w) -> p t x w", t=NT, x=2)
            kd_ = krb.rearrange("p (t x w) -> p t x w", t=NT, x=2)
            cA = CF[:, CF_COSA:CF_COSA + 256].rearrange("p (t w) -> p t w", t=NT)
            sA = CF[:, CF_SINA:CF_SINA + 256].rearrange("p (t w) -> p t w", t=NT)
            a3 = tA[:, 0:256].rearrange("p (t w) -> p t w", t=NT)
            b3 = tB[:, 0:256].rearrange("p (t w) -> p t w", t=NT)
            kAq = [("sqA", 0), ("tA", 0)]
            kBq = [("sqA", 1), ("tB", 0)]
            op("pool", lambda e: e.tensor_tensor(out=a3, in0=kv_[:, :, 0, :], in1=cA, op=ALU.mult), ["krf", "CF"], kAq)
            op("pool", lambda e: e.tensor_tensor(out=b3, in0=kv_[:, :, 1, :], in1=sA, op=ALU.mult), ["krf", "CF"], kBq)
            op("pool", lambda e: e.tensor_tensor(out=kd_[:, :, 0, :], in0=a3, in1=b3, op=ALU.subtract), kAq + kBq, ["krb"])
            op("pool", lambda e: e.tensor_tensor(out=a3, in0=kv_[:, :, 0, :], in1=sA, op=ALU.mult), ["krf", "CF"], kAq)
            op("pool", lambda e: e.tensor_tensor(out=b3, in0=kv_[:, :, 1, :], in1=cA, op=ALU.mult), ["krf", "CF"], kBq)
            op("pool", lambda e: e.tensor_tensor(out=kd_[:, :, 1, :], in0=a3, in1=b3, op=ALU.add), kAq + kBq, ["krb"])
            for t4 in range(2):
                TP, tk = transposes([(krb[:, (t4 * 8 + i) * 32:(t4 * 8 + i + 1) * 32], 32) for i in range(8)], ["krb"])
                copy_op(evac_eng(), krT[0:32, t4 * 1024:(t4 + 1) * 1024], TP[0:32, :], [], [tk, "krT"])
            qf4 = uf(tb + 3072, 384)
            kvf4 = uf(tb + 3840, 512)
            qb4 = ub(20480, 384)
            kvb4 = ub(20864, 256)
            ra4 = uf(21120, 64)
            rb4 = uf(21248, 64)
            for h in range(4):
                hp = h % 2
                op("pool", lambda e: e.memset(Vh[:, :, (1 - hp) * 64:(1 - hp) * 64 + 64], 1.0), [], ["Vh"])
                copy_op("dve", kTh[64:96, :], krT[0:32, :], ["krT"], ["kTh"])
                rrg = {"r": 0}

                def stage1(s4):
                    rrg["r"] ^= 1
                    r = rrg["r"]
                    R = PSR[r]
                    keys = [("ps", 2 * r), ("ps", 2 * r + 1)]
                    for i in range(4):
                        t = 4 * s4 + i
                        for c in range(2):
                            op("pe", lambda e: e.matmul(R[:, i * 256:i * 256 + 96], lhsT=qlT[:, c, t * 128:(t + 1) * 128], rhs=WQ[:, c, h * 96:(h + 1) * 96],
                                                        start=(c == 0), stop=(c == 1)), ["qlT", "WQ"], keys)
                        op("pe", lambda e: e.matmul(R[:, i * 256 + 128:i * 256 + 256], lhsT=kvlT[:, t * 128:(t + 1) * 128], rhs=WKV[:, h * 128:(h + 1) * 128],
                                                    start=True, stop=True, skip_group_check=True), ["kvlT", "WKV"], keys)
                    return R, keys

                def stage2(s4, R, keys):
                    t0 = 4 * s4
                    R3 = R.rearrange("p (t c) -> p t c", t=4)
                    q3 = qf4.rearrange("p (t c) -> p t c", t=4)
                    kv3 = kvf4.rearrange("p (t c) -> p t c", t=4)
                    qb3 = qb4.rearrange("p (t c) -> p t c", t=4)
                    kb3 = kvb4.rearrange("p (t c) -> p t c", t=4)
                    rq = ST[:, 32 + t0:36 + t0].unsqueeze(2).to_broadcast([128, 4, 96])
                    rkv = ST[:, 48 + t0:52 + t0].unsqueeze(2).to_broadcast([128, 4, 128])
                    op("dve", lambda e: e.tensor_tensor(out=q3, in0=R3[:, :, 0:96], in1=rq, op=ALU.mult), ["ssA"], keys + ["qf4"])
                    op("dve", lambda e: e.tensor_tensor(out=kv3, in0=R3[:, :, 128:256], in1=rkv, op=ALU.mult), ["ssA"], keys + ["kvf4"])
                    op("pool", lambda e: e.tensor_copy(out=qb3[:, :, 0:64], in_=q3[:, :, 0:64]), ["qf4"], ["qb4"])
                    op("pool", lambda e: e.tensor_copy(out=kb3, in_=kv3[:, :, 0:64]), ["kvf4"], ["kvb4"])
                    op("act", lambda e: e.activation(out=Vh[:, t0:t0 + 4, hp * 64:hp * 64 + 64], in_=kv3[:, :, 64:128], func=AF.Copy), ["kvf4"], ["Vh"])
                    sv = qf4.rearrange("p (t c) -> p t c", t=4)[:, :, 64:96].rearrange("p t (x w) -> p t x w", x=2)
                    dv = qb4.rearrange("p (t c) -> p t c", t=4)[:, :, 64:96].rearrange("p t (x w) -> p t x w", x=2)
                    cA4 = CF[:, CF_COSA + t0 * 16:CF_COSA + (t0 + 4) * 16].rearrange("p (t w) -> p t w", t=4)
                    sA4 = CF[:, CF_SINA + t0 * 16:CF_SINA + (t0 + 4) * 16].rearrange("p (t w) -> p t w", t=4)
                    a = ra4.rearrange("p (t w) -> p t w", t=4)
                    b = rb4.rearrange("p (t w) -> p t w", t=4)
                    rd = ["qf4", "CF"]
                    op("pool", lambda e: e.tensor_tensor(out=a, in0=sv[:, :, 0, :], in1=cA4, op=ALU.mult), rd, ["ra4"])
                    op("dve", lambda e: e.tensor_tensor(out=b, in0=sv[:, :, 1, :], in1=sA4, op=ALU.mult), rd, ["rb4"])
                    op("dve", lambda e: e.tensor_tensor(out=dv[:, :, 0, :], in0=a, in1=b, op=ALU.subtract), ["ra4", "rb4"], ["qb4"])
                    op("pool", lambda e: e.tensor_tensor(out=a, in0=sv[:, :, 0, :], in1=sA4, op=ALU.mult), rd, ["ra4"])
                    op("dve", lambda e: e.tensor_tensor(out=b, in0=sv[:, :, 1, :], in1=cA4, op=ALU.mult), rd, ["rb4"])
                    op("pool", lambda e: e.tensor_tensor(out=dv[:, :, 1, :], in0=a, in1=b, op=ALU.add), ["ra4", "rb4"], ["qb4"])
                    return qb3, kb3

                def stageT(qb3, kb3):
                    return transposes([(qb3[:, i, :], 96) for i in range(4)] + [(kb3[:, i, :], 64) for i in range(4)], ["qb4", "kvb4"])

                def stageE(s4, TP, tk):
                    t0 = 4 * s4
                    copy_op(evac_eng(), qTk[0:96, t0 * 128:(t0 + 4) * 128], TP[0:96, 0:512], [], [tk, "qT"])
                    copy_op(evac_eng(), kTh[0:64, t0 * 128:(t0 + 4) * 128], TP[0:64, 512:1024], [], [tk, "kTh"])

                pend = {0: stage1(0), 1: stage1(1)}
                tps = {}
                R, keys = pend.pop(0)
                tps[0] = stageT(*stage2(0, R, keys))
                for s4 in range(1, 4):
                    if s4 + 1 < 4:
                        pend[s4 + 1] = stage1(s4 + 1)
                    R, keys = pend.pop(s4)
                    qk_ = stage2(s4, R, keys)
                    stageE(s4 - 1, *tps.pop(s4 - 1))
                    tps[s4] = stageT(*qk_)
                stageE(3, *tps.pop(3))
                hd = dict(qT=lambda q0, q1: qTk[0:96, q0:q1], kT=lambda b_: kTh[0:96, b_ * 128:(b_ + 1) * 128],
                          V=lambda b_: Vh[:, b_, :])

                def finA(hi, g, OT, okey, h_=h, hp_=hp):
                    norm_finish(OT, okey, hp_, g, Rt, 0 + h_ // 2)

                attention([hd], dense_blocks, 96 ** -0.5, finA, PT, ["qT", "kTh", "Vh"])
            fence()

            def rope_ops(x1, x2, o1, o2, cb_, sb_, a, b, reads, writes, sfx):
                rd = list(reads) + ["CF"]
                kA, kB = ("tA", sfx), ("tB", sfx)
                op("pool", lambda e: e.tensor_tensor(out=a, in0=x1, in1=cb_, op=ALU.mult), rd, [kA])
                op("dve", lambda e: e.tensor_tensor(out=b, in0=x2, in1=sb_, op=ALU.mult), rd, [kB])
                op("dve", lambda e: e.tensor_tensor(out=o1, in0=a, in1=b, op=ALU.subtract), [kA, kB], writes)
                op("pool", lambda e: e.tensor_tensor(out=a, in0=x1, in1=sb_, op=ALU.mult), rd, [kA])
                op("dve", lambda e: e.tensor_tensor(out=b, in0=x2, in1=cb_, op=ALU.mult), rd, [kB])
                op("pool", lambda e: e.tensor_tensor(out=o2, in0=a, in1=b, op=ALU.add), [kA, kB], writes)

            def gqa_phase(kind, j, g3=0, ACC=None):
                qT = ub(0, 4096).rearrange("p (h t) -> p h t", h=2)
                kT = ub(4096, 2048)
                V = ub(6144, 3072).rearrange("p (t c) -> p t c", t=NT)
                isD = kind == "D"
                o = 9216 + (8192 if isD else 0)
                sets = []
                for i in range(1 if isD else 2):
                    d_ = dict(hf=uf(o, 768), hb=ub(o + 1536, 768), tA=uf(o + 2304, 384), tB=uf(o + 3072, 384))
                    o += 3840
                    if kind == "B":
                        d_["sq"] = uf(o, 768)
                        o += 1536
                    sets.append(d_)
                o0 = 9216 + (8192 if isD else 0)
                PT = [ub(o0 + i * 1024, 1024) for i in range(3)]
                Rt = uf(o0 + 3072, 512) if isD else uf(o, 512)
                if isD:
                    o = o0 + 3072
                assert o + 1024 <= UW, o
                dil = (1, 4, 16)[g3] if isD else 1
                wbuf, wkey = wst.get()
                ncol = 256 if dil == 1 else 192
                rrg = {"r": 0}

                def stage1(s4):
                    rrg["r"] ^= 1
                    r = rrg["r"]
                    R = PSR[r]
                    keys = [("ps", 2 * r), ("ps", 2 * r + 1)]
                    for i in range(4):
                        t = 4 * s4 + i
                        for k in range(8):
                            op("pe", lambda e: e.matmul(R[:, i * 256:i * 256 + ncol], lhsT=XNT[:, k, t * 128:(t + 1) * 128], rhs=wbuf[:, k, 0:ncol],
                                                        start=(k == 0), stop=(k == 7)), [("XNT", t), wkey], keys)
                    return R, keys

                def stage2(s4, R, keys):
                    x = s4 % len(sets)
                    T_ = sets[x]
                    hf, hb, tA, tB = T_["hf"], T_["hb"], T_["tA"], T_["tB"]
                    khf, khb = ("hf", x), ("hb", x)
                    R3 = R.rearrange("p (t c) -> p t c", t=4)
                    hf3 = hf.rearrange("p (t c) -> p t c", t=4)
                    hb3 = hb.rearrange("p (t c) -> p t c", t=4)
                    t0 = 4 * s4
                    if kind == "B":
                        sq = T_["sq"]
                        s12 = ST[:, 32 + 12 * x:44 + 12 * x]
                        op("act", lambda e: e.activation(out=sq.rearrange("p (t c) -> p t c", t=4), in_=R3[:, :, 0:192], func=AF.Square), [], keys + [("sq", x)])
                        op("dve", lambda e: e.tensor_reduce(out=s12, in_=sq.rearrange("p (h c) -> p h c", h=12), axis=AX.X, op=ALU.add),
                           [("sq", x)], [("ss3", x)])
                        op("act", lambda e: e.activation(out=s12, in_=s12, func=AF.Ln, scale=1.0 / 64, bias=eps_t), ["CF"], [("ss3", x)])
                        op("act", lambda e: e.activation(out=s12, in_=s12, func=AF.Exp, scale=-0.5), [], [("ss3", x)])
                        op("dve", lambda e: e.tensor_tensor(
                            out=hf.rearrange("p (t h c) -> p t h c", t=4, h=3), in0=R3[:, :, 0:192].rearrange("p t (h c) -> p t h c", h=3),
                            in1=s12.rearrange("p (t h) -> p t h", t=4).unsqueeze(3).to_broadcast([128, 4, 3, 64]), op=ALU.mult), [("ss3", x)], keys + [khf])
                        gq = PF[:, PF_GB + l * 384: PF_GB + l * 384 + 128].unsqueeze(1).to_broadcast([128, 4, 128])
                        gk = PF[:, PF_GB + l * 384 + 256: PF_GB + l * 384 + 320].unsqueeze(1).to_broadcast([128, 4, 64])
                        op("pool", lambda e: e.tensor_tensor(out=hf3[:, :, 0:128], in0=hf3[:, :, 0:128], in1=gq, op=ALU.mult), ["PF"], [khf])
                        op("pool", lambda e: e.tensor_tensor(out=hf3[:, :, 128:192], in0=hf3[:, :, 128:192], in1=gk, op=ALU.mult), ["PF"], [khf])
                        sv = hf.rearrange("p (t h n x w) -> p t h n x w", t=4, h=3, n=2, x=2, w=16)
                        dv = hb.rearrange("p (t h n x w) -> p t h n x w", t=4, h=3, n=2, x=2, w=16)
                        cX = CF[:, CF_COSX + t0 * 32:CF_COSX + (t0 + 4) * 32].rearrange("p (t n w) -> p t n w", t=4, n=2)
                        sX = CF[:, CF_SINX + t0 * 32:CF_SINX + (t0 + 4) * 32].rearrange("p (t n w) -> p t n w", t=4, n=2)
                        a = tA[:, 0:192].rearrange("p (t h w) -> p t h w", t=4, h=3)
                        b = tB[:, 0:192].rearrange("p (t h w) -> p t h w", t=4, h=3)
                        a5 = tA.rearrange("p (t h n w) -> p t h n w", t=4, h=3, n=2)
                        b5 = tB.rearrange("p (t h n w) -> p t h n w", t=4, h=3, n=2)
                        rope_ops(sv[:, :, :, :, 0, :], sv[:, :, :, :, 1, :], dv[:, :, :, :, 0, :], dv[:, :, :, :, 1, :],
                                 cX.unsqueeze(2).to_broadcast([128, 4, 3, 2, 16]), sX.unsqueeze(2).to_broadcast([128, 4, 3, 2, 16]),
                                 a5, b5, [khf], [khb], x)
                    else:
                        copy_op("dve", hf3, R3[:, :, 0:192], [], keys + [khf])
                        sv = hf.rearrange("p (t h x w) -> p t h x w", t=4, h=3, x=2, w=32)
                        dv = hb.rearrange("p (t h x w) -> p t h x w", t=4, h=3, x=2, w=32)
                        c6 = CF[:, CF_COS64 + t0 * 32:CF_COS64 + (t0 + 4) * 32].rearrange("p (t w) -> p t w", t=4).unsqueeze(2).to_broadcast([128, 4, 3, 32])
                        s6 = CF[:, CF_SIN64 + t0 * 32:CF_SIN64 + (t0 + 4) * 32].rearrange("p (t w) -> p t w", t=4).unsqueeze(2).to_broadcast([128, 4, 3, 32])
                        a = tA.rearrange("p (t h w) -> p t h w", t=4, h=3)
                        b = tB.rearrange("p (t h w) -> p t h w", t=4, h=3)
                        rope_ops(sv[:, :, :, 0, :], sv[:, :, :, 1, :], dv[:, :, :, 0, :], dv[:, :, :, 1, :], c6, s6, a, b, [khf], [khb], x)
                    if dil == 1:
                        op("act", lambda e: e.activation(out=V[:, t0:t0 + 4, 0:64], in_=R3[:, :, 192:256], func=AF.Copy), [], keys + ["V"])
                        op("act", lambda e: e.activation(out=V[:, t0:t0 + 4, 128:192], in_=R3[:, :, 192:256], func=AF.Copy), [], keys + ["V"])
                    return hb3, khb

                def stageT(hb3, khb):
                    return transposes([(hb3[:, i, 0:128], 128) for i in range(4)] + [(hb3[:, i, 128:192], 64) for i in range(4)], [khb])

                def stageE(s4, TP, tk):
                    t0 = 4 * s4
                    if dil == 1:
                        copy_op(evac_eng(), qT[0:64, 0, t0 * 128:(t0 + 4) * 128], TP[0:64, 0:512], [], [tk, "qT"])
                        copy_op(evac_eng(), qT[64:128, 1, t0 * 128:(t0 + 4) * 128], TP[64:128, 0:512], [], [tk, "qT"])
                        copy_op(evac_eng(), kT[0:64, t0 * 128:(t0 + 4) * 128], TP[0:64, 512:1024], [], [tk, "kT"])
                        copy_op(evac_eng(), kT[64:128, t0 * 128:(t0 + 4) * 128], TP[0:64, 512:1024], [], [tk, "kT"])
                    else:
                        w4 = 512 // dil
                        for hh in range(2):
                            dq = qT[hh * 64:(hh + 1) * 64, hh, :].rearrange("p (r u) -> p r u", r=dil)[:, :, s4 * w4:(s4 + 1) * w4]
                            copy_op(evac_eng(), dq, TP[hh * 64:(hh + 1) * 64, 0:512].rearrange("p (u r) -> p r u", r=dil), [], [tk, "qT"])
                        for half in range(2):
                            dk = kT[half * 64:(half + 1) * 64, :].rearrange("p (r u) -> p r u", r=dil)[:, :, s4 * w4:(s4 + 1) * w4]
                            copy_op(evac_eng(), dk, TP[0:64, 512:1024].rearrange("p (u r) -> p r u", r=dil), [], [tk, "kT"])

                def vstage(m):
                    cs = NT // dil
                    r_, u0 = m // cs, (m % cs) * 128
                    rr["ot"] ^= 1
                    b = 4 + rr["ot"]
                    P = PSB[b]
                    for k in range(8):
                        lhs = XNT[:, k, :].rearrange("p (u r) -> p r u", r=dil)[:, r_, u0:u0 + 128]
                        op("pe", lambda e: e.matmul(P[:, 0:64], lhsT=lhs, rhs=wbuf[:, k, 192:256], start=(k == 0), stop=(k == 7)),
                           allx + [wkey], [("ps", b)])
                    op("act", lambda e: e.activation(out=V[:, m, 0:64], in_=P[:, 0:64], func=AF.Copy), [], [("ps", b), "V"])
                    op("act", lambda e: e.activation(out=V[:, m, 128:192], in_=P[:, 0:64], func=AF.Copy), [], [("ps", b), "V"])

                lazy["early"] = True
                pend = {0: stage1(0), 1: stage1(1)}
                lazy["early"] = False
                if kind == "B" and j == 0:
                    op("pool", lambda e: e.memset(V[:, :, 64:128], 1.0), [], ["V"])
                    op("pool", lambda e: e.memset(qT[64:128, 0, :], 0.0), [], ["qT"])
                    op("pool", lambda e: e.memset(qT[0:64, 1, :], 0.0), [], ["qT"])
                tps = {}
                R, keys = pend.pop(0)
                tps[0] = stageT(*stage2(0, R, keys))
                for s4 in range(1, 4):
                    if s4 + 1 < 4:
                        pend[s4 + 1] = stage1(s4 + 1)
                    if dil != 1:
                        for m in range(4 * (s4 - 1), 4 * s4):
                            vstage(m)
                    R, keys = pend.pop(s4)
                    hb3_, khb_ = stage2(s4, R, keys)
                    stageE(s4 - 1, *tps.pop(s4 - 1))
                    tps[s4] = stageT(hb3_, khb_)
                if dil != 1:
                    for m in range(12, 16):
                        vstage(m)
                stageE(3, *tps.pop(3))
                heads = [dict(qT=(lambda hh_: (lambda q0, q1: qT[:, hh_, q0:q1]))(hh),
                              kT=(lambda b_: kT[:, b_ * 128:(b_ + 1) * 128]),
                              V=(lambda hh_: (lambda b_: V[:, b_, hh_ * 64:hh_ * 64 + 128]))(hh)) for hh in range(2)]
                if kind == "B":
                    def fin(hi, g, OT, okey):
                        norm_finish(OT, okey, hi, g, Rt, 2 + j)
                    attention(heads, dense_blocks, 0.125, fin, PT, ["qT", "kT", "V"])
                elif kind == "C":
                    def fin(hi, g, OT, okey):
                        norm_finish(OT, okey, hi, g, [Rt, uf(o0 + 3072, 512)], 4 + j, esk_col=l * 4 + 2 * j + hi, use_act=True)
                    attention(heads, band_blocks(CB_MC, NT, (-1, 1)), 0.125, fin, PT, ["qT", "kT", "V"])
                else:
                    def fin(hi, g, OT, okey):
                        A_ = ACC[hi]
                        if dil == 1:
                            copy_op("dve", A_[:, g * 512:(g + 1) * 512], OT[:, :], [], [okey, ("ACC", hi)])
                        else:
                            if dil == 4:
                                dst = A_.rearrange("p (u r) -> p r u", r=4)[:, g, :]
                                src = OT[:, :]
                            else:
                                dst = A_.rearrange("p (u r) -> p r u", r=16)[:, 4 * g:4 * g + 4, :]
                                src = OT[:, :].rearrange("p (r u) -> p r u", r=4)
                            op("dve", lambda e: e.tensor_tensor(out=dst, in0=src, in1=dst, op=ALU.add), [], [okey, ("ACC", hi)])
                    attention(heads, band_blocks(CB_MD, NT // dil, (-1, 0, 1)), 0.125, fin, PT, ["qT", "kT", "V"])
                fence()
                return Rt

            for j in range(2):
                gqa_phase("B", j)
            for j in range(2):
                gqa_phase("C", j)
            for j in range(2):
                ACC = [uf(9216, 2048), uf(9216 + 4096, 2048)]
                for g3 in range(3):
                    Rt = gqa_phase("D", j, g3, ACC)
                for hi in range(2):
                    nr = slice(64 * hi, 64 * hi + 64)
                    zr = slice(64 * (1 - hi), 64 * (1 - hi) + 64)
                    A_ = ACC[hi]
                    for g in range(4):
                        cs_ = slice(g * 512, (g + 1) * 512)
                        Rg = [Rt, uf(9216 + 8192, 512)][g % 2]
                        rk = ("RtD", g % 2)
                        op("act", lambda e: e.activation(out=Rg[zr, :], in_=A_[zr, cs_], func=AF.Ln), [("ACC", hi)], [rk])
                        op("act", lambda e: e.activation(out=Rg[nr, :], in_=Rg[zr, :], func=AF.Exp, scale=-1.0), [], [rk])
                        op("dve", lambda e: e.tensor_tensor(out=Rg[nr, :], in0=A_[nr, cs_], in1=Rg[nr, :], op=ALU.mult), [("ACC", hi)], [rk])
                        op("pool", lambda e: e.tensor_tensor(out=YT[nr, 6 + j, cs_], in0=YT[nr, 6 + j, cs_], in1=Rg[nr, :], op=ALU.mult),
                           [rk], [("YT", 6 + j)])
                fence()

            if dbg and s == 0 and l == 0:
                S.dma("sp", lambda e: e.dma_start(out=dbg_d, in_=YT[:, :, :].rearrange("p a b -> p (a b)")),
                      reads=[("YT", p) for p in range(8)] + ["Uown"], writes=["dbg"], res="dbg")

            Macc = uf(0, 4096).rearrange("p (t c) -> p t c", t=4)
            mT = ub(8192, 4096).rearrange("p (k c) -> p k c", k=8)
            Gt = [uf(12288, 512), uf(13312, 512)]
            Tm = [uf(14336, 512), uf(15360, 512)]
            Mb = ub(16384, 1024)
            for G in range(4):
                for i in range(4):
                    for c in range(2):
                        wbuf, wkey = wst.get()
                        bbuf, bkey = bst.get()
                        for tt in range(4):
                            t = 4 * G + tt
                            Pg, pgk = proj(wbuf, wkey, 512, xnt_tile(t), [("XNT", t)])
                            rr["st"] ^= 1
                            ub_ = 2 + rr["st"]
                            Pu = PSB[ub_]
                            for pp in range(2):
                                op("pe", (lambda pp_, Pu_=Pu, t_=t, i_=i: (lambda e: e.matmul(
                                    Pu_[:, :], lhsT=YT[:, 2 * i_ + pp_, t_ * 128:(t_ + 1) * 128], rhs=bbuf[:, pp_, :],
                                    start=(pp_ == 0), stop=(pp_ == 1))))(pp), [("YT", 2 * i), ("YT", 2 * i + 1), bkey], [("ps", ub_)])
                            gs = (tt + c) % 2
                            op("act", (lambda Pg_=Pg, gs_=gs: (lambda e: e.activation(out=Gt[gs_], in_=Pg_[:, :], func=AF.Sigmoid)))(),
                               [], [pgk, ("Gt", gs)])
                            mdst = Macc[:, tt, c * 512:(c + 1) * 512]
                            if i == 0:
                                op("dve", (lambda Pu_=Pu, gs_=gs, md=mdst: (lambda e: e.tensor_tensor(out=md, in0=Pu_[:, :], in1=Gt[gs_], op=ALU.mult)))(),
                                   [("Gt", gs)], [("ps", ub_), ("Macc", tt, c)])
                            else:
                                op("dve", (lambda Pu_=Pu, gs_=gs: (lambda e: e.tensor_tensor(out=Tm[gs_], in0=Pu_[:, :], in1=Gt[gs_], op=ALU.mult)))(),
                                   [("Gt", gs)], [("ps", ub_), ("Tm", gs)])
                                op("pool", (lambda gs_=gs, md=mdst: (lambda e: e.tensor_tensor(out=md, in0=md, in1=Tm[gs_], op=ALU.add)))(),
                                   [("Tm", gs)], [("Macc", tt, c)])
                for tt in range(4):
                    op("act", (lambda tt_: (lambda e: e.activation(out=Mb, in_=Macc[:, tt_, :], func=AF.Copy)))(tt),
                       [("Macc", tt, 0), ("Macc", tt, 1)], ["Mb"])
                    TP, tk = transposes([(Mb[:, k * 128:(k + 1) * 128], 128) for k in range(8)], ["Mb"])
                    copy_op("dve", mT[:, :, tt * 128:(tt + 1) * 128], TP[:, :].rearrange("p (k c) -> p k c", k=8), [], [tk, "mT"])
                for c in range(2):
                    wbuf, wkey = wst.get()
                    for tt in range(4):
                        t = 4 * G + tt
                        rr["ot"] ^= 1
                        ob = 4 + rr["ot"]
                        Po = PSB[ob]
                        for k in range(8):
                            op("pe", (lambda k_, Po_=Po, tt_=tt: (lambda e: e.matmul(Po_[:, :], lhsT=mT[:, k_, tt_ * 128:(tt_ + 1) * 128], rhs=wbuf[:, k_, :],
                                                                                   start=(k_ == 0), stop=(k_ == 7))))(k), ["mT", wkey], [("ps", ob)])
                        xs = X[:, t, c * 512:(c + 1) * 512]
                        op("dve", (lambda Po_=Po, xs_=xs: (lambda e: e.tensor_tensor(out=xs_, in0=Po_[:, :], in1=xs_, op=ALU.add)))(),
                           [], [("ps", ob), ("X", t)])
            fence()

        def final(s):
            junk = ub(0, 1024)
            gF = uf(1024, 1024)
            OUT = [uf(3072 + 2048 * i, 1024) for i in range(4)]
            ss = ST[:, 0:16]
            rstd = ST[:, 16:32]
            flush_fence()
            S.dma("sp", lambda e: e.dma_start(out=gF, in_=gfin_d), reads=["Uown"], writes=["gF"], res="gF")
            op("pool", lambda e: e.memset(ss, 0.0), [], ["ss"])
            for t in range(NT):
                op("act", (lambda t_: (lambda e: e.activation(out=junk, in_=X[:, t_, :], func=AF.Square, accum_out=ST[:, t_:t_ + 1])))(t),
                   [("X", t)], ["junk", "ss"])
            op("act", lambda e: e.activation(out=rstd, in_=ss, func=AF.Ln, scale=1.0 / D, bias=eps_t), ["ss", "CF"], ["rstd"])
            op("act", lambda e: e.activation(out=rstd, in_=rstd, func=AF.Exp, scale=-0.5), [], ["rstd"])
            for t in range(NT):
                o = OUT[t % 4]
                op("dve", (lambda t_, o_: (lambda e: e.scalar_tensor_tensor(out=o_, in0=X[:, t_, :], scalar=ST[:, 16 + t_:17 + t_], in1=gF,
                                                                            op0=ALU.mult, op1=ALU.mult)))(t, o),
                   [("X", t), "rstd", "gF"], [("OUT", t % 4)])
                S.dma("sp", (lambda t_, o_: (lambda e: e.dma_start(out=y_d[s, t_ * 128:(t_ + 1) * 128, :], in_=o_)))(t, o),
                      reads=[("OUT", t % 4), "Uown"], writes=[("y", s, t)], res=("y", t % 4))
                if s + 1 < nseq:
                    S.dma("sp", (lambda t_: (lambda e: e.dma_start(out=X[:, t_, :], in_=x_d[s + 1, t_ * 128:(t_ + 1) * 128, :])))(t),
                          writes=[("X", t)], res=("X", t))
            fence()

        x_load(0)
        fence()
        for s in range(nseq):
            S.epoch = s
            for l in range(depth):
                layer(s, l)
            final(s)
        fin_res = [("y", i) for i in range(4)] + (["dbg"] if dbg else [])
        nsem = S.emit(nc, final_wait_res=fin_res)
    stats = {k: len(v) for k, v in S.q.items()}
    stats["sems"] = nsem
    return nc, stats


_CACHE = {}


def kernel(x_prompt, x_sample, norm_in, w_in, a_q_norm, w_q_up, a_kv_norm, w_kv_up,
           b_q_norm, b_k_norm, c_sink, w_branch, w_out, final_norm):
    f = np.float32
    xs = np.concatenate([np.asarray(x_prompt, f), np.asarray(x_sample, f)], axis=0)
    nseq = xs.shape[0] // NCORES
    if "nc" not in _CACHE:
        _CACHE["nc"] = build(nseq)[0]
    nc = _CACHE["nc"]
    cf, cb = _host_consts()
    pf = _host_params(np.asarray(norm_in, f), np.asarray(a_q_norm, f), np.asarray(a_kv_norm, f),
                      np.asarray(b_q_norm, f), np.asarray(b_k_norm, f), np.asarray(c_sink, f))
    gfin = np.ascontiguousarray(np.broadcast_to(np.asarray(final_norm, f)[None, :], (128, D)))
    shared = {
        "w_in": np.ascontiguousarray(np.asarray(w_in, f)),
        "w_q_up": np.ascontiguousarray(np.asarray(w_q_up, f)),
        "w_kv_up": np.ascontiguousarray(np.asarray(w_kv_up, f)),
        "w_branch": np.ascontiguousarray(np.asarray(w_branch, f).reshape(L, 1024, 1024)),
        "w_out": np.ascontiguousarray(np.asarray(w_out, f)),
        "gfin": gfin, "cf": cf, "cb": cb, "pf": pf,
    }
    in_maps = []
    for c in range(NCORES):
        m = dict(shared)
        m["x"] = np.ascontiguousarray(xs[c * nseq:(c + 1) * nseq])
        in_maps.append(m)
    res = run_bass_kernel_spmd(nc, in_maps, core_ids=list(range(NCORES)))
    ys = np.concatenate([np.asarray(r["y"], f) for r in res.results], axis=0)
    nb = x_prompt.shape[0]
    return (np.ascontiguousarray(ys[:nb]), np.ascontiguousarray(ys[nb:]))
```

```python
import math
from contextlib import ExitStack

import numpy as np
import ml_dtypes
import concourse.bass as bass
import concourse.mybir as mybir
from concourse.bass_utils import run_bass_kernel_spmd

F32 = mybir.dt.float32
BF16 = mybir.dt.bfloat16
ALU = mybir.AluOpType
AF = mybir.ActivationFunctionType
AX = mybir.AxisListType

T = 2048
D = 1024
NT = 16
NIN = 8096
L = 2
OFF_A, OFF_B, OFF_C, OFF_D, OFF_Z, OFF_MG = 0, 416, 928, 1440, 2976, 4000
EPS = 1e-6
NCORES = 8


class Op:
    __slots__ = ("eng", "fn", "idx", "waits", "signal", "sem", "count", "is_dma", "epoch", "deps")

    def __init__(self, eng, fn, is_dma):
        self.eng = eng
        self.fn = fn
        self.is_dma = is_dma
        self.waits = {}
        self.signal = False
        self.sem = None
        self.count = 0
        self.deps = ()


class _Rec:
    def __init__(self):
        self.call = None

    def __getattr__(self, name):
        def f(*a, **k):
            self.call = (name, a, k)
            return self
        return f


class Sched:
    def __init__(self):
        self.q = {k: [] for k in ("pe", "act", "dve", "pool", "sp")}
        self.last_w = {}
        self.readers = {}
        self.epoch = 0
        self.dma_res_count = {}

    def _add(self, eng, fn, reads, writes, is_dma, dma_res=None):
        rec = _Rec()
        fn(rec)
        call = rec.call
        assert call is not None
        fn = (lambda c: (lambda e: getattr(e, c[0])(*c[1], **c[2])))(call)
        op = Op(eng, fn, is_dma)
        op.epoch = self.epoch
        op.idx = len(self.q[eng])
        deps = set()
        for r in reads:
            w = self.last_w.get(r)
            if w is not None:
                deps.add(w)
        for w_ in writes:
            w = self.last_w.get(w_)
            if w is not None:
                deps.add(w)
            rl = self.readers.get(w_)
            if rl:
                deps.update(rl)
        for r in reads:
            self.readers.setdefault(r, []).append(op)
        for w_ in writes:
            self.last_w[w_] = op
            self.readers[w_] = []
        if is_dma:
            op.sem = ("dma", dma_res)
            c = self.dma_res_count.get(dma_res, 0) + 16
            self.dma_res_count[dma_res] = c
            op.count = c
        best = {}
        keep = []
        for d in deps:
            if d.is_dma:
                keep.append(d)
            else:
                b = best.get(d.eng)
                if b is None or d.idx > b.idx:
                    best[d.eng] = d
        keep.extend(best.values())
        op.deps = keep
        self.q[eng].append(op)
        return op

    def op(self, eng, fn, reads=(), writes=()):
        return self._add(eng, fn, reads, writes, False)

    def dma(self, eng, fn, reads=(), writes=(), res=None):
        return self._add(eng, fn, reads, writes, True, res)

    @staticmethod
    def _skip(d, op):
        if d.is_dma or op.is_dma or d.eng != op.eng:
            return False
        return d.eng == "pe" or (op.idx - d.idx) > 2

    def finalize(self):
        for ops in self.q.values():
            for op in ops:
                for d in op.deps:
                    if not d.is_dma and not self._skip(d, op):
                        d.signal = True
        cnt = {}
        for eng, ops in self.q.items():
            for op in ops:
                if not op.is_dma and op.signal:
                    key = ("eng", eng, op.epoch)
                    cnt[key] = cnt.get(key, 0) + 1
                    op.sem = key
                    op.count = cnt[key]
        sems = set()
        for eng, ops in self.q.items():
            seen = {}
            for op in ops:
                w = {}
                for d in op.deps:
                    if self._skip(d, op):
                        continue
                    if w.get(d.sem, 0) < d.count:
                        w[d.sem] = d.count
                for s, c in list(w.items()):
                    if seen.get(s, 0) >= c:
                        del w[s]
                    else:
                        seen[s] = c
                op.waits = w
                sems.update(w.keys())
                if op.is_dma or op.signal:
                    sems.add(op.sem)
        return sorted(sems, key=str)

    def emit(self, nc, final_wait_res=()):
        sems = self.finalize()
        with ExitStack() as es:
            h = {}
            for i, s in enumerate(sems):
                h[s] = es.enter_context(nc.semaphore("s%d" % i))
            block = es.enter_context(nc.Block())
            finals = [(("dma", r), self.dma_res_count[r]) for r in final_wait_res if r in self.dma_res_count]

            def run(engobj, ops, extra=()):
                for op in ops:
                    for s, c in op.waits.items():
                        engobj.wait_ge(h[s], c)
                    ins = op.fn(engobj)
                    if op.is_dma:
                        ins.then_inc(h[op.sem], 16)
                    elif op.signal:
                        ins.then_inc(h[op.sem], 1)
                for s, c in extra:
                    engobj.wait_ge(h[s], c)

            @block.sync
            def _(e):
                run(e, self.q["sp"], finals)

            @block.tensor
            def _(e):
                run(e, self.q["pe"])

            @block.scalar
            def _(e):
                run(e, self.q["act"])

            @block.vector
            def _(e):
                run(e, self.q["dve"])

            @block.gpsimd
            def _(e):
                run(e, self.q["pool"])
        return len(sems)


def _host_consts():
    pos = (np.arange(NT)[None, :] * 128 + np.arange(128)[:, None]).astype(np.float64)
    f64 = np.power(np.float32(10000.0), -np.arange(32, dtype=np.float32) * np.float32(2.0) / np.float32(64)).astype(np.float32)
    f32_ = np.power(np.float32(10000.0), -np.arange(16, dtype=np.float32) * np.float32(2.0) / np.float32(32)).astype(np.float32)
    pos = pos.astype(np.float32)
    a64 = (pos[:, :, None] * f64[None, None, :]).astype(np.float64)
    cos64, sin64 = np.cos(a64), np.sin(a64)
    aA = (pos[:, :, None] * f32_[None, None, :]).astype(np.float64)
    cosA, sinA = np.cos(aA), np.sin(aA)
    rows = np.floor(pos / 64)
    cols = pos - rows * 64
    ar = (rows[:, :, None] * f32_[None, None, :]).astype(np.float64)
    ac = (cols[:, :, None] * f32_[None, None, :]).astype(np.float64)
    cosX = np.stack([np.cos(ar), np.cos(ac)], axis=2)
    sinX = np.stack([np.sin(ar), np.sin(ac)], axis=2)
    cf = np.concatenate([
        cos64.reshape(128, -1), sin64.reshape(128, -1),
        cosA.reshape(128, -1), sinA.reshape(128, -1),
        cosX.reshape(128, -1), sinX.reshape(128, -1),
        np.full((128, 1), EPS),
    ], axis=1).astype(np.float32)
    ident = np.eye(128)
    ki = np.arange(128)[:, None]
    qq = np.arange(384)[None, :] - 128
    maskC = np.where(np.abs(qq - ki) <= 128, 0.0, -30000.0)
    maskD = np.where(np.abs(qq - ki) <= 64, 0.0, -30000.0)
    cb = np.concatenate([ident, maskC, maskD], axis=1).astype(ml_dtypes.bfloat16)
    return np.ascontiguousarray(cf), np.ascontiguousarray(cb)


CF_COS64, CF_SIN64, CF_COSA, CF_SINA, CF_COSX, CF_SINX, CF_EPS = 0, 512, 1024, 1280, 1536, 2048, 2560
NCF = 2561
CB_ID, CB_MC, CB_MD = 0, 128, 512
NCB = 896
PF_GB, PF_SINK, PF_GIN, PF_GQ, PF_GKV = 0, 768, 776, 792, 796
NPF = 798


def _host_params(norm_in, a_q_norm, a_kv_norm, b_q_norm, b_k_norm, c_sink):
    pf = np.zeros((128, NPF), np.float32)
    for l in range(L):
        gb = np.concatenate([np.tile(b_q_norm[l][None, :], (4, 1)), np.tile(b_k_norm[l][None, :], (2, 1))], 0)
        pf[:, PF_GB + l * 384: PF_GB + (l + 1) * 384] = gb.reshape(1, 384)
        pf[:, PF_SINK + l * 4: PF_SINK + (l + 1) * 4] = c_sink[l][None, :]
        pf[:, PF_GIN + l * 8: PF_GIN + (l + 1) * 8] = norm_in[l].reshape(8, 128).T
        pf[:, PF_GQ + l * 2: PF_GQ + (l + 1) * 2] = a_q_norm[l].reshape(2, 128).T
        pf[:, PF_GKV + l] = a_kv_norm[l]
    return pf


def build(nseq, depth=L, dbg=False):
    nc = bass.Bass("TRN2", target_bir_lowering=False)
    dram = {}

    def din(name, shape, dt=F32):
        dram[name] = nc.dram_tensor(name, list(shape), dt, kind="ExternalInput")
        return dram[name].ap()

    x_d = din("x", [nseq, T, D])
    win_d = din("w_in", [L, D, NIN])
    wq_d = din("w_q_up", [L, 256, 384])
    wkv_d = din("w_kv_up", [L, 128, 512])
    wbr_d = din("w_branch", [L, 1024, 1024])
    wout_d = din("w_out", [L, D, D])
    gfin_d = din("gfin", [128, D])
    cf_d = din("cf", [128, NCF])
    cb_d = din("cb", [128, NCB], BF16)
    pf_d = din("pf", [128, NPF])
    y_d = nc.dram_tensor("y", [nseq, T, D], F32, kind="ExternalOutput").ap()
    s_in = nc.dram_tensor("s_in", [L, D, NIN], BF16).ap()
    s_q = nc.dram_tensor("s_q", [L, 256, 384], BF16).ap()
    s_kv = nc.dram_tensor("s_kv", [L, 128, 512], BF16).ap()
    s_br = nc.dram_tensor("s_br", [L, 1024, 1024], BF16).ap()
    s_out = nc.dram_tensor("s_out", [L, D, D], BF16).ap()
    if dbg:
        dbg_d = nc.dram_tensor("dbg", [128, 8 * T], BF16, kind="ExternalOutput").ap()

    S = Sched()
    UW = 21504

    with ExitStack() as es:
        def sb(name, shape, dt):
            return es.enter_context(nc.sbuf_tensor(name, list(shape), dt))

        def ps(name, shape, dt):
            return es.enter_context(nc.psum_tensor(name, list(shape), dt))

        X = sb("X", [128, NT, D], F32)
        XNT = sb("XNT", [128, 8, T], BF16)
        YT = sb("YT", [128, 8, T], BF16)
        WB = [sb("WB%d" % i, [128, 8, 512], BF16) for i in range(2)]
        WBR = [sb("WBR%d" % i, [128, 2, 512], BF16) for i in range(2)]
        WQ = sb("WQ", [128, 2, 384], BF16)
        WKV = sb("WKV", [128, 512], BF16)
        CF = sb("CF", [128, NCF], F32)
        CB = sb("CB", [128, NCB], BF16)
        PF = sb("PF", [128, NPF], F32)
        ST = sb("ST", [128, 64], F32)
        ESK = sb("ESK", [128, 8], F32)
        DUM = sb("DUM", [128, 2], F32)
        U = sb("U", [128, UW], BF16)
        PSALL = ps("PSALL", [128, 4096], F32)
        PSB = [PSALL[:, i * 512:(i + 1) * 512] for i in range(6)]
        PSR = [PSALL[:, r * 1024:(r + 1) * 1024] for r in range(2)]
        TPB = [PSALL[:, (6 + i) * 512:(7 + i) * 512].bitcast(BF16) for i in range(2)]
        STR = [(PSALL[:, 0:1024], [("ps", 0), ("ps", 1)]), (PSALL[:, 1024:2048], [("ps", 2), ("ps", 3)]),
               (PSALL[:, 3072:4096], [("tp", 0), ("tp", 1)])]

        ident = CB[:, CB_ID:CB_ID + 128]
        eps_t = CF[:, CF_EPS:CF_EPS + 1]

        def ub(off, n):
            assert off + n <= UW, (off, n)
            return U[:, off:off + n]

        def uf(off, n):
            assert off % 2 == 0 and off + 2 * n <= UW, (off, n)
            return U[:, off:off + 2 * n].bitcast(F32)

        lazy = {"pending": False, "early": False}

        def op(eng, fn, reads=(), writes=()):
            if lazy["early"]:
                return S.op(eng, fn, list(reads), writes)
            if lazy["pending"]:
                flush_fence()
            return S.op(eng, fn, list(reads) + ["Uown"], writes)

        def fence():
            lazy["pending"] = True

        def flush_fence():
            if lazy["pending"]:
                lazy["pending"] = False
                S.op("pool", lambda e: e.memset(DUM[:, 0:1], 0.0), reads=(), writes=["Uown", "DUM"])

        SCR = ["scr0", "scr1", "scr2"]
        rr = {"ev": 0, "ps": 0, "tp": 0, "st": 0, "ot": 0}

        def evac_eng():
            rr["ev"] ^= 1
            return "act" if rr["ev"] else "dve"

        def copy_op(eng, out, in_, reads, writes):
            if eng == "act":
                return op("act", lambda e: e.activation(out=out, in_=in_, func=AF.Copy), reads, writes)
            return op(eng, lambda e: e.tensor_copy(out=out, in_=in_), reads, writes)

        def next_ps():
            rr["ps"] ^= 1
            return rr["ps"]

        def next_tp():
            rr["tp"] ^= 1
            return rr["tp"]

        class WStream:
            def __init__(self, bufs, tag):
                self.bufs = bufs
                self.tag = tag
                self.descs = []
                self.issued = 0
                self.i = 0

            def _issue(self, j):
                slot = j % len(self.bufs)
                for (dst_fn, src) in self.descs[j]:
                    dst = dst_fn(self.bufs[slot])
                    S.dma("sp", (lambda d, s: (lambda e: e.dma_start(out=d, in_=s)))(dst, src),
                          reads=SCR, writes=[(self.tag, slot)], res=(self.tag, slot))

            def get(self):
                n = len(self.bufs)
                while self.issued < min(len(self.descs), self.i + n):
                    self._issue(self.issued)
                    self.issued += 1
                slot = self.i % n
                self.i += 1
                return self.bufs[slot], (self.tag, slot)

        wst = WStream(WB, "WB")
        bst = WStream(WBR, "WBR")

        def win_cols(l, ranges):
            out = []
            for (doff, c0, n) in ranges:
                src = s_in[l, :, c0:c0 + n].rearrange("(k p) c -> p k c", p=128)
                out.append(((lambda o, nn: (lambda buf: buf[:, :, o:o + nn]))(doff, n), src))
            return out

        def build_descs():
            for _s in range(nseq):
                for l in range(depth):
                    wst.descs.append(win_cols(l, [(0, OFF_Z, 512)]))
                    wst.descs.append(win_cols(l, [(0, OFF_Z + 512, 512)]))
                    wst.descs.append(win_cols(l, [(0, OFF_A, 416)]))
                    for off in (OFF_B, OFF_C):
                        for j in range(2):
                            wst.descs.append(win_cols(l, [(0, off + 128 * j, 128), (128, off + 256 + 64 * j, 64),
                                                          (192, off + 384 + 64 * j, 64)]))
                    for j in range(2):
                        for g in range(3):
                            off = OFF_D + 512 * g
                            wst.descs.append(win_cols(l, [(0, off + 128 * j, 128), (128, off + 256 + 64 * j, 64),
                                                          (192, off + 384 + 64 * j, 64)]))
                    for G in range(4):
                        for i in range(4):
                            for c in range(2):
                                wst.descs.append(win_cols(l, [(0, OFF_MG + i * 1024 + c * 512, 512)]))
                                src = s_br[l, i * 256:(i + 1) * 256, c * 512:(c + 1) * 512].rearrange("(k p) c -> p k c", p=128)
                                bst.descs.append([((lambda buf: buf[:, :, :]), src)])
                        for c in range(2):
                            src = s_out[l, :, c * 512:(c + 1) * 512].rearrange("(k p) c -> p k c", p=128)
                            wst.descs.append([((lambda buf: buf[:, :, :]), src)])

        build_descs()

        S.dma("sp", lambda e: e.dma_start(out=CF[:], in_=cf_d), writes=["CF"], res="CF")
        S.dma("sp", lambda e: e.dma_start(out=CB[:], in_=cb_d), writes=["CB"], res="CB")
        S.dma("sp", lambda e: e.dma_start(out=PF[:], in_=pf_d), writes=["PF"], res="PF")
        op("act", lambda e: e.activation(out=ESK[:, 0:8], in_=PF[:, PF_SINK:PF_SINK + 8], func=AF.Exp), ["PF"], ["ESK"])

        def convert():
            stg = [uf(0, 2048), uf(4096, 2048), uf(8192, 2048)]
            stb = [ub(12288, 2048), ub(14336, 2048), ub(16384, 2048)]
            pieces = []
            for l in range(L):
                for k in range(8):
                    for c0 in range(0, NIN, 2048):
                        n = min(2048, NIN - c0)
                        pieces.append((win_d[l, k * 128:(k + 1) * 128, c0:c0 + n], s_in[l, k * 128:(k + 1) * 128, c0:c0 + n], n,
                                       PF[:, PF_GIN + l * 8 + k: PF_GIN + l * 8 + k + 1]))
                for k in range(2):
                    pieces.append((wq_d[l, k * 128:(k + 1) * 128, :], s_q[l, k * 128:(k + 1) * 128, :], 384,
                                   PF[:, PF_GQ + l * 2 + k: PF_GQ + l * 2 + k + 1]))
                pieces.append((wkv_d[l], s_kv[l], 512, PF[:, PF_GKV + l: PF_GKV + l + 1]))
                for k in range(8):
                    pieces.append((wbr_d[l, k * 128:(k + 1) * 128, :], s_br[l, k * 128:(k + 1) * 128, :], 1024, None))
                for k in range(8):
                    pieces.append((wout_d[l, k * 128:(k + 1) * 128, :], s_out[l, k * 128:(k + 1) * 128, :], 1024, None))
            engs = ["dve", "act"]
            for i, (src, dst, n, g) in enumerate(pieces):
                sl = i % 3
                a, b = stg[sl], stb[sl]
                S.dma("sp", (lambda a_, s_, n_: (lambda e: e.dma_start(out=a_[:, 0:n_], in_=s_)))(a, src, n),
                      reads=["Uown"], writes=[("stg", sl)], res=("stg", sl))
                eng = engs[i % 2]
                if g is None:
                    copy_op(eng, b[:, 0:n], a[:, 0:n], [("stg", sl)], [("stb", sl)])
                elif eng == "act":
                    op("act", (lambda a_, b_, n_, g_: (lambda e: e.activation(out=b_[:, 0:n_], in_=a_[:, 0:n_], func=AF.Copy, scale=g_)))(a, b, n, g),
                       [("stg", sl), "PF"], [("stb", sl)])
                else:
                    op(eng, (lambda a_, b_, n_, g_: (lambda e: e.tensor_scalar_mul(out=b_[:, 0:n_], in0=a_[:, 0:n_], scalar1=g_)))(a, b, n, g),
                       [("stg", sl), "PF"], [("stb", sl)])
                S.dma("pool", (lambda b_, d_, n_: (lambda e: e.dma_start(out=d_, in_=b_[:, 0:n_])))(b, dst, n),
                      reads=[("stb", sl), "Uown"], writes=[("scr", i)], res=("scr", sl))
                S.last_w["scr%d" % sl] = S.q["pool"][-1]

        convert()

        def x_load(s):
            for t in range(NT):
                S.dma("sp", (lambda t_: (lambda e: e.dma_start(out=X[:, t_, :], in_=x_d[s, t_ * 128:(t_ + 1) * 128, :])))(t),
                      writes=[("X", t)], res=("X", t))

        def proj(wbuf, wkey, ncols, lhs_fn, xkeys, nk=8, wcol0=0):
            b = next_ps()
            P = PSB[b]
            for k in range(nk):
                op("pe", (lambda k_: (lambda e: e.matmul(P[:, 0:ncols], lhsT=lhs_fn(k_), rhs=wbuf[:, k_, wcol0:wcol0 + ncols],
                                                         start=(k_ == 0), stop=(k_ == nk - 1))))(k),
                   list(xkeys) + [wkey], [("ps", b)])
            return P, ("ps", b)

        def xnt_tile(t):
            return lambda k: XNT[:, k, t * 128:(t + 1) * 128]

        def transposes(srcs, reads):
            b = next_tp()
            TP = TPB[b]
            for i, (ap, w) in enumerate(srcs):
                op("pe", (lambda i_, ap_, w_: (lambda e: e.transpose(out=TP[0:w_, i_ * 128:(i_ + 1) * 128], in_=ap_, identity=ident)))(i, ap, w),
                   list(reads) + ["CB"], [("tp", b)])
            return TP, ("tp", b)

        def rope(src, dst, cos, sin, H, nh, w, tA, tB, reads, writes, eng="pool", sfx=0):
            sv = src.rearrange("p (h n x w) -> p h n x w", h=H, n=nh, x=2, w=w)
            dv = dst.rearrange("p (h n x w) -> p h n x w", h=H, n=nh, x=2, w=w)
            shp = [128, H, nh, w]
            cb_ = cos.rearrange("p (n w) -> p n w", n=nh).unsqueeze(1).to_broadcast(shp)
            sb_ = sin.rearrange("p (n w) -> p n w", n=nh).unsqueeze(1).to_broadcast(shp)
            x1, x2 = sv[:, :, :, 0, :], sv[:, :, :, 1, :]
            a = tA.rearrange("p (h n w) -> p h n w", h=H, n=nh)
            b = tB.rearrange("p (h n w) -> p h n w", h=H, n=nh)
            rd = list(reads) + ["CF"]
            kA, kB = ("tA", sfx), ("tB", sfx)
            op(eng, lambda e: e.tensor_tensor(out=a, in0=x1, in1=cb_, op=ALU.mult), rd, [kA])
            op(eng, lambda e: e.tensor_tensor(out=b, in0=x2, in1=sb_, op=ALU.mult), rd, [kB])
            op(eng, lambda e: e.tensor_tensor(out=dv[:, :, :, 0, :], in0=a, in1=b, op=ALU.subtract), [kA, kB], writes)
            op(eng, lambda e: e.tensor_tensor(out=a, in0=x1, in1=sb_, op=ALU.mult), rd, [kA])
            op(eng, lambda e: e.tensor_tensor(out=b, in0=x2, in1=cb_, op=ALU.mult), rd, [kB])
            op(eng, lambda e: e.tensor_tensor(out=dv[:, :, :, 1, :], in0=a, in1=b, op=ALU.add), [kA, kB], writes)

        def attention(heads, blocks_fn, scale, finish, PT, reads):
            items = []
            for hi in range(len(heads)):
                for g in range(4):
                    bl = blocks_fn(g)
                    for bi, (b, qlo, qhi, mask) in enumerate(bl):
                        items.append((hi, g, bi, len(bl), b, qlo, qhi, mask))
            groups = []
            i = 0
            while i < len(items):
                if i + 1 < len(items) and (items[i][6] - items[i][5]) == (items[i + 1][6] - items[i + 1][5]):
                    groups.append([items[i], items[i + 1]])
                    i += 2
                else:
                    groups.append([items[i]])
                    i += 1
            ng = len(groups)
            assert len(PT) == 3

            def issue_qk(gi):
                R, rkeys = STR[gi % 3]
                for ii, (hi, g, bi, nb, b, qlo, qhi, mask) in enumerate(groups[gi]):
                    hd = heads[hi]
                    n = qhi - qlo
                    kT_, qT_ = hd["kT"](b), hd["qT"](qlo, qhi)
                    ml = mask or []
                    op("pe", lambda e: e.matmul(R[:, ii * 512:ii * 512 + n], lhsT=kT_, rhs=qT_, start=True, stop=True), reads, rkeys)
                    for mi, (co, mo, mw) in enumerate(ml):
                        op("pe", lambda e: e.matmul(R[:, ii * 512 + co:ii * 512 + co + mw], lhsT=ident, rhs=CB[:, mo:mo + mw],
                                                    start=False, stop=True, skip_group_check=True), ["CB"], rkeys)

            issue_qk(0)
            if ng > 1:
                issue_qk(1)
            ob = None
            for gi, grp in enumerate(groups):
                if gi + 2 < ng:
                    issue_qk(gi + 2)
                R, rkeys = STR[gi % 3]
                slot = gi % 3
                PTt = PT[slot]
                ns = [it[6] - it[5] for it in grp]
                if len(grp) == 2 and ns[0] == ns[1]:
                    n = ns[0]
                    op("act", lambda e: e.activation(out=PTt.rearrange("p (g c) -> p g c", g=2)[:, :, 0:n],
                                                     in_=R.rearrange("p (g c) -> p g c", g=2)[:, :, 0:n], func=AF.Exp, scale=scale),
                       [], rkeys + [("PT", slot)])
                else:
                    for ii, n in enumerate(ns):
                        op("act", lambda e: e.activation(out=PTt[:, ii * 512:ii * 512 + n], in_=R[:, ii * 512:ii * 512 + n], func=AF.Exp, scale=scale),
                           [], rkeys + [("PT", slot)])
                for ii, (hi, g, bi, nb, b, qlo, qhi, mask) in enumerate(grp):
                    n = ns[ii]
                    hd = heads[hi]
                    if bi == 0:
                        rr["ot"] ^= 1
                        ob = 4 + rr["ot"]
                    OT = PSB[ob]
                    c0 = qlo - 512 * g
                    V_ = hd["V"](b)
                    op("pe", lambda e: e.matmul(OT[:, c0:c0 + n], lhsT=V_, rhs=PTt[:, ii * 512:ii * 512 + n], start=(bi == 0), stop=(bi == nb - 1),
                                                skip_group_check=True),
                       list(reads) + [("PT", slot)], [("ps", ob)])
                    if bi == nb - 1:
                        finish(hi, g, OT, ("ps", ob))

        def norm_finish(OT, okey, hp, g, Rt, ypair, esk_col=None, use_act=False):
            rk = "Rt"
            if isinstance(Rt, list):
                rr["rt"] = rr.get("rt", 0) ^ 1
                rk = ("Rt", rr["rt"])
                Rt = Rt[rr["rt"]]
            nr = slice(64 * hp, 64 * hp + 64)
            zr = slice(64 * (1 - hp), 64 * (1 - hp) + 64)
            cols = slice(512 * g, 512 * (g + 1))
            if use_act:
                if esk_col is not None:
                    op("act", lambda e: e.activation(out=Rt[zr, :], in_=OT[zr, :], func=AF.Ln, bias=ESK[zr, esk_col:esk_col + 1]), ["ESK"], [okey, rk])
                else:
                    op("act", lambda e: e.activation(out=Rt[zr, :], in_=OT[zr, :], func=AF.Ln), [], [okey, rk])
                op("act", lambda e: e.activation(out=Rt[nr, :], in_=Rt[zr, :], func=AF.Exp, scale=-1.0), [], [rk])
            elif esk_col is not None:
                op("dve", lambda e: e.tensor_scalar_add(out=Rt[zr, :], in0=OT[zr, :], scalar1=ESK[zr, esk_col:esk_col + 1]),
                   ["ESK"], [okey, rk])
                op("dve", lambda e: e.reciprocal(out=Rt[nr, :], in_=Rt[zr, :]), [], [rk])
            else:
                op("dve", lambda e: e.reciprocal(out=Rt[nr, :], in_=OT[zr, :]), [], [okey, rk])
            op("dve", lambda e: e.tensor_tensor(out=Rt[nr, :], in0=OT[nr, :], in1=Rt[nr, :], op=ALU.mult), [], [okey, rk])
            op("pool", lambda e: e.tensor_tensor(out=YT[nr, ypair, cols], in0=YT[nr, ypair, cols], in1=Rt[nr, :], op=ALU.mult),
               [rk], [("YT", ypair)])

        def dense_blocks(g):
            return [(b, 512 * g, 512 * (g + 1), None) for b in range(NT)]

        def band_blocks(maskoff, cs, need):
            def f(g):
                out = []
                for b in range(NT):
                    c = b // cs
                    alo = max(b - 1, c * cs, 4 * g)
                    ahi = min(b + 1, c * cs + cs - 1, 4 * g + 3)
                    if alo > ahi:
                        continue
                    ml = []
                    for a in range(alo, ahi + 1):
                        if (a - b) in need:
                            co, mo = (a - alo) * 128, maskoff + (a - b + 1) * 128
                            if ml and ml[-1][0] + ml[-1][2] == co and ml[-1][1] + ml[-1][2] == mo:
                                ml[-1] = (ml[-1][0], ml[-1][1], ml[-1][2] + 128)
                            else:
                                ml.append((co, mo, 128))
                    out.append((b, alo * 128, (ahi + 1) * 128, ml))
                return out
            return f

        def layer(s, l):
            junk = ub(0, 1024)
            xn = [ub(1024, 1024), ub(2048, 1024)]
            ss = ST[:, 0:16]
            rstd = ST[:, 16:32]
            have_stats = l > 0
            if not have_stats:
                op("pool", lambda e: e.memset(ss, 0.0), [], [("ss", g4) for g4 in range(4)])

            def stats(g4):
                if not have_stats:
                    for t in range(4 * g4, 4 * g4 + 4):
                        op("act", lambda e: e.activation(out=junk, in_=X[:, t, :], func=AF.Square, accum_out=ST[:, t:t + 1]),
                           [("X", t)], ["junk", ("ss", g4)])
                r4 = ST[:, 16 + 4 * g4:20 + 4 * g4]
                op("act", lambda e: e.activation(out=r4, in_=ST[:, 4 * g4:4 * g4 + 4], func=AF.Ln, scale=1.0 / D, bias=eps_t),
                   [("ss", g4), "CF"], [("rstd", g4)])
                op("act", lambda e: e.activation(out=r4, in_=r4, func=AF.Exp, scale=-0.5), [], [("rstd", g4)])

            def normalize(g4):
                for t in range(4 * g4, 4 * g4 + 4):
                    xs = xn[t % 2]
                    op("dve", lambda e: e.tensor_scalar_mul(out=xs, in0=X[:, t, :], scalar1=ST[:, 16 + t:17 + t]),
                       [("X", t), ("rstd", g4)], [("xn", t % 2)])
                    TP, tk = transposes([(xs[:, k * 128:(k + 1) * 128], 128) for k in range(8)], [("xn", t % 2)])
                    copy_op(evac_eng(), XNT[:, :, t * 128:(t + 1) * 128], TP[:, :].rearrange("p (k c) -> p k c", k=8), [], [tk, ("XNT", t)])

            stats(0)
            for g4 in range(4):
                if g4 + 1 < 4:
                    stats(g4 + 1)
                normalize(g4)
            allx = [("XNT", t) for t in range(NT)]
            S.dma("sp", lambda e: e.dma_start(out=WQ[:], in_=s_q[l].rearrange("(k p) c -> p k c", p=128)), reads=SCR + ["WQ"], writes=["WQ"], res="WQ")
            S.dma("sp", lambda e: e.dma_start(out=WKV[:], in_=s_kv[l]), reads=SCR, writes=["WKV"], res="WKV")
            fence()

            lazy["early"] = True
            for half in range(2):
                wbuf, wkey = wst.get()
                for pp in range(4):
                    pair = half * 4 + pp
                    for g in range(4):
                        b = next_ps()
                        P = PSB[b]
                        for k in range(8):
                            op("pe", (lambda k_, P_=P, pp_=pp, g_=g: (lambda e: e.matmul(
                                P_[:, :], lhsT=wbuf[:, k_, pp_ * 128:(pp_ + 1) * 128], rhs=XNT[:, k_, g_ * 512:(g_ + 1) * 512],
                                start=(k_ == 0), stop=(k_ == 7))))(k), allx + [wkey], [("ps", b)])
                        op("act", (lambda P_=P, pair_=pair, g_=g: (lambda e: e.activation(
                            out=YT[:, pair_, g_ * 512:(g_ + 1) * 512], in_=P_[:, :], func=AF.Silu)))(), [], [("ps", b), ("YT", pair)])
            lazy["early"] = False

            qlT = ub(0, 4096).rearrange("p (c t) -> p c t", c=2)
            kvlT = ub(4096, 2048)
            qTk = ub(6144, 2048)
            kTh = ub(8192, 2048)
            Vh = ub(10240, 2048).rearrange("p (t c) -> p t c", t=NT)
            tb = 12288
            krf = uf(tb + 512, 512)
            krb = ub(tb + 1536, 512)
            tA = uf(tb + 2048, 256)
            tB = uf(tb + 2560, 256)
            setsA = [dict(hAb=ub(tb, 416), qf=uf(tb + 3072, 96), qb=ub(tb + 3264, 96), kvb=ub(tb + 3392, 64),
                          ta=uf(21312, 16), tb=uf(21344, 16), sq=tA[:, 0:256]),
                     dict(hAb=ub(20480, 416), qf=uf(20896, 96), qb=ub(21088, 96), kvb=ub(21184, 64),
                          ta=uf(21248, 16), tb=uf(21280, 16), sq=tB[:, 0:256])]
            PT = [ub(tb + i * 1024, 1024) for i in range(3)]
            Rt = uf(tb + 5120, 512)
            krT = ub(18432, 2048)
            ssq, sskv = ST[:, 32:48], ST[:, 48:64]
            wbuf, wkey = wst.get()
            lazy["early"] = True
            pendA0 = proj(wbuf, wkey, 416, xnt_tile(0), [("XNT", 0)])
            lazy["early"] = False
            op("pool", lambda e: e.memset(ST[:, 32:64], 0.0), [], ["ssA"])

            def bodyA(t, P, pk):
                x = t % 2
                T_ = setsA[x]
                hAb, sq = T_["hAb"], T_["sq"]
                op("act", lambda e: e.activation(out=sq[:, 0:256], in_=P[:, 0:256], func=AF.Square, accum_out=ST[:, 32 + t:33 + t]),
                   [], [pk, ("sqA", x), "ssA"])
                op("act", lambda e: e.activation(out=sq[:, 0:128], in_=P[:, 256:384], func=AF.Square, accum_out=ST[:, 48 + t:49 + t]),
                   [], [pk, ("sqA", x), "ssA"])
                op("dve", lambda e: e.tensor_copy(out=hAb[:, 0:384], in_=P[:, 0:384]), [], [pk, ("hAb", x)])
                op("dve", lambda e: e.tensor_copy(out=krf[:, t * 32:(t + 1) * 32], in_=P[:, 384:416]), [], [pk, "krf"])
                return hAb, x

            def transA(hAb, x):
                return transposes([(hAb[:, 0:128], 128), (hAb[:, 128:256], 128), (hAb[:, 256:384], 128)], [("hAb", x)])

            def evacA(t, TP, tk):
                copy_op(evac_eng(), qlT[:, :, t * 128:(t + 1) * 128], TP[:, 0:256].rearrange("p (c k) -> p c k", c=2), [], [tk, "qlT"])
                copy_op(evac_eng(), kvlT[:, t * 128:(t + 1) * 128], TP[:, 256:384], [], [tk, "kvlT"])

            pend = {0: pendA0}
            tpsA = {}
            for t in range(NT):
                if t + 1 < NT:
                    pend[t + 1] = proj(wbuf, wkey, 416, xnt_tile(t + 1), [("XNT", t + 1)])
                P, pk = pend.pop(t)
                hx = bodyA(t, P, pk)
                if t > 0:
                    evacA(t - 1, *tpsA.pop(t - 1))
                tpsA[t] = transA(*hx)
            evacA(NT - 1, *tpsA.pop(NT - 1))
            op("act", lambda e: e.activation(out=ssq, in_=ssq, func=AF.Ln, scale=1.0 / 256, bias=eps_t), ["ssA", "CF"], ["ssA"])
            op("act", lambda e: e.activation(out=ssq, in_=ssq, func=AF.Exp, scale=-0.5), [], ["ssA"])
            op("act", lambda e: e.activation(out=sskv, in_=sskv, func=AF.Ln, scale=1.0 / 128, bias=eps_t), ["CF"], ["ssA"])
            op("act", lambda e: e.activation(out=sskv, in_=sskv, func=AF.Exp, scale=-0.5), [], ["ssA"])
            kv_ = krf.rearrange("p (t x w) -> p t x w", t=NT, x=2)
            kd_ = krb.rearrange("p (t x w) -> p t x w", t=NT, x=2)
            cA = CF[:, CF_COSA:CF_COSA + 256].rearrange("p (t w) -> p t w", t=NT)
            sA = CF[:, CF_SINA:CF_SINA + 256].rearrange("p (t w) -> p t w", t=NT)
            a3 = tA[:, 0:256].rearrange("p (t w) -> p t w", t=NT)
            b3 = tB[:, 0:256].rearrange("p (t w) -> p t w", t=NT)
            kAq = [("sqA", 0), ("tA", 0)]
            kBq = [("sqA", 1), ("tB", 0)]
            op("pool", lambda e: e.tensor_tensor(out=a3, in0=kv_[:, :, 0, :], in1=cA, op=ALU.mult), ["krf", "CF"], kAq)
            op("pool", lambda e: e.tensor_tensor(out=b3, in0=kv_[:, :, 1, :], in1=sA, op=ALU.mult), ["krf", "CF"], kBq)
            op("pool", lambda e: e.tensor_tensor(out=kd_[:, :, 0, :], in0=a3, in1=b3, op=ALU.subtract), kAq + kBq, ["krb"])
            op("pool", lambda e: e.tensor_tensor(out=a3, in0=kv_[:, :, 0, :], in1=sA, op=ALU.mult), ["krf", "CF"], kAq)
            op("pool", lambda e: e.tensor_tensor(out=b3, in0=kv_[:, :, 1, :], in1=cA, op=ALU.mult), ["krf", "CF"], kBq)
            op("pool", lambda e: e.tensor_tensor(out=kd_[:, :, 1, :], in0=a3, in1=b3, op=ALU.add), kAq + kBq, ["krb"])
            for t4 in range(2):
                TP, tk = transposes([(krb[:, (t4 * 8 + i) * 32:(t4 * 8 + i + 1) * 32], 32) for i in range(8)], ["krb"])
                copy_op(evac_eng(), krT[0:32, t4 * 1024:(t4 + 1) * 1024], TP[0:32, :], [], [tk, "krT"])
            qf4 = uf(tb + 3072, 384)
            kvf4 = uf(tb + 3840, 512)
            qb4 = ub(20480, 384)
            kvb4 = ub(20864, 256)
            ra4 = uf(21120, 64)
            rb4 = uf(21248, 64)
            for h in range(4):
                hp = h % 2
                op("pool", lambda e: e.memset(Vh[:, :, (1 - hp) * 64:(1 - hp) * 64 + 64], 1.0), [], ["Vh"])
                copy_op("dve", kTh[64:96, :], krT[0:32, :], ["krT"], ["kTh"])
                rrg = {"r": 0}

                def stage1(s4):
                    rrg["r"] ^= 1
                    r = rrg["r"]
                    R = PSR[r]
                    keys = [("ps", 2 * r), ("ps", 2 * r + 1)]
                    for i in range(4):
                        t = 4 * s4 + i
                        for c in range(2):
                            op("pe", lambda e: e.matmul(R[:, i * 256:i * 256 + 96], lhsT=qlT[:, c, t * 128:(t + 1) * 128], rhs=WQ[:, c, h * 96:(h + 1) * 96],
                                                        start=(c == 0), stop=(c == 1)), ["qlT", "WQ"], keys)
                        op("pe", lambda e: e.matmul(R[:, i * 256 + 128:i * 256 + 256], lhsT=kvlT[:, t * 128:(t + 1) * 128], rhs=WKV[:, h * 128:(h + 1) * 128],
                                                    start=True, stop=True, skip_group_check=True), ["kvlT", "WKV"], keys)
                    return R, keys

                def stage2(s4, R, keys):
                    t0 = 4 * s4
                    R3 = R.rearrange("p (t c) -> p t c", t=4)
                    q3 = qf4.rearrange("p (t c) -> p t c", t=4)
                    kv3 = kvf4.rearrange("p (t c) -> p t c", t=4)
                    qb3 = qb4.rearrange("p (t c) -> p t c", t=4)
                    kb3 = kvb4.rearrange("p (t c) -> p t c", t=4)
                    rq = ST[:, 32 + t0:36 + t0].unsqueeze(2).to_broadcast([128, 4, 96])
                    rkv = ST[:, 48 + t0:52 + t0].unsqueeze(2).to_broadcast([128, 4, 128])
                    op("dve", lambda e: e.tensor_tensor(out=q3, in0=R3[:, :, 0:96], in1=rq, op=ALU.mult), ["ssA"], keys + ["qf4"])
                    op("dve", lambda e: e.tensor_tensor(out=kv3, in0=R3[:, :, 128:256], in1=rkv, op=ALU.mult), ["ssA"], keys + ["kvf4"])
                    op("pool", lambda e: e.tensor_copy(out=qb3[:, :, 0:64], in_=q3[:, :, 0:64]), ["qf4"], ["qb4"])
                    op("pool", lambda e: e.tensor_copy(out=kb3, in_=kv3[:, :, 0:64]), ["kvf4"], ["kvb4"])
                    op("act", lambda e: e.activation(out=Vh[:, t0:t0 + 4, hp * 64:hp * 64 + 64], in_=kv3[:, :, 64:128], func=AF.Copy), ["kvf4"], ["Vh"])
                    sv = qf4.rearrange("p (t c) -> p t c", t=4)[:, :, 64:96].rearrange("p t (x w) -> p t x w", x=2)
                    dv = qb4.rearrange("p (t c) -> p t c", t=4)[:, :, 64:96].rearrange("p t (x w) -> p t x w", x=2)
                    cA4 = CF[:, CF_COSA + t0 * 16:CF_COSA + (t0 + 4) * 16].rearrange("p (t w) -> p t w", t=4)
                    sA4 = CF[:, CF_SINA + t0 * 16:CF_SINA + (t0 + 4) * 16].rearrange("p (t w) -> p t w", t=4)
                    a = ra4.rearrange("p (t w) -> p t w", t=4)
                    b = rb4.rearrange("p (t w) -> p t w", t=4)
                    rd = ["qf4", "CF"]
                    op("pool", lambda e: e.tensor_tensor(out=a, in0=sv[:, :, 0, :], in1=cA4, op=ALU.mult), rd, ["ra4"])
                    op("dve", lambda e: e.tensor_tensor(out=b, in0=sv[:, :, 1, :], in1=sA4, op=ALU.mult), rd, ["rb4"])
                    op("dve", lambda e: e.tensor_tensor(out=dv[:, :, 0, :], in0=a, in1=b, op=ALU.subtract), ["ra4", "rb4"], ["qb4"])
                    op("pool", lambda e: e.tensor_tensor(out=a, in0=sv[:, :, 0, :], in1=sA4, op=ALU.mult), rd, ["ra4"])
                    op("dve", lambda e: e.tensor_tensor(out=b, in0=sv[:, :, 1, :], in1=cA4, op=ALU.mult), rd, ["rb4"])
                    op("pool", lambda e: e.tensor_tensor(out=dv[:, :, 1, :], in0=a, in1=b, op=ALU.add), ["ra4", "rb4"], ["qb4"])
                    return qb3, kb3

                def stageT(qb3, kb3):
                    return transposes([(qb3[:, i, :], 96) for i in range(4)] + [(kb3[:, i, :], 64) for i in range(4)], ["qb4", "kvb4"])

                def stageE(s4, TP, tk):
                    t0 = 4 * s4
                    copy_op(evac_eng(), qTk[0:96, t0 * 128:(t0 + 4) * 128], TP[0:96, 0:512], [], [tk, "qT"])
                    copy_op(evac_eng(), kTh[0:64, t0 * 128:(t0 + 4) * 128], TP[0:64, 512:1024], [], [tk, "kTh"])

                pend = {0: stage1(0), 1: stage1(1)}
                tps = {}
                R, keys = pend.pop(0)
                tps[0] = stageT(*stage2(0, R, keys))
                for s4 in range(1, 4):
                    if s4 + 1 < 4:
                        pend[s4 + 1] = stage1(s4 + 1)
                    R, keys = pend.pop(s4)
                    qk_ = stage2(s4, R, keys)
                    stageE(s4 - 1, *tps.pop(s4 - 1))
                    tps[s4] = stageT(*qk_)
                stageE(3, *tps.pop(3))
                hd = dict(qT=lambda q0, q1: qTk[0:96, q0:q1], kT=lambda b_: kTh[0:96, b_ * 128:(b_ + 1) * 128],
                          V=lambda b_: Vh[:, b_, :])

                def finA(hi, g, OT, okey, h_=h, hp_=hp):
                    norm_finish(OT, okey, hp_, g, Rt, 0 + h_ // 2)

                attention([hd], dense_blocks, 96 ** -0.5, finA, PT, ["qT", "kTh", "Vh"])
            fence()

            def rope_ops(x1, x2, o1, o2, cb_, sb_, a, b, reads, writes, sfx):
                rd = list(reads) + ["CF"]
                kA, kB = ("tA", sfx), ("tB", sfx)
                op("pool", lambda e: e.tensor_tensor(out=a, in0=x1, in1=cb_, op=ALU.mult), rd, [kA])
                op("dve", lambda e: e.tensor_tensor(out=b, in0=x2, in1=sb_, op=ALU.mult), rd, [kB])
                op("dve", lambda e: e.tensor_tensor(out=o1, in0=a, in1=b, op=ALU.subtract), [kA, kB], writes)
                op("pool", lambda e: e.tensor_tensor(out=a, in0=x1, in1=sb_, op=ALU.mult), rd, [kA])
                op("dve", lambda e: e.tensor_tensor(out=b, in0=x2, in1=cb_, op=ALU.mult), rd, [kB])
                op("pool", lambda e: e.tensor_tensor(out=o2, in0=a, in1=b, op=ALU.add), [kA, kB], writes)

            def gqa_phase(kind, j, g3=0, ACC=None):
                qT = ub(0, 4096).rearrange("p (h t) -> p h t", h=2)
                kT = ub(4096, 2048)
                V = ub(6144, 3072).rearrange("p (t c) -> p t c", t=NT)
                isD = kind == "D"
                o = 9216 + (8192 if isD else 0)
                sets = []
                for i in range(1 if isD else 2):
                    d_ = dict(hf=uf(o, 768), hb=ub(o + 1536, 768), tA=uf(o + 2304, 384), tB=uf(o + 3072, 384))
                    o += 3840
                    if kind == "B":
                        d_["sq"] = uf(o, 768)
                        o += 1536
                    sets.append(d_)
                o0 = 9216 + (8192 if isD else 0)
                PT = [ub(o0 + i * 1024, 1024) for i in range(3)]
                Rt = uf(o0 + 3072, 512) if isD else uf(o, 512)
                if isD:
                    o = o0 + 3072
                assert o + 1024 <= UW, o
                dil = (1, 4, 16)[g3] if isD else 1
                wbuf, wkey = wst.get()
                ncol = 256 if dil == 1 else 192
                rrg = {"r": 0}

                def stage1(s4):
                    rrg["r"] ^= 1
                    r = rrg["r"]
                    R = PSR[r]
                    keys = [("ps", 2 * r), ("ps", 2 * r + 1)]
                    for i in range(4):
                        t = 4 * s4 + i
                        for k in range(8):
                            op("pe", lambda e: e.matmul(R[:, i * 256:i * 256 + ncol], lhsT=XNT[:, k, t * 128:(t + 1) * 128], rhs=wbuf[:, k, 0:ncol],
                                                        start=(k == 0), stop=(k == 7)), [("XNT", t), wkey], keys)
                    return R, keys

                def stage2(s4, R, keys):
                    x = s4 % len(sets)
                    T_ = sets[x]
                    hf, hb, tA, tB = T_["hf"], T_["hb"], T_["tA"], T_["tB"]
                    khf, khb = ("hf", x), ("hb", x)
                    R3 = R.rearrange("p (t c) -> p t c", t=4)
                    hf3 = hf.rearrange("p (t c) -> p t c", t=4)
                    hb3 = hb.rearrange("p (t c) -> p t c", t=4)
                    t0 = 4 * s4
                    if kind == "B":
                        sq = T_["sq"]
                        s12 = ST[:, 32 + 12 * x:44 + 12 * x]
                        op("act", lambda e: e.activation(out=sq.rearrange("p (t c) -> p t c", t=4), in_=R3[:, :, 0:192], func=AF.Square), [], keys + [("sq", x)])
                        op("dve", lambda e: e.tensor_reduce(out=s12, in_=sq.rearrange("p (h c) -> p h c", h=12), axis=AX.X, op=ALU.add),
                           [("sq", x)], [("ss3", x)])
                        op("act", lambda e: e.activation(out=s12, in_=s12, func=AF.Ln, scale=1.0 / 64, bias=eps_t), ["CF"], [("ss3", x)])
                        op("act", lambda e: e.activation(out=s12, in_=s12, func=AF.Exp, scale=-0.5), [], [("ss3", x)])
                        op("dve", lambda e: e.tensor_tensor(
                            out=hf.rearrange("p (t h c) -> p t h c", t=4, h=3), in0=R3[:, :, 0:192].rearrange("p t (h c) -> p t h c", h=3),
                            in1=s12.rearrange("p (t h) -> p t h", t=4).unsqueeze(3).to_broadcast([128, 4, 3, 64]), op=ALU.mult), [("ss3", x)], keys + [khf])
                        gq = PF[:, PF_GB + l * 384: PF_GB + l * 384 + 128].unsqueeze(1).to_broadcast([128, 4, 128])
                        gk = PF[:, PF_GB + l * 384 + 256: PF_GB + l * 384 + 320].unsqueeze(1).to_broadcast([128, 4, 64])
                        op("pool", lambda e: e.tensor_tensor(out=hf3[:, :, 0:128], in0=hf3[:, :, 0:128], in1=gq, op=ALU.mult), ["PF"], [khf])
                        op("pool", lambda e: e.tensor_tensor(out=hf3[:, :, 128:192], in0=hf3[:, :, 128:192], in1=gk, op=ALU.mult), ["PF"], [khf])
                        sv = hf.rearrange("p (t h n x w) -> p t h n x w", t=4, h=3, n=2, x=2, w=16)
                        dv = hb.rearrange("p (t h n x w) -> p t h n x w", t=4, h=3, n=2, x=2, w=16)
                        cX = CF[:, CF_COSX + t0 * 32:CF_COSX + (t0 + 4) * 32].rearrange("p (t n w) -> p t n w", t=4, n=2)
                        sX = CF[:, CF_SINX + t0 * 32:CF_SINX + (t0 + 4) * 32].rearrange("p (t n w) -> p t n w", t=4, n=2)
                        a = tA[:, 0:192].rearrange("p (t h w) -> p t h w", t=4, h=3)
                        b = tB[:, 0:192].rearrange("p (t h w) -> p t h w", t=4, h=3)
                        a5 = tA.rearrange("p (t h n w) -> p t h n w", t=4, h=3, n=2)
                        b5 = tB.rearrange("p (t h n w) -> p t h n w", t=4, h=3, n=2)
                        rope_ops(sv[:, :, :, :, 0, :], sv[:, :, :, :, 1, :], dv[:, :, :, :, 0, :], dv[:, :, :, :, 1, :],
                                 cX.unsqueeze(2).to_broadcast([128, 4, 3, 2, 16]), sX.unsqueeze(2).to_broadcast([128, 4, 3, 2, 16]),
                                 a5, b5, [khf], [khb], x)
                    else:
                        copy_op("dve", hf3, R3[:, :, 0:192], [], keys + [khf])
                        sv = hf.rearrange("p (t h x w) -> p t h x w", t=4, h=3, x=2, w=32)
                        dv = hb.rearrange("p (t h x w) -> p t h x w", t=4, h=3, x=2, w=32)
                        c6 = CF[:, CF_COS64 + t0 * 32:CF_COS64 + (t0 + 4) * 32].rearrange("p (t w) -> p t w", t=4).unsqueeze(2).to_broadcast([128, 4, 3, 32])
                        s6 = CF[:, CF_SIN64 + t0 * 32:CF_SIN64 + (t0 + 4) * 32].rearrange("p (t w) -> p t w", t=4).unsqueeze(2).to_broadcast([128, 4, 3, 32])
                        a = tA.rearrange("p (t h w) -> p t h w", t=4, h=3)
                        b = tB.rearrange("p (t h w) -> p t h w", t=4, h=3)
                        rope_ops(sv[:, :, :, 0, :], sv[:, :, :, 1, :], dv[:, :, :, 0, :], dv[:, :, :, 1, :], c6, s6, a, b, [khf], [khb], x)
                    if dil == 1:
                        op("act", lambda e: e.activation(out=V[:, t0:t0 + 4, 0:64], in_=R3[:, :, 192:256], func=AF.Copy), [], keys + ["V"])
                        op("act", lambda e: e.activation(out=V[:, t0:t0 + 4, 128:192], in_=R3[:, :, 192:256], func=AF.Copy), [], keys + ["V"])
                    return hb3, khb

                def stageT(hb3, khb):
                    return transposes([(hb3[:, i, 0:128], 128) for i in range(4)] + [(hb3[:, i, 128:192], 64) for i in range(4)], [khb])

                def stageE(s4, TP, tk):
                    t0 = 4 * s4
                    if dil == 1:
                        copy_op(evac_eng(), qT[0:64, 0, t0 * 128:(t0 + 4) * 128], TP[0:64, 0:512], [], [tk, "qT"])
                        copy_op(evac_eng(), qT[64:128, 1, t0 * 128:(t0 + 4) * 128], TP[64:128, 0:512], [], [tk, "qT"])
                        copy_op(evac_eng(), kT[0:64, t0 * 128:(t0 + 4) * 128], TP[0:64, 512:1024], [], [tk, "kT"])
                        copy_op(evac_eng(), kT[64:128, t0 * 128:(t0 + 4) * 128], TP[0:64, 512:1024], [], [tk, "kT"])
                    else:
                        w4 = 512 // dil
                        for hh in range(2):
                            dq = qT[hh * 64:(hh + 1) * 64, hh, :].rearrange("p (r u) -> p r u", r=dil)[:, :, s4 * w4:(s4 + 1) * w4]
                            copy_op(evac_eng(), dq, TP[hh * 64:(hh + 1) * 64, 0:512].rearrange("p (u r) -> p r u", r=dil), [], [tk, "qT"])
                        for half in range(2):
                            dk = kT[half * 64:(half + 1) * 64, :].rearrange("p (r u) -> p r u", r=dil)[:, :, s4 * w4:(s4 + 1) * w4]
                            copy_op(evac_eng(), dk, TP[0:64, 512:1024].rearrange("p (u r) -> p r u", r=dil), [], [tk, "kT"])

                def vstage(m):
                    cs = NT // dil
                    r_, u0 = m // cs, (m % cs) * 128
                    rr["ot"] ^= 1
                    b = 4 + rr["ot"]
                    P = PSB[b]
                    for k in range(8):
                        lhs = XNT[:, k, :].rearrange("p (u r) -> p r u", r=dil)[:, r_, u0:u0 + 128]
                        op("pe", lambda e: e.matmul(P[:, 0:64], lhsT=lhs, rhs=wbuf[:, k, 192:256], start=(k == 0), stop=(k == 7)),
                           allx + [wkey], [("ps", b)])
                    op("act", lambda e: e.activation(out=V[:, m, 0:64], in_=P[:, 0:64], func=AF.Copy), [], [("ps", b), "V"])
                    op("act", lambda e: e.activation(out=V[:, m, 128:192], in_=P[:, 0:64], func=AF.Copy), [], [("ps", b), "V"])

                lazy["early"] = True
                pend = {0: stage1(0), 1: stage1(1)}
                lazy["early"] = False
                if kind == "B" and j == 0:
                    op("pool", lambda e: e.memset(V[:, :, 64:128], 1.0), [], ["V"])
                    op("pool", lambda e: e.memset(qT[64:128, 0, :], 0.0), [], ["qT"])
                    op("pool", lambda e: e.memset(qT[0:64, 1, :], 0.0), [], ["qT"])
                tps = {}
                R, keys = pend.pop(0)
                tps[0] = stageT(*stage2(0, R, keys))
                for s4 in range(1, 4):
                    if s4 + 1 < 4:
                        pend[s4 + 1] = stage1(s4 + 1)
                    if dil != 1:
                        for m in range(4 * (s4 - 1), 4 * s4):
                            vstage(m)
                    R, keys = pend.pop(s4)
                    hb3_, khb_ = stage2(s4, R, keys)
                    stageE(s4 - 1, *tps.pop(s4 - 1))
                    tps[s4] = stageT(hb3_, khb_)
                if dil != 1:
                    for m in range(12, 16):
                        vstage(m)
                stageE(3, *tps.pop(3))
                heads = [dict(qT=(lambda hh_: (lambda q0, q1: qT[:, hh_, q0:q1]))(hh),
                              kT=(lambda b_: kT[:, b_ * 128:(b_ + 1) * 128]),
                              V=(lambda hh_: (lambda b_: V[:, b_, hh_ * 64:hh_ * 64 + 128]))(hh)) for hh in range(2)]
                if kind == "B":
                    def fin(hi, g, OT, okey):
                        norm_finish(OT, okey, hi, g, Rt, 2 + j)
                    attention(heads, dense_blocks, 0.125, fin, PT, ["qT", "kT", "V"])
                elif kind == "C":
                    def fin(hi, g, OT, okey):
                        norm_finish(OT, okey, hi, g, [Rt, uf(o0 + 3072, 512)], 4 + j, esk_col=l * 4 + 2 * j + hi, use_act=True)
                    attention(heads, band_blocks(CB_MC, NT, (-1, 1)), 0.125, fin, PT, ["qT", "kT", "V"])
                else:
                    def fin(hi, g, OT, okey):
                        A_ = ACC[hi]
                        if dil == 1:
                            copy_op("dve", A_[:, g * 512:(g + 1) * 512], OT[:, :], [], [okey, ("ACC", hi)])
                        else:
                            if dil == 4:
                                dst = A_.rearrange("p (u r) -> p r u", r=4)[:, g, :]
                                src = OT[:, :]
                            else:
                                dst = A_.rearrange("p (u r) -> p r u", r=16)[:, 4 * g:4 * g + 4, :]
                                src = OT[:, :].rearrange("p (r u) -> p r u", r=4)
                            op("dve", lambda e: e.tensor_tensor(out=dst, in0=src, in1=dst, op=ALU.add), [], [okey, ("ACC", hi)])
                    attention(heads, band_blocks(CB_MD, NT // dil, (-1, 0, 1)), 0.125, fin, PT, ["qT", "kT", "V"])
                fence()
                return Rt

            for j in range(2):
                gqa_phase("B", j)
            for j in range(2):
                gqa_phase("C", j)
            for j in range(2):
                ACC = [uf(9216, 2048), uf(9216 + 4096, 2048)]
                for g3 in range(3):
                    Rt = gqa_phase("D", j, g3, ACC)
                for hi in range(2):
                    nr = slice(64 * hi, 64 * hi + 64)
                    zr = slice(64 * (1 - hi), 64 * (1 - hi) + 64)
                    A_ = ACC[hi]
                    for g in range(4):
                        cs_ = slice(g * 512, (g + 1) * 512)
                        Rg = [Rt, uf(9216 + 8192, 512)][g % 2]
                        rk = ("RtD", g % 2)
                        op("act", lambda e: e.activation(out=Rg[zr, :], in_=A_[zr, cs_], func=AF.Ln), [("ACC", hi)], [rk])
                        op("act", lambda e: e.activation(out=Rg[nr, :], in_=Rg[zr, :], func=AF.Exp, scale=-1.0), [], [rk])
                        op("dve", lambda e: e.tensor_tensor(out=Rg[nr, :], in0=A_[nr, cs_], in1=Rg[nr, :], op=ALU.mult), [("ACC", hi)], [rk])
                        op("pool", lambda e: e.tensor_tensor(out=YT[nr, 6 + j, cs_], in0=YT[nr, 6 + j, cs_], in1=Rg[nr, :], op=ALU.mult),
                           [rk], [("YT", 6 + j)])
                fence()

            if dbg and s == 0 and l == 0:
                S.dma("sp", lambda e: e.dma_start(out=dbg_d, in_=YT[:, :, :].rearrange("p a b -> p (a b)")),
                      reads=[("YT", p) for p in range(8)] + ["Uown"], writes=["dbg"], res="dbg")

            Macc = uf(0, 4096).rearrange("p (t c) -> p t c", t=4)
            mT = ub(8192, 4096).rearrange("p (k c) -> p k c", k=8)
            Gt = [uf(12288, 512), uf(13312, 512)]
            Tm = [uf(14336, 512), uf(15360, 512)]
            Mb = ub(16384, 1024)
            junkM = ub(18432, 1024)
            op("pool", lambda e: e.memset(ST[:, 0:16], 0.0), [], [("ss", g4) for g4 in range(4)])
            for G in range(4):
                for i in range(4):
                    for c in range(2):
                        wbuf, wkey = wst.get()
                        bbuf, bkey = bst.get()
                        for tt in range(4):
                            t = 4 * G + tt
                            Pg, pgk = proj(wbuf, wkey, 512, xnt_tile(t), [("XNT", t)])
                            rr["st"] ^= 1
                            ub_ = 2 + rr["st"]
                            Pu = PSB[ub_]
                            for pp in range(2):
                                op("pe", (lambda pp_, Pu_=Pu, t_=t, i_=i: (lambda e: e.matmul(
                                    Pu_[:, :], lhsT=YT[:, 2 * i_ + pp_, t_ * 128:(t_ + 1) * 128], rhs=bbuf[:, pp_, :],
                                    start=(pp_ == 0), stop=(pp_ == 1))))(pp), [("YT", 2 * i), ("YT", 2 * i + 1), bkey], [("ps", ub_)])
                            gs = (tt + c) % 2
                            op("act", (lambda Pg_=Pg, gs_=gs: (lambda e: e.activation(out=Gt[gs_], in_=Pg_[:, :], func=AF.Sigmoid)))(),
                               [], [pgk, ("Gt", gs)])
                            mdst = Macc[:, tt, c * 512:(c + 1) * 512]
                            if i == 0:
                                op("dve", (lambda Pu_=Pu, gs_=gs, md=mdst: (lambda e: e.tensor_tensor(out=md, in0=Pu_[:, :], in1=Gt[gs_], op=ALU.mult)))(),
                                   [("Gt", gs)], [("ps", ub_), ("Macc", tt, c)])
                            else:
                                op("dve", (lambda Pu_=Pu, gs_=gs: (lambda e: e.tensor_tensor(out=Tm[gs_], in0=Pu_[:, :], in1=Gt[gs_], op=ALU.mult)))(),
                                   [("Gt", gs)], [("ps", ub_), ("Tm", gs)])
                                op("pool", (lambda gs_=gs, md=mdst: (lambda e: e.tensor_tensor(out=md, in0=md, in1=Tm[gs_], op=ALU.add)))(),
                                   [("Tm", gs)], [("Macc", tt, c)])
                for tt in range(4):
                    op("act", (lambda tt_: (lambda e: e.activation(out=Mb, in_=Macc[:, tt_, :], func=AF.Copy)))(tt),
                       [("Macc", tt, 0), ("Macc", tt, 1)], ["Mb"])
                    TP, tk = transposes([(Mb[:, k * 128:(k + 1) * 128], 128) for k in range(8)], ["Mb"])
                    copy_op("dve", mT[:, :, tt * 128:(tt + 1) * 128], TP[:, :].rearrange("p (k c) -> p k c", k=8), [], [tk, "mT"])
                for c in range(2):
                    wbuf, wkey = wst.get()
                    for tt in range(4):
                        t = 4 * G + tt
                        rr["ot"] ^= 1
                        ob = 4 + rr["ot"]
                        Po = PSB[ob]
                        for k in range(8):
                            op("pe", (lambda k_, Po_=Po, tt_=tt: (lambda e: e.matmul(Po_[:, :], lhsT=mT[:, k_, tt_ * 128:(tt_ + 1) * 128], rhs=wbuf[:, k_, :],
                                                                                   start=(k_ == 0), stop=(k_ == 7))))(k), ["mT", wkey], [("ps", ob)])
                        xs = X[:, t, c * 512:(c + 1) * 512]
                        op("dve", (lambda Po_=Po, xs_=xs: (lambda e: e.tensor_tensor(out=xs_, in0=Po_[:, :], in1=xs_, op=ALU.add)))(),
                           [], [("ps", ob), ("X", t)])
                for tt in range(4):
                    t = 4 * G + tt
                    op("act", lambda e: e.activation(out=junkM, in_=X[:, t, :], func=AF.Square, accum_out=ST[:, t:t + 1]),
                       [("X", t)], ["junkM", ("ss", G)])
            fence()

        def final(s):
            junk = ub(0, 1024)
            gF = uf(1024, 1024)
            OUT = [uf(3072 + 2048 * i, 1024) for i in range(4)]
            ss = ST[:, 0:16]
            rstd = ST[:, 16:32]
            flush_fence()
            S.dma("sp", lambda e: e.dma_start(out=gF, in_=gfin_d), reads=["Uown"], writes=["gF"], res="gF")
            op("act", lambda e: e.activation(out=rstd, in_=ss, func=AF.Ln, scale=1.0 / D, bias=eps_t), [("ss", g4) for g4 in range(4)] + ["CF"], ["rstd"])
            op("act", lambda e: e.activation(out=rstd, in_=rstd, func=AF.Exp, scale=-0.5), [], ["rstd"])
            for t in range(NT):
                o = OUT[t % 4]
                op("dve", (lambda t_, o_: (lambda e: e.scalar_tensor_tensor(out=o_, in0=X[:, t_, :], scalar=ST[:, 16 + t_:17 + t_], in1=gF,
                                                                            op0=ALU.mult, op1=ALU.mult)))(t, o),
                   [("X", t), "rstd", "gF"], [("OUT", t % 4)])
                S.dma("sp", (lambda t_, o_: (lambda e: e.dma_start(out=y_d[s, t_ * 128:(t_ + 1) * 128, :], in_=o_)))(t, o),
                      reads=[("OUT", t % 4), "Uown"], writes=[("y", s, t)], res=("y", t % 4))
                if s + 1 < nseq:
                    S.dma("sp", (lambda t_: (lambda e: e.dma_start(out=X[:, t_, :], in_=x_d[s + 1, t_ * 128:(t_ + 1) * 128, :])))(t),
                          writes=[("X", t)], res=("X", t))
            fence()

        x_load(0)
        fence()
        for s in range(nseq):
            S.epoch = s
            for l in range(depth):
                layer(s, l)
            final(s)
        fin_res = [("y", i) for i in range(4)] + (["dbg"] if dbg else [])
        nsem = S.emit(nc, final_wait_res=fin_res)
    stats = {k: len(v) for k, v in S.q.items()}
    stats["sems"] = nsem
    return nc, stats


_CACHE = {}


def kernel(x_prompt, x_sample, norm_in, w_in, a_q_norm, w_q_up, a_kv_norm, w_kv_up,
           b_q_norm, b_k_norm, c_sink, w_branch, w_out, final_norm):
    f = np.float32
    xs = np.concatenate([np.asarray(x_prompt, f), np.asarray(x_sample, f)], axis=0)
    nseq = xs.shape[0] // NCORES
    if "nc" not in _CACHE:
        _CACHE["nc"] = build(nseq)[0]
    nc = _CACHE["nc"]
    cf, cb = _host_consts()
    pf = _host_params(np.asarray(norm_in, f), np.asarray(a_q_norm, f), np.asarray(a_kv_norm, f),
                      np.asarray(b_q_norm, f), np.asarray(b_k_norm, f), np.asarray(c_sink, f))
    gfin = np.ascontiguousarray(np.broadcast_to(np.asarray(final_norm, f)[None, :], (128, D)))
    shared = {
        "w_in": np.ascontiguousarray(np.asarray(w_in, f)),
        "w_q_up": np.ascontiguousarray(np.asarray(w_q_up, f)),
        "w_kv_up": np.ascontiguousarray(np.asarray(w_kv_up, f)),
        "w_branch": np.ascontiguousarray(np.asarray(w_branch, f).reshape(L, 1024, 1024)),
        "w_out": np.ascontiguousarray(np.asarray(w_out, f)),
        "gfin": gfin, "cf": cf, "cb": cb, "pf": pf,
    }
    in_maps = []
    for c in range(NCORES):
        m = dict(shared)
        m["x"] = np.ascontiguousarray(xs[c * nseq:(c + 1) * nseq])
        in_maps.append(m)
    res = run_bass_kernel_spmd(nc, in_maps, core_ids=list(range(NCORES)))
    ys = np.concatenate([np.asarray(r["y"], f) for r in res.results], axis=0)
    nb = x_prompt.shape[0]
    return (np.ascontiguousarray(ys[:nb]), np.ascontiguousarray(ys[nb:]))
```

```python
import math
from contextlib import ExitStack

import numpy as np
import ml_dtypes
import concourse.bass as bass
import concourse.mybir as mybir
from concourse.bass_utils import run_bass_kernel_spmd

F32 = mybir.dt.float32
BF16 = mybir.dt.bfloat16
ALU = mybir.AluOpType
AF = mybir.ActivationFunctionType
AX = mybir.AxisListType

T = 2048
D = 1024
NT = 16
NIN = 8096
L = 2
OFF_A, OFF_B, OFF_C, OFF_D, OFF_Z, OFF_MG = 0, 416, 928, 1440, 2976, 4000
EPS = 1e-6
NCORES = 8


class Op:
    __slots__ = ("eng", "fn", "idx", "waits", "signal", "sem", "count", "is_dma", "epoch", "deps")

    def __init__(self, eng, fn, is_dma):
        self.eng = eng
        self.fn = fn
        self.is_dma = is_dma
        self.waits = {}
        self.signal = False
        self.sem = None
        self.count = 0
        self.deps = ()


class _Rec:
    def __init__(self):
        self.call = None

    def __getattr__(self, name):
        def f(*a, **k):
            self.call = (name, a, k)
            return self
        return f


class Sched:
    def __init__(self):
        self.q = {k: [] for k in ("pe", "act", "dve", "pool", "sp")}
        self.last_w = {}
        self.readers = {}
        self.epoch = 0
        self.dma_res_count = {}

    def _add(self, eng, fn, reads, writes, is_dma, dma_res=None):
        rec = _Rec()
        fn(rec)
        call = rec.call
        assert call is not None
        fn = (lambda c: (lambda e: getattr(e, c[0])(*c[1], **c[2])))(call)
        op = Op(eng, fn, is_dma)
        op.epoch = self.epoch
        op.idx = len(self.q[eng])
        deps = set()
        for r in reads:
            w = self.last_w.get(r)
            if w is not None:
                deps.add(w)
        for w_ in writes:
            w = self.last_w.get(w_)
            if w is not None:
                deps.add(w)
            rl = self.readers.get(w_)
            if rl:
                deps.update(rl)
        for r in reads:
            self.readers.setdefault(r, []).append(op)
        for w_ in writes:
            self.last_w[w_] = op
            self.readers[w_] = []
        if is_dma:
            op.sem = ("dma", dma_res)
            c = self.dma_res_count.get(dma_res, 0) + 16
            self.dma_res_count[dma_res] = c
            op.count = c
        best = {}
        keep = []
        for d in deps:
            if d.is_dma:
                keep.append(d)
            else:
                b = best.get(d.eng)
                if b is None or d.idx > b.idx:
                    best[d.eng] = d
        keep.extend(best.values())
        op.deps = keep
        self.q[eng].append(op)
        return op

    def op(self, eng, fn, reads=(), writes=()):
        return self._add(eng, fn, reads, writes, False)

    def dma(self, eng, fn, reads=(), writes=(), res=None):
        return self._add(eng, fn, reads, writes, True, res)

    @staticmethod
    def _skip(d, op):
        if d.is_dma or op.is_dma or d.eng != op.eng:
            return False
        return d.eng == "pe" or (op.idx - d.idx) > 2

    def finalize(self):
        for ops in self.q.values():
            for op in ops:
                for d in op.deps:
                    if not d.is_dma and not self._skip(d, op):
                        d.signal = True
        cnt = {}
        for eng, ops in self.q.items():
            for op in ops:
                if not op.is_dma and op.signal:
                    key = ("eng", eng, op.epoch)
                    cnt[key] = cnt.get(key, 0) + 1
                    op.sem = key
                    op.count = cnt[key]
        sems = set()
        for eng, ops in self.q.items():
            seen = {}
            for op in ops:
                w = {}
                for d in op.deps:
                    if self._skip(d, op):
                        continue
                    if w.get(d.sem, 0) < d.count:
                        w[d.sem] = d.count
                for s, c in list(w.items()):
                    if seen.get(s, 0) >= c:
                        del w[s]
                    else:
                        seen[s] = c
                op.waits = w
                sems.update(w.keys())
                if op.is_dma or op.signal:
                    sems.add(op.sem)
        return sorted(sems, key=str)

    def emit(self, nc, final_wait_res=()):
        sems = self.finalize()
        with ExitStack() as es:
            h = {}
            for i, s in enumerate(sems):
                h[s] = es.enter_context(nc.semaphore("s%d" % i))
            block = es.enter_context(nc.Block())
            finals = [(("dma", r), self.dma_res_count[r]) for r in final_wait_res if r in self.dma_res_count]

            def run(engobj, ops, extra=()):
                for op in ops:
                    for s, c in op.waits.items():
                        engobj.wait_ge(h[s], c)
                    ins = op.fn(engobj)
                    if op.is_dma:
                        ins.then_inc(h[op.sem], 16)
                    elif op.signal:
                        ins.then_inc(h[op.sem], 1)
                for s, c in extra:
                    engobj.wait_ge(h[s], c)

            @block.sync
            def _(e):
                run(e, self.q["sp"], finals)

            @block.tensor
            def _(e):
                run(e, self.q["pe"])

            @block.scalar
            def _(e):
                run(e, self.q["act"])

            @block.vector
            def _(e):
                run(e, self.q["dve"])

            @block.gpsimd
            def _(e):
                run(e, self.q["pool"])
        return len(sems)


def _host_consts():
    pos = (np.arange(NT)[None, :] * 128 + np.arange(128)[:, None]).astype(np.float64)
    f64 = np.power(np.float32(10000.0), -np.arange(32, dtype=np.float32) * np.float32(2.0) / np.float32(64)).astype(np.float32)
    f32_ = np.power(np.float32(10000.0), -np.arange(16, dtype=np.float32) * np.float32(2.0) / np.float32(32)).astype(np.float32)
    pos = pos.astype(np.float32)
    a64 = (pos[:, :, None] * f64[None, None, :]).astype(np.float64)
    cos64, sin64 = np.cos(a64), np.sin(a64)
    aA = (pos[:, :, None] * f32_[None, None, :]).astype(np.float64)
    cosA, sinA = np.cos(aA), np.sin(aA)
    rows = np.floor(pos / 64)
    cols = pos - rows * 64
    ar = (rows[:, :, None] * f32_[None, None, :]).astype(np.float64)
    ac = (cols[:, :, None] * f32_[None, None, :]).astype(np.float64)
    cosX = np.stack([np.cos(ar), np.cos(ac)], axis=2)
    sinX = np.stack([np.sin(ar), np.sin(ac)], axis=2)
    cf = np.concatenate([
        cos64.reshape(128, -1), sin64.reshape(128, -1),
        cosA.reshape(128, -1), sinA.reshape(128, -1),
        cosX.reshape(128, -1), sinX.reshape(128, -1),
        np.full((128, 1), EPS),
    ], axis=1).astype(np.float32)
    ident = np.eye(128)
    ki = np.arange(128)[:, None]
    qq = np.arange(384)[None, :] - 128
    maskC = np.where(np.abs(qq - ki) <= 128, 0.0, -30000.0)
    maskD = np.where(np.abs(qq - ki) <= 64, 0.0, -30000.0)
    cb = np.concatenate([ident, maskC, maskD], axis=1).astype(ml_dtypes.bfloat16)
    return np.ascontiguousarray(cf), np.ascontiguousarray(cb)


CF_COS64, CF_SIN64, CF_COSA, CF_SINA, CF_COSX, CF_SINX, CF_EPS = 0, 512, 1024, 1280, 1536, 2048, 2560
NCF = 2561
CB_ID, CB_MC, CB_MD = 0, 128, 512
NCB = 896
PF_GB, PF_SINK, PF_GIN, PF_GQ, PF_GKV = 0, 768, 776, 792, 796
NPF = 798


def _host_params(norm_in, a_q_norm, a_kv_norm, b_q_norm, b_k_norm, c_sink):
    pf = np.zeros((128, NPF), np.float32)
    for l in range(L):
        gb = np.concatenate([np.tile(b_q_norm[l][None, :], (4, 1)), np.tile(b_k_norm[l][None, :], (2, 1))], 0)
        pf[:, PF_GB + l * 384: PF_GB + (l + 1) * 384] = gb.reshape(1, 384)
        pf[:, PF_SINK + l * 4: PF_SINK + (l + 1) * 4] = c_sink[l][None, :]
        pf[:, PF_GIN + l * 8: PF_GIN + (l + 1) * 8] = norm_in[l].reshape(8, 128).T
        pf[:, PF_GQ + l * 2: PF_GQ + (l + 1) * 2] = a_q_norm[l].reshape(2, 128).T
        pf[:, PF_GKV + l] = a_kv_norm[l]
    return pf


def build(nseq, depth=L, dbg=False):
    nc = bass.Bass("TRN2", target_bir_lowering=False)
    dram = {}

    def din(name, shape, dt=F32):
        dram[name] = nc.dram_tensor(name, list(shape), dt, kind="ExternalInput")
        return dram[name].ap()

    x_d = din("x", [nseq, T, D])
    win_d = din("w_in", [L, D, NIN])
    wq_d = din("w_q_up", [L, 256, 384])
    wkv_d = din("w_kv_up", [L, 128, 512])
    wbr_d = din("w_branch", [L, 1024, 1024])
    wout_d = din("w_out", [L, D, D])
    gfin_d = din("gfin", [128, D])
    cf_d = din("cf", [128, NCF])
    cb_d = din("cb", [128, NCB], BF16)
    pf_d = din("pf", [128, NPF])
    y_d = nc.dram_tensor("y", [nseq, T, D], F32, kind="ExternalOutput").ap()
    s_in = nc.dram_tensor("s_in", [L, D, NIN], BF16).ap()
    s_q = nc.dram_tensor("s_q", [L, 256, 384], BF16).ap()
    s_kv = nc.dram_tensor("s_kv", [L, 128, 512], BF16).ap()
    s_br = nc.dram_tensor("s_br", [L, 1024, 1024], BF16).ap()
    s_out = nc.dram_tensor("s_out", [L, D, D], BF16).ap()
    if dbg:
        dbg_d = nc.dram_tensor("dbg", [128, 8 * T], BF16, kind="ExternalOutput").ap()

    S = Sched()
    UW = 21504

    with ExitStack() as es:
        def sb(name, shape, dt):
            return es.enter_context(nc.sbuf_tensor(name, list(shape), dt))

        def ps(name, shape, dt):
            return es.enter_context(nc.psum_tensor(name, list(shape), dt))

        X = sb("X", [128, NT, D], F32)
        XNT = sb("XNT", [128, 8, T], BF16)
        YT = sb("YT", [128, 8, T], BF16)
        WB = [sb("WB%d" % i, [128, 8, 512], BF16) for i in range(2)]
        WBR = [sb("WBR%d" % i, [128, 2, 512], BF16) for i in range(2)]
        WQ = sb("WQ", [128, 2, 384], BF16)
        WKV = sb("WKV", [128, 512], BF16)
        CF = sb("CF", [128, NCF], F32)
        CB = sb("CB", [128, NCB], BF16)
        PF = sb("PF", [128, NPF], F32)
        ST = sb("ST", [128, 64], F32)
        ESK = sb("ESK", [128, 8], F32)
        DUM = sb("DUM", [128, 2], F32)
        U = sb("U", [128, UW], BF16)
        PSALL = ps("PSALL", [128, 4096], F32)
        PSB = [PSALL[:, i * 512:(i + 1) * 512] for i in range(6)]
        PSR = [PSALL[:, r * 1024:(r + 1) * 1024] for r in range(2)]
        TPB = [PSALL[:, (6 + i) * 512:(7 + i) * 512].bitcast(BF16) for i in range(2)]
        STR = [(PSALL[:, 0:1024], [("ps", 0), ("ps", 1)]), (PSALL[:, 1024:2048], [("ps", 2), ("ps", 3)]),
               (PSALL[:, 3072:4096], [("tp", 0), ("tp", 1)])]

        ident = CB[:, CB_ID:CB_ID + 128]
        eps_t = CF[:, CF_EPS:CF_EPS + 1]

        def ub(off, n):
            assert off + n <= UW, (off, n)
            return U[:, off:off + n]

        def uf(off, n):
            assert off % 2 == 0 and off + 2 * n <= UW, (off, n)
            return U[:, off:off + 2 * n].bitcast(F32)

        lazy = {"pending": False, "early": False}

        def op(eng, fn, reads=(), writes=()):
            if lazy["early"]:
                return S.op(eng, fn, list(reads), writes)
            if lazy["pending"]:
                flush_fence()
            return S.op(eng, fn, list(reads) + ["Uown"], writes)

        def fence():
            lazy["pending"] = True

        def flush_fence():
            if lazy["pending"]:
                lazy["pending"] = False
                S.op("pool", lambda e: e.memset(DUM[:, 0:1], 0.0), reads=(), writes=["Uown", "DUM"])

        SCR = ["scr0", "scr1", "scr2"]
        rr = {"ev": 0, "ps": 0, "tp": 0, "st": 0, "ot": 0}

        def evac_eng():
            rr["ev"] ^= 1
            return "act" if rr["ev"] else "dve"

        def copy_op(eng, out, in_, reads, writes):
            if eng == "act":
                return op("act", lambda e: e.activation(out=out, in_=in_, func=AF.Copy), reads, writes)
            return op(eng, lambda e: e.tensor_copy(out=out, in_=in_), reads, writes)

        def next_ps():
            rr["ps"] ^= 1
            return rr["ps"]

        def next_tp():
            rr["tp"] ^= 1
            return rr["tp"]

        class WStream:
            def __init__(self, bufs, tag):
                self.bufs = bufs
                self.tag = tag
                self.descs = []
                self.issued = 0
                self.i = 0

            def _issue(self, j):
                slot = j % len(self.bufs)
                for (dst_fn, src) in self.descs[j]:
                    dst = dst_fn(self.bufs[slot])
                    S.dma("sp", (lambda d, s: (lambda e: e.dma_start(out=d, in_=s)))(dst, src),
                          reads=SCR, writes=[(self.tag, slot)], res=(self.tag, slot))

            def get(self):
                n = len(self.bufs)
                while self.issued < min(len(self.descs), self.i + n):
                    self._issue(self.issued)
                    self.issued += 1
                slot = self.i % n
                self.i += 1
                return self.bufs[slot], (self.tag, slot)

        wst = WStream(WB, "WB")
        bst = WStream(WBR, "WBR")

        def win_cols(l, ranges):
            out = []
            for (doff, c0, n) in ranges:
                src = s_in[l, :, c0:c0 + n].rearrange("(k p) c -> p k c", p=128)
                out.append(((lambda o, nn: (lambda buf: buf[:, :, o:o + nn]))(doff, n), src))
            return out

        def build_descs():
            for _s in range(nseq):
                for l in range(depth):
                    wst.descs.append(win_cols(l, [(0, OFF_Z, 512)]))
                    wst.descs.append(win_cols(l, [(0, OFF_Z + 512, 512)]))
                    wst.descs.append(win_cols(l, [(0, OFF_A, 416)]))
                    for off in (OFF_B, OFF_C):
                        for j in range(2):
                            wst.descs.append(win_cols(l, [(0, off + 128 * j, 128), (128, off + 256 + 64 * j, 64),
                                                          (192, off + 384 + 64 * j, 64)]))
                    for j in range(2):
                        for g in range(3):
                            off = OFF_D + 512 * g
                            wst.descs.append(win_cols(l, [(0, off + 128 * j, 128), (128, off + 256 + 64 * j, 64),
                                                          (192, off + 384 + 64 * j, 64)]))
                    for G in range(4):
                        for i in range(4):
                            for c in range(2):
                                wst.descs.append(win_cols(l, [(0, OFF_MG + i * 1024 + c * 512, 512)]))
                                src = s_br[l, i * 256:(i + 1) * 256, c * 512:(c + 1) * 512].rearrange("(k p) c -> p k c", p=128)
                                bst.descs.append([((lambda buf: buf[:, :, :]), src)])
                        for c in range(2):
                            src = s_out[l, :, c * 512:(c + 1) * 512].rearrange("(k p) c -> p k c", p=128)
                            wst.descs.append([((lambda buf: buf[:, :, :]), src)])

        build_descs()

        S.dma("sp", lambda e: e.dma_start(out=CF[:], in_=cf_d), writes=["CF"], res="CF")
        S.dma("sp", lambda e: e.dma_start(out=CB[:], in_=cb_d), writes=["CB"], res="CB")
        S.dma("sp", lambda e: e.dma_start(out=PF[:], in_=pf_d), writes=["PF"], res="PF")
        op("act", lambda e: e.activation(out=ESK[:, 0:8], in_=PF[:, PF_SINK:PF_SINK + 8], func=AF.Exp), ["PF"], ["ESK"])

        def convert():
            stg = [uf(0, 2048), uf(4096, 2048), uf(8192, 2048)]
            stb = [ub(12288, 2048), ub(14336, 2048), ub(16384, 2048)]
            pieces = []
            for l in range(L):
                for k in range(8):
                    for c0 in range(0, NIN, 2048):
                        n = min(2048, NIN - c0)
                        pieces.append((win_d[l, k * 128:(k + 1) * 128, c0:c0 + n], s_in[l, k * 128:(k + 1) * 128, c0:c0 + n], n,
                                       PF[:, PF_GIN + l * 8 + k: PF_GIN + l * 8 + k + 1]))
                for k in range(2):
                    pieces.append((wq_d[l, k * 128:(k + 1) * 128, :], s_q[l, k * 128:(k + 1) * 128, :], 384,
                                   PF[:, PF_GQ + l * 2 + k: PF_GQ + l * 2 + k + 1]))
                pieces.append((wkv_d[l], s_kv[l], 512, PF[:, PF_GKV + l: PF_GKV + l + 1]))
                for k in range(8):
                    pieces.append((wbr_d[l, k * 128:(k + 1) * 128, :], s_br[l, k * 128:(k + 1) * 128, :], 1024, None))
                for k in range(8):
                    pieces.append((wout_d[l, k * 128:(k + 1) * 128, :], s_out[l, k * 128:(k + 1) * 128, :], 1024, None))
            engs = ["dve", "act"]
            for i, (src, dst, n, g) in enumerate(pieces):
                sl = i % 3
                a, b = stg[sl], stb[sl]
                S.dma("sp", (lambda a_, s_, n_: (lambda e: e.dma_start(out=a_[:, 0:n_], in_=s_)))(a, src, n),
                      reads=["Uown"], writes=[("stg", sl)], res=("stg", sl))
                eng = engs[i % 2]
                if g is None:
                    copy_op(eng, b[:, 0:n], a[:, 0:n], [("stg", sl)], [("stb", sl)])
                elif eng == "act":
                    op("act", (lambda a_, b_, n_, g_: (lambda e: e.activation(out=b_[:, 0:n_], in_=a_[:, 0:n_], func=AF.Copy, scale=g_)))(a, b, n, g),
                       [("stg", sl), "PF"], [("stb", sl)])
                else:
                    op(eng, (lambda a_, b_, n_, g_: (lambda e: e.tensor_scalar_mul(out=b_[:, 0:n_], in0=a_[:, 0:n_], scalar1=g_)))(a, b, n, g),
                       [("stg", sl), "PF"], [("stb", sl)])
                S.dma("pool", (lambda b_, d_, n_: (lambda e: e.dma_start(out=d_, in_=b_[:, 0:n_])))(b, dst, n),
                      reads=[("stb", sl), "Uown"], writes=[("scr", i)], res=("scr", sl))
                S.last_w["scr%d" % sl] = S.q["pool"][-1]

        convert()

        def x_load(s):
            for t in range(NT):
                S.dma("sp", (lambda t_: (lambda e: e.dma_start(out=X[:, t_, :], in_=x_d[s, t_ * 128:(t_ + 1) * 128, :])))(t),
                      writes=[("X", t)], res=("X", t))

        def proj(wbuf, wkey, ncols, lhs_fn, xkeys, nk=8, wcol0=0):
            b = next_ps()
            P = PSB[b]
            for k in range(nk):
                op("pe", (lambda k_: (lambda e: e.matmul(P[:, 0:ncols], lhsT=lhs_fn(k_), rhs=wbuf[:, k_, wcol0:wcol0 + ncols],
                                                         start=(k_ == 0), stop=(k_ == nk - 1))))(k),
                   list(xkeys) + [wkey], [("ps", b)])
            return P, ("ps", b)

        def xnt_tile(t):
            return lambda k: XNT[:, k, t * 128:(t + 1) * 128]

        def transposes(srcs, reads):
            b = next_tp()
            TP = TPB[b]
            for i, (ap, w) in enumerate(srcs):
                op("pe", (lambda i_, ap_, w_: (lambda e: e.transpose(out=TP[0:w_, i_ * 128:(i_ + 1) * 128], in_=ap_, identity=ident)))(i, ap, w),
                   list(reads) + ["CB"], [("tp", b)])
            return TP, ("tp", b)

        def rope(src, dst, cos, sin, H, nh, w, tA, tB, reads, writes, eng="pool", sfx=0):
            sv = src.rearrange("p (h n x w) -> p h n x w", h=H, n=nh, x=2, w=w)
            dv = dst.rearrange("p (h n x w) -> p h n x w", h=H, n=nh, x=2, w=w)
            shp = [128, H, nh, w]
            cb_ = cos.rearrange("p (n w) -> p n w", n=nh).unsqueeze(1).to_broadcast(shp)
            sb_ = sin.rearrange("p (n w) -> p n w", n=nh).unsqueeze(1).to_broadcast(shp)
            x1, x2 = sv[:, :, :, 0, :], sv[:, :, :, 1, :]
            a = tA.rearrange("p (h n w) -> p h n w", h=H, n=nh)
            b = tB.rearrange("p (h n w) -> p h n w", h=H, n=nh)
            rd = list(reads) + ["CF"]
            kA, kB = ("tA", sfx), ("tB", sfx)
            op(eng, lambda e: e.tensor_tensor(out=a, in0=x1, in1=cb_, op=ALU.mult), rd, [kA])
            op(eng, lambda e: e.tensor_tensor(out=b, in0=x2, in1=sb_, op=ALU.mult), rd, [kB])
            op(eng, lambda e: e.tensor_tensor(out=dv[:, :, :, 0, :], in0=a, in1=b, op=ALU.subtract), [kA, kB], writes)
            op(eng, lambda e: e.tensor_tensor(out=a, in0=x1, in1=sb_, op=ALU.mult), rd, [kA])
            op(eng, lambda e: e.tensor_tensor(out=b, in0=x2, in1=cb_, op=ALU.mult), rd, [kB])
            op(eng, lambda e: e.tensor_tensor(out=dv[:, :, :, 1, :], in0=a, in1=b, op=ALU.add), [kA, kB], writes)

        def attention(heads, blocks_fn, scale, finish, PT, reads):
            items = []
            for hi in range(len(heads)):
                for g in range(4):
                    bl = blocks_fn(g)
                    for bi, (b, qlo, qhi, mask) in enumerate(bl):
                        items.append((hi, g, bi, len(bl), b, qlo, qhi, mask))
            groups = []
            i = 0
            while i < len(items):
                if i + 1 < len(items) and (items[i][6] - items[i][5]) == (items[i + 1][6] - items[i + 1][5]):
                    groups.append([items[i], items[i + 1]])
                    i += 2
                else:
                    groups.append([items[i]])
                    i += 1
            ng = len(groups)
            assert len(PT) == 3

            def issue_qk(gi):
                R, rkeys = STR[gi % 3]
                for ii, (hi, g, bi, nb, b, qlo, qhi, mask) in enumerate(groups[gi]):
                    hd = heads[hi]
                    n = qhi - qlo
                    kT_, qT_ = hd["kT"](b), hd["qT"](qlo, qhi)
                    ml = mask or []
                    op("pe", lambda e: e.matmul(R[:, ii * 512:ii * 512 + n], lhsT=kT_, rhs=qT_, start=True, stop=True), reads, rkeys)
                    for mi, (co, mo, mw) in enumerate(ml):
                        op("pe", lambda e: e.matmul(R[:, ii * 512 + co:ii * 512 + co + mw], lhsT=ident, rhs=CB[:, mo:mo + mw],
                                                    start=False, stop=True, skip_group_check=True), ["CB"], rkeys)

            issue_qk(0)
            if ng > 1:
                issue_qk(1)
            ob = None
            for gi, grp in enumerate(groups):
                if gi + 2 < ng:
                    issue_qk(gi + 2)
                R, rkeys = STR[gi % 3]
                slot = gi % 3
                PTt = PT[slot]
                ns = [it[6] - it[5] for it in grp]
                if len(grp) == 2 and ns[0] == ns[1]:
                    n = ns[0]
                    op("act", lambda e: e.activation(out=PTt.rearrange("p (g c) -> p g c", g=2)[:, :, 0:n],
                                                     in_=R.rearrange("p (g c) -> p g c", g=2)[:, :, 0:n], func=AF.Exp, scale=scale),
                       [], rkeys + [("PT", slot)])
                else:
                    for ii, n in enumerate(ns):
                        op("act", lambda e: e.activation(out=PTt[:, ii * 512:ii * 512 + n], in_=R[:, ii * 512:ii * 512 + n], func=AF.Exp, scale=scale),
                           [], rkeys + [("PT", slot)])
                for ii, (hi, g, bi, nb, b, qlo, qhi, mask) in enumerate(grp):
                    n = ns[ii]
                    hd = heads[hi]
                    if bi == 0:
                        rr["ot"] ^= 1
                        ob = 4 + rr["ot"]
                    OT = PSB[ob]
                    c0 = qlo - 512 * g
                    V_ = hd["V"](b)
                    op("pe", lambda e: e.matmul(OT[:, c0:c0 + n], lhsT=V_, rhs=PTt[:, ii * 512:ii * 512 + n], start=(bi == 0), stop=(bi == nb - 1),
                                                skip_group_check=True),
                       list(reads) + [("PT", slot)], [("ps", ob)])
                    if bi == nb - 1:
                        finish(hi, g, OT, ("ps", ob))

        def norm_finish(OT, okey, hp, g, Rt, ypair, esk_col=None, use_act=False):
            rk = "Rt"
            if isinstance(Rt, list):
                rr["rt"] = rr.get("rt", 0) ^ 1
                rk = ("Rt", rr["rt"])
                Rt = Rt[rr["rt"]]
            nr = slice(64 * hp, 64 * hp + 64)
            zr = slice(64 * (1 - hp), 64 * (1 - hp) + 64)
            cols = slice(512 * g, 512 * (g + 1))
            if use_act:
                if esk_col is not None:
                    op("act", lambda e: e.activation(out=Rt[zr, :], in_=OT[zr, :], func=AF.Ln, bias=ESK[zr, esk_col:esk_col + 1]), ["ESK"], [okey, rk])
                else:
                    op("act", lambda e: e.activation(out=Rt[zr, :], in_=OT[zr, :], func=AF.Ln), [], [okey, rk])
                op("act", lambda e: e.activation(out=Rt[nr, :], in_=Rt[zr, :], func=AF.Exp, scale=-1.0), [], [rk])
            elif esk_col is not None:
                op("dve", lambda e: e.tensor_scalar_add(out=Rt[zr, :], in0=OT[zr, :], scalar1=ESK[zr, esk_col:esk_col + 1]),
                   ["ESK"], [okey, rk])
                op("dve", lambda e: e.reciprocal(out=Rt[nr, :], in_=Rt[zr, :]), [], [rk])
            else:
                op("dve", lambda e: e.reciprocal(out=Rt[nr, :], in_=OT[zr, :]), [], [okey, rk])
            op("dve", lambda e: e.tensor_tensor(out=Rt[nr, :], in0=OT[nr, :], in1=Rt[nr, :], op=ALU.mult), [], [okey, rk])
            op("pool", lambda e: e.tensor_tensor(out=YT[nr, ypair, cols], in0=YT[nr, ypair, cols], in1=Rt[nr, :], op=ALU.mult),
               [rk], [("YT", ypair)])

        def dense_blocks(g):
            return [(b, 512 * g, 512 * (g + 1), None) for b in range(NT)]

        def band_blocks(maskoff, cs, need):
            def f(g):
                out = []
                for b in range(NT):
                    c = b // cs
                    alo = max(b - 1, c * cs, 4 * g)
                    ahi = min(b + 1, c * cs + cs - 1, 4 * g + 3)
                    if alo > ahi:
                        continue
                    ml = []
                    for a in range(alo, ahi + 1):
                        if (a - b) in need:
                            co, mo = (a - alo) * 128, maskoff + (a - b + 1) * 128
                            if ml and ml[-1][0] + ml[-1][2] == co and ml[-1][1] + ml[-1][2] == mo:
                                ml[-1] = (ml[-1][0], ml[-1][1], ml[-1][2] + 128)
                            else:
                                ml.append((co, mo, 128))
                    out.append((b, alo * 128, (ahi + 1) * 128, ml))
                return out
            return f

        def layer(s, l):
            junk = ub(0, 1024)
            xn = [ub(1024, 1024), ub(2048, 1024)]
            ss = ST[:, 0:16]
            rstd = ST[:, 16:32]
            have_stats = l > 0
            if not have_stats:
                op("pool", lambda e: e.memset(ss, 0.0), [], [("ss", g4) for g4 in range(4)])

            def stats(g4):
                if not have_stats:
                    for t in range(4 * g4, 4 * g4 + 4):
                        op("act", lambda e: e.activation(out=junk, in_=X[:, t, :], func=AF.Square, accum_out=ST[:, t:t + 1]),
                           [("X", t)], ["junk", ("ss", g4)])
                r4 = ST[:, 16 + 4 * g4:20 + 4 * g4]
                op("act", lambda e: e.activation(out=r4, in_=ST[:, 4 * g4:4 * g4 + 4], func=AF.Ln, scale=1.0 / D, bias=eps_t),
                   [("ss", g4), "CF"], [("rstd", g4)])
                op("act", lambda e: e.activation(out=r4, in_=r4, func=AF.Exp, scale=-0.5), [], [("rstd", g4)])

            def normalize(g4):
                for t in range(4 * g4, 4 * g4 + 4):
                    xs = xn[t % 2]
                    op("dve", lambda e: e.tensor_scalar_mul(out=xs, in0=X[:, t, :], scalar1=ST[:, 16 + t:17 + t]),
                       [("X", t), ("rstd", g4)], [("xn", t % 2)])
                    TP, tk = transposes([(xs[:, k * 128:(k + 1) * 128], 128) for k in range(8)], [("xn", t % 2)])
                    copy_op(evac_eng(), XNT[:, :, t * 128:(t + 1) * 128], TP[:, :].rearrange("p (k c) -> p k c", k=8), [], [tk, ("XNT", t)])

            stats(0)
            for g4 in range(4):
                if g4 + 1 < 4:
                    stats(g4 + 1)
                normalize(g4)
            allx = [("XNT", t) for t in range(NT)]
            S.dma("sp", lambda e: e.dma_start(out=WQ[:], in_=s_q[l].rearrange("(k p) c -> p k c", p=128)), reads=SCR + ["WQ"], writes=["WQ"], res="WQ")
            S.dma("sp", lambda e: e.dma_start(out=WKV[:], in_=s_kv[l]), reads=SCR, writes=["WKV"], res="WKV")
            fence()

            lazy["early"] = True
            for half in range(2):
                wbuf, wkey = wst.get()
                for pp in range(4):
                    pair = half * 4 + pp
                    for g in range(4):
                        b = next_ps()
                        P = PSB[b]
                        for k in range(8):
                            op("pe", (lambda k_, P_=P, pp_=pp, g_=g: (lambda e: e.matmul(
                                P_[:, :], lhsT=wbuf[:, k_, pp_ * 128:(pp_ + 1) * 128], rhs=XNT[:, k_, g_ * 512:(g_ + 1) * 512],
                                start=(k_ == 0), stop=(k_ == 7))))(k), allx + [wkey], [("ps", b)])
                        op("act", (lambda P_=P, pair_=pair, g_=g: (lambda e: e.activation(
                            out=YT[:, pair_, g_ * 512:(g_ + 1) * 512], in_=P_[:, :], func=AF.Silu)))(), [], [("ps", b), ("YT", pair)])
            lazy["early"] = False

            qlT = ub(0, 4096).rearrange("p (c t) -> p c t", c=2)
            kvlT = ub(4096, 2048)
            qTk = ub(6144, 2048)
            kTh = ub(8192, 2048)
            Vh = ub(10240, 2048).rearrange("p (t c) -> p t c", t=NT)
            tb = 12288
            krf = uf(tb + 512, 512)
            krb = ub(tb + 1536, 512)
            tA = uf(tb + 2048, 256)
            tB = uf(tb + 2560, 256)
            setsA = [dict(hAb=ub(tb, 416), qf=uf(tb + 3072, 96), qb=ub(tb + 3264, 96), kvb=ub(tb + 3392, 64),
                          ta=uf(21312, 16), tb=uf(21344, 16), sq=tA[:, 0:256]),
                     dict(hAb=ub(20480, 416), qf=uf(20896, 96), qb=ub(21088, 96), kvb=ub(21184, 64),
                          ta=uf(21248, 16), tb=uf(21280, 16), sq=tB[:, 0:256])]
            PT = [ub(tb + i * 1024, 1024) for i in range(3)]
            Rt = uf(tb + 5120, 512)
            krT = ub(18432, 2048)
            ssq, sskv = ST[:, 32:48], ST[:, 48:64]
            wbuf, wkey = wst.get()
            lazy["early"] = True
            pendA0 = proj(wbuf, wkey, 416, xnt_tile(0), [("XNT", 0)])
            lazy["early"] = False
            op("pool", lambda e: e.memset(ST[:, 32:64], 0.0), [], ["ssA"])

            def bodyA(t, P, pk):
                x = t % 2
                T_ = setsA[x]
                hAb, sq = T_["hAb"], T_["sq"]
                op("act", lambda e: e.activation(out=sq[:, 0:256], in_=P[:, 0:256], func=AF.Square, accum_out=ST[:, 32 + t:33 + t]),
                   [], [pk, ("sqA", x), "ssA"])
                op("act", lambda e: e.activation(out=sq[:, 0:128], in_=P[:, 256:384], func=AF.Square, accum_out=ST[:, 48 + t:49 + t]),
                   [], [pk, ("sqA", x), "ssA"])
                op("dve", lambda e: e.tensor_copy(out=hAb[:, 0:384], in_=P[:, 0:384]), [], [pk, ("hAb", x)])
                op("dve", lambda e: e.tensor_copy(out=krf[:, t * 32:(t + 1) * 32], in_=P[:, 384:416]), [], [pk, "krf"])
                return hAb, x

            def transA(hAb, x):
                return transposes([(hAb[:, 0:128], 128), (hAb[:, 128:256], 128), (hAb[:, 256:384], 128)], [("hAb", x)])

            def evacA(t, TP, tk):
                copy_op(evac_eng(), qlT[:, :, t * 128:(t + 1) * 128], TP[:, 0:256].rearrange("p (c k) -> p c k", c=2), [], [tk, "qlT"])
                copy_op(evac_eng(), kvlT[:, t * 128:(t + 1) * 128], TP[:, 256:384], [], [tk, "kvlT"])

            pend = {0: pendA0}
            tpsA = {}
            for t in range(NT):
                if t + 1 < NT:
                    pend[t + 1] = proj(wbuf, wkey, 416, xnt_tile(t + 1), [("XNT", t + 1)])
                P, pk = pend.pop(t)
                hx = bodyA(t, P, pk)
                if t > 0:
                    evacA(t - 1, *tpsA.pop(t - 1))
                tpsA[t] = transA(*hx)
            evacA(NT - 1, *tpsA.pop(NT - 1))
            op("act", lambda e: e.activation(out=ssq, in_=ssq, func=AF.Ln, scale=1.0 / 256, bias=eps_t), ["ssA", "CF"], ["ssA"])
            op("act", lambda e: e.activation(out=ssq, in_=ssq, func=AF.Exp, scale=-0.5), [], ["ssA"])
            op("act", lambda e: e.activation(out=sskv, in_=sskv, func=AF.Ln, scale=1.0 / 128, bias=eps_t), ["CF"], ["ssA"])
            op("act", lambda e: e.activation(out=sskv, in_=sskv, func=AF.Exp, scale=-0.5), [], ["ssA"])
            kv_ = krf.rearrange("p (t x w) -> p t x w", t=NT, x=2)
            kd_ = krb.rearrange("p (t x w) -> p t x w", t=NT, x=2)
            cA = CF[:, CF_COSA:CF_COSA + 256].rearrange("p (t w) -> p t w", t=NT)
            sA = CF[:, CF_SINA:CF_SINA + 256].rearrange("p (t w) -> p t w", t=NT)
            a3 = tA[:, 0:256].rearrange("p (t w) -> p t w", t=NT)
            b3 = tB[:, 0:256].rearrange("p (t w) -> p t w", t=NT)
            kAq = [("sqA", 0), ("tA", 0)]
            kBq = [("sqA", 1), ("tB", 0)]
            op("pool", lambda e: e.tensor_tensor(out=a3, in0=kv_[:, :, 0, :], in1=cA, op=ALU.mult), ["krf", "CF"], kAq)
            op("pool", lambda e: e.tensor_tensor(out=b3, in0=kv_[:, :, 1, :], in1=sA, op=ALU.mult), ["krf", "CF"], kBq)
            op("pool", lambda e: e.tensor_tensor(out=kd_[:, :, 0, :], in0=a3, in1=b3, op=ALU.subtract), kAq + kBq, ["krb"])
            op("pool", lambda e: e.tensor_tensor(out=a3, in0=kv_[:, :, 0, :], in1=sA, op=ALU.mult), ["krf", "CF"], kAq)
            op("pool", lambda e: e.tensor_tensor(out=b3, in0=kv_[:, :, 1, :], in1=cA, op=ALU.mult), ["krf", "CF"], kBq)
            op("pool", lambda e: e.tensor_tensor(out=kd_[:, :, 1, :], in0=a3, in1=b3, op=ALU.add), kAq + kBq, ["krb"])
            for t4 in range(2):
                TP, tk = transposes([(krb[:, (t4 * 8 + i) * 32:(t4 * 8 + i + 1) * 32], 32) for i in range(8)], ["krb"])
                copy_op(evac_eng(), krT[0:32, t4 * 1024:(t4 + 1) * 1024], TP[0:32, :], [], [tk, "krT"])
            qf4 = uf(tb + 3072, 384)
            kvf4 = uf(tb + 3840, 512)
            qb4 = ub(20480, 384)
            kvb4 = ub(20864, 256)
            ra4 = uf(21120, 64)
            rb4 = uf(21248, 64)
            for h in range(4):
                hp = h % 2
                op("pool", lambda e: e.memset(Vh[:, :, (1 - hp) * 64:(1 - hp) * 64 + 64], 1.0), [], ["Vh"])
                copy_op("dve", kTh[64:96, :], krT[0:32, :], ["krT"], ["kTh"])
                rrg = {"r": 0}

                def stage1(s4):
                    rrg["r"] ^= 1
                    r = rrg["r"]
                    R = PSR[r]
                    keys = [("ps", 2 * r), ("ps", 2 * r + 1)]
                    for i in range(4):
                        t = 4 * s4 + i
                        for c in range(2):
                            op("pe", lambda e: e.matmul(R[:, i * 256:i * 256 + 96], lhsT=qlT[:, c, t * 128:(t + 1) * 128], rhs=WQ[:, c, h * 96:(h + 1) * 96],
                                                        start=(c == 0), stop=(c == 1)), ["qlT", "WQ"], keys)
                        op("pe", lambda e: e.matmul(R[:, i * 256 + 128:i * 256 + 256], lhsT=kvlT[:, t * 128:(t + 1) * 128], rhs=WKV[:, h * 128:(h + 1) * 128],
                                                    start=True, stop=True, skip_group_check=True), ["kvlT", "WKV"], keys)
                    return R, keys

                def stage2(s4, R, keys):
                    t0 = 4 * s4
                    R3 = R.rearrange("p (t c) -> p t c", t=4)
                    q3 = qf4.rearrange("p (t c) -> p t c", t=4)
                    kv3 = kvf4.rearrange("p (t c) -> p t c", t=4)
                    qb3 = qb4.rearrange("p (t c) -> p t c", t=4)
                    kb3 = kvb4.rearrange("p (t c) -> p t c", t=4)
                    rq = ST[:, 32 + t0:36 + t0].unsqueeze(2).to_broadcast([128, 4, 96])
                    rkv = ST[:, 48 + t0:52 + t0].unsqueeze(2).to_broadcast([128, 4, 128])
                    op("dve", lambda e: e.tensor_tensor(out=q3, in0=R3[:, :, 0:96], in1=rq, op=ALU.mult), ["ssA"], keys + ["qf4"])
                    op("dve", lambda e: e.tensor_tensor(out=kv3, in0=R3[:, :, 128:256], in1=rkv, op=ALU.mult), ["ssA"], keys + ["kvf4"])
                    op("pool", lambda e: e.tensor_copy(out=qb3[:, :, 0:64], in_=q3[:, :, 0:64]), ["qf4"], ["qb4"])
                    op("pool", lambda e: e.tensor_copy(out=kb3, in_=kv3[:, :, 0:64]), ["kvf4"], ["kvb4"])
                    op("act", lambda e: e.activation(out=Vh[:, t0:t0 + 4, hp * 64:hp * 64 + 64], in_=kv3[:, :, 64:128], func=AF.Copy), ["kvf4"], ["Vh"])
                    sv = qf4.rearrange("p (t c) -> p t c", t=4)[:, :, 64:96].rearrange("p t (x w) -> p t x w", x=2)
                    dv = qb4.rearrange("p (t c) -> p t c", t=4)[:, :, 64:96].rearrange("p t (x w) -> p t x w", x=2)
                    cA4 = CF[:, CF_COSA + t0 * 16:CF_COSA + (t0 + 4) * 16].rearrange("p (t w) -> p t w", t=4)
                    sA4 = CF[:, CF_SINA + t0 * 16:CF_SINA + (t0 + 4) * 16].rearrange("p (t w) -> p t w", t=4)
                    a = ra4.rearrange("p (t w) -> p t w", t=4)
                    b = rb4.rearrange("p (t w) -> p t w", t=4)
                    rd = ["qf4", "CF"]
                    op("pool", lambda e: e.tensor_tensor(out=a, in0=sv[:, :, 0, :], in1=cA4, op=ALU.mult), rd, ["ra4"])
                    op("dve", lambda e: e.tensor_tensor(out=b, in0=sv[:, :, 1, :], in1=sA4, op=ALU.mult), rd, ["rb4"])
                    op("dve", lambda e: e.tensor_tensor(out=dv[:, :, 0, :], in0=a, in1=b, op=ALU.subtract), ["ra4", "rb4"], ["qb4"])
                    op("pool", lambda e: e.tensor_tensor(out=a, in0=sv[:, :, 0, :], in1=sA4, op=ALU.mult), rd, ["ra4"])
                    op("dve", lambda e: e.tensor_tensor(out=b, in0=sv[:, :, 1, :], in1=cA4, op=ALU.mult), rd, ["rb4"])
                    op("pool", lambda e: e.tensor_tensor(out=dv[:, :, 1, :], in0=a, in1=b, op=ALU.add), ["ra4", "rb4"], ["qb4"])
                    return qb3, kb3

                def stageT(qb3, kb3):
                    return transposes([(qb3[:, i, :], 96) for i in range(4)] + [(kb3[:, i, :], 64) for i in range(4)], ["qb4", "kvb4"])

                def stageE(s4, TP, tk):
                    t0 = 4 * s4
                    copy_op(evac_eng(), qTk[0:96, t0 * 128:(t0 + 4) * 128], TP[0:96, 0:512], [], [tk, "qT"])
                    copy_op(evac_eng(), kTh[0:64, t0 * 128:(t0 + 4) * 128], TP[0:64, 512:1024], [], [tk, "kTh"])

                pend = {0: stage1(0), 1: stage1(1)}
                tps = {}
                R, keys = pend.pop(0)
                tps[0] = stageT(*stage2(0, R, keys))
                for s4 in range(1, 4):
                    if s4 + 1 < 4:
                        pend[s4 + 1] = stage1(s4 + 1)
                    R, keys = pend.pop(s4)
                    qk_ = stage2(s4, R, keys)
                    stageE(s4 - 1, *tps.pop(s4 - 1))
                    tps[s4] = stageT(*qk_)
                stageE(3, *tps.pop(3))
                hd = dict(qT=lambda q0, q1: qTk[0:96, q0:q1], kT=lambda b_: kTh[0:96, b_ * 128:(b_ + 1) * 128],
                          V=lambda b_: Vh[:, b_, :])

                def finA(hi, g, OT, okey, h_=h, hp_=hp):
                    norm_finish(OT, okey, hp_, g, Rt, 0 + h_ // 2)

                attention([hd], dense_blocks, 96 ** -0.5, finA, PT, ["qT", "kTh", "Vh"])
            fence()

            def rope_ops(x1, x2, o1, o2, cb_, sb_, a, b, reads, writes, sfx):
                rd = list(reads) + ["CF"]
                kA, kB = ("tA", sfx), ("tB", sfx)
                op("pool", lambda e: e.tensor_tensor(out=a, in0=x1, in1=cb_, op=ALU.mult), rd, [kA])
                op("dve", lambda e: e.tensor_tensor(out=b, in0=x2, in1=sb_, op=ALU.mult), rd, [kB])
                op("dve", lambda e: e.tensor_tensor(out=o1, in0=a, in1=b, op=ALU.subtract), [kA, kB], writes)
                op("pool", lambda e: e.tensor_tensor(out=a, in0=x1, in1=sb_, op=ALU.mult), rd, [kA])
                op("dve", lambda e: e.tensor_tensor(out=b, in0=x2, in1=cb_, op=ALU.mult), rd, [kB])
                op("pool", lambda e: e.tensor_tensor(out=o2, in0=a, in1=b, op=ALU.add), [kA, kB], writes)

            def gqa_phase(kind, j, g3=0, ACC=None):
                qT = ub(0, 4096).rearrange("p (h t) -> p h t", h=2)
                kT = ub(4096, 2048)
                V = ub(6144, 3072).rearrange("p (t c) -> p t c", t=NT)
                isD = kind == "D"
                o = 9216 + (8192 if isD else 0)
                sets = []
                for i in range(1 if isD else 2):
                    d_ = dict(hf=uf(o, 768), hb=ub(o + 1536, 768), tA=uf(o + 2304, 384), tB=uf(o + 3072, 384))
                    o += 3840
                    if kind == "B":
                        d_["sq"] = uf(o, 768)
                        o += 1536
                    sets.append(d_)
                o0 = 9216 + (8192 if isD else 0)
                PT = [ub(o0 + i * 1024, 1024) for i in range(3)]
                Rt = uf(o0 + 3072, 512) if isD else uf(o, 512)
                if isD:
                    o = o0 + 3072
                assert o + 1024 <= UW, o
                dil = (1, 4, 16)[g3] if isD else 1
                wbuf, wkey = wst.get()
                ncol = 256 if dil == 1 else 192
                rrg = {"r": 0}

                def stage1(s4):
                    rrg["r"] ^= 1
                    r = rrg["r"]
                    R = PSR[r]
                    keys = [("ps", 2 * r), ("ps", 2 * r + 1)]
                    for i in range(4):
                        t = 4 * s4 + i
                        for k in range(8):
                            op("pe", lambda e: e.matmul(R[:, i * 256:i * 256 + ncol], lhsT=XNT[:, k, t * 128:(t + 1) * 128], rhs=wbuf[:, k, 0:ncol],
                                                        start=(k == 0), stop=(k == 7)), [("XNT", t), wkey], keys)
                    return R, keys

                def stage2(s4, R, keys):
                    x = s4 % len(sets)
                    T_ = sets[x]
                    hf, hb, tA, tB = T_["hf"], T_["hb"], T_["tA"], T_["tB"]
                    khf, khb = ("hf", x), ("hb", x)
                    R3 = R.rearrange("p (t c) -> p t c", t=4)
                    hf3 = hf.rearrange("p (t c) -> p t c", t=4)
                    hb3 = hb.rearrange("p (t c) -> p t c", t=4)
                    t0 = 4 * s4
                    if kind == "B":
                        sq = T_["sq"]
                        s12 = ST[:, 32 + 12 * x:44 + 12 * x]
                        op("act", lambda e: e.activation(out=sq.rearrange("p (t c) -> p t c", t=4), in_=R3[:, :, 0:192], func=AF.Square), [], keys + [("sq", x)])
                        op("dve", lambda e: e.tensor_reduce(out=s12, in_=sq.rearrange("p (h c) -> p h c", h=12), axis=AX.X, op=ALU.add),
                           [("sq", x)], [("ss3", x)])
                        op("act", lambda e: e.activation(out=s12, in_=s12, func=AF.Ln, scale=1.0 / 64, bias=eps_t), ["CF"], [("ss3", x)])
                        op("act", lambda e: e.activation(out=s12, in_=s12, func=AF.Exp, scale=-0.5), [], [("ss3", x)])
                        op("dve", lambda e: e.tensor_tensor(
                            out=hf.rearrange("p (t h c) -> p t h c", t=4, h=3), in0=R3[:, :, 0:192].rearrange("p t (h c) -> p t h c", h=3),
                            in1=s12.rearrange("p (t h) -> p t h", t=4).unsqueeze(3).to_broadcast([128, 4, 3, 64]), op=ALU.mult), [("ss3", x)], keys + [khf])
                        gq = PF[:, PF_GB + l * 384: PF_GB + l * 384 + 128].unsqueeze(1).to_broadcast([128, 4, 128])
                        gk = PF[:, PF_GB + l * 384 + 256: PF_GB + l * 384 + 320].unsqueeze(1).to_broadcast([128, 4, 64])
                        op("pool", lambda e: e.tensor_tensor(out=hf3[:, :, 0:128], in0=hf3[:, :, 0:128], in1=gq, op=ALU.mult), ["PF"], [khf])
                        op("pool", lambda e: e.tensor_tensor(out=hf3[:, :, 128:192], in0=hf3[:, :, 128:192], in1=gk, op=ALU.mult), ["PF"], [khf])
                        sv = hf.rearrange("p (t h n x w) -> p t h n x w", t=4, h=3, n=2, x=2, w=16)
                        dv = hb.rearrange("p (t h n x w) -> p t h n x w", t=4, h=3, n=2, x=2, w=16)
                        cX = CF[:, CF_COSX + t0 * 32:CF_COSX + (t0 + 4) * 32].rearrange("p (t n w) -> p t n w", t=4, n=2)
                        sX = CF[:, CF_SINX + t0 * 32:CF_SINX + (t0 + 4) * 32].rearrange("p (t n w) -> p t n w", t=4, n=2)
                        a = tA[:, 0:192].rearrange("p (t h w) -> p t h w", t=4, h=3)
                        b = tB[:, 0:192].rearrange("p (t h w) -> p t h w", t=4, h=3)
                        a5 = tA.rearrange("p (t h n w) -> p t h n w", t=4, h=3, n=2)
                        b5 = tB.rearrange("p (t h n w) -> p t h n w", t=4, h=3, n=2)
                        rope_ops(sv[:, :, :, :, 0, :], sv[:, :, :, :, 1, :], dv[:, :, :, :, 0, :], dv[:, :, :, :, 1, :],
                                 cX.unsqueeze(2).to_broadcast([128, 4, 3, 2, 16]), sX.unsqueeze(2).to_broadcast([128, 4, 3, 2, 16]),
                                 a5, b5, [khf], [khb], x)
                    else:
                        copy_op("dve", hf3, R3[:, :, 0:192], [], keys + [khf])
                        sv = hf.rearrange("p (t h x w) -> p t h x w", t=4, h=3, x=2, w=32)
                        dv = hb.rearrange("p (t h x w) -> p t h x w", t=4, h=3, x=2, w=32)
                        c6 = CF[:, CF_COS64 + t0 * 32:CF_COS64 + (t0 + 4) * 32].rearrange("p (t w) -> p t w", t=4).unsqueeze(2).to_broadcast([128, 4, 3, 32])
                        s6 = CF[:, CF_SIN64 + t0 * 32:CF_SIN64 + (t0 + 4) * 32].rearrange("p (t w) -> p t w", t=4).unsqueeze(2).to_broadcast([128, 4, 3, 32])
                        a = tA.rearrange("p (t h w) -> p t h w", t=4, h=3)
                        b = tB.rearrange("p (t h w) -> p t h w", t=4, h=3)
                        rope_ops(sv[:, :, :, 0, :], sv[:, :, :, 1, :], dv[:, :, :, 0, :], dv[:, :, :, 1, :], c6, s6, a, b, [khf], [khb], x)
                    if dil == 1:
                        op("act", lambda e: e.activation(out=V[:, t0:t0 + 4, 0:64], in_=R3[:, :, 192:256], func=AF.Copy), [], keys + ["V"])
                        op("act", lambda e: e.activation(out=V[:, t0:t0 + 4, 128:192], in_=R3[:, :, 192:256], func=AF.Copy), [], keys + ["V"])
                    return hb3, khb

                def stageT(hb3, khb):
                    return transposes([(hb3[:, i, 0:128], 128) for i in range(4)] + [(hb3[:, i, 128:192], 64) for i in range(4)], [khb])

                def stageE(s4, TP, tk):
                    t0 = 4 * s4
                    if dil == 1:
                        copy_op(evac_eng(), qT[0:64, 0, t0 * 128:(t0 + 4) * 128], TP[0:64, 0:512], [], [tk, "qT"])
                        copy_op(evac_eng(), qT[64:128, 1, t0 * 128:(t0 + 4) * 128], TP[64:128, 0:512], [], [tk, "qT"])
                        copy_op(evac_eng(), kT[0:64, t0 * 128:(t0 + 4) * 128], TP[0:64, 512:1024], [], [tk, "kT"])
                        copy_op(evac_eng(), kT[64:128, t0 * 128:(t0 + 4) * 128], TP[0:64, 512:1024], [], [tk, "kT"])
                    else:
                        w4 = 512 // dil
                        for hh in range(2):
                            dq = qT[hh * 64:(hh + 1) * 64, hh, :].rearrange("p (r u) -> p r u", r=dil)[:, :, s4 * w4:(s4 + 1) * w4]
                            copy_op(evac_eng(), dq, TP[hh * 64:(hh + 1) * 64, 0:512].rearrange("p (u r) -> p r u", r=dil), [], [tk, "qT"])
                        for half in range(2):
                            dk = kT[half * 64:(half + 1) * 64, :].rearrange("p (r u) -> p r u", r=dil)[:, :, s4 * w4:(s4 + 1) * w4]
                            copy_op(evac_eng(), dk, TP[0:64, 512:1024].rearrange("p (u r) -> p r u", r=dil), [], [tk, "kT"])

                def vstage(m):
                    cs = NT // dil
                    r_, u0 = m // cs, (m % cs) * 128
                    rr["ot"] ^= 1
                    b = 4 + rr["ot"]
                    P = PSB[b]
                    for k in range(8):
                        lhs = XNT[:, k, :].rearrange("p (u r) -> p r u", r=dil)[:, r_, u0:u0 + 128]
                        op("pe", lambda e: e.matmul(P[:, 0:64], lhsT=lhs, rhs=wbuf[:, k, 192:256], start=(k == 0), stop=(k == 7)),
                           allx + [wkey], [("ps", b)])
                    op("act", lambda e: e.activation(out=V[:, m, 0:64], in_=P[:, 0:64], func=AF.Copy), [], [("ps", b), "V"])
                    op("act", lambda e: e.activation(out=V[:, m, 128:192], in_=P[:, 0:64], func=AF.Copy), [], [("ps", b), "V"])

                lazy["early"] = True
                pend = {0: stage1(0), 1: stage1(1)}
                lazy["early"] = False
                if kind == "B" and j == 0:
                    op("pool", lambda e: e.memset(V[:, :, 64:128], 1.0), [], ["V"])
                    op("pool", lambda e: e.memset(qT[64:128, 0, :], 0.0), [], ["qT"])
                    op("pool", lambda e: e.memset(qT[0:64, 1, :], 0.0), [], ["qT"])
                tps = {}
                R, keys = pend.pop(0)
                tps[0] = stageT(*stage2(0, R, keys))
                for s4 in range(1, 4):
                    if s4 + 1 < 4:
                        pend[s4 + 1] = stage1(s4 + 1)
                    if dil != 1:
                        for m in range(4 * (s4 - 1), 4 * s4):
                            vstage(m)
                    R, keys = pend.pop(s4)
                    hb3_, khb_ = stage2(s4, R, keys)
                    stageE(s4 - 1, *tps.pop(s4 - 1))
                    tps[s4] = stageT(hb3_, khb_)
                if dil != 1:
                    for m in range(12, 16):
                        vstage(m)
                stageE(3, *tps.pop(3))
                heads = [dict(qT=(lambda hh_: (lambda q0, q1: qT[:, hh_, q0:q1]))(hh),
                              kT=(lambda b_: kT[:, b_ * 128:(b_ + 1) * 128]),
                              V=(lambda hh_: (lambda b_: V[:, b_, hh_ * 64:hh_ * 64 + 128]))(hh)) for hh in range(2)]
                if kind == "B":
                    def fin(hi, g, OT, okey):
                        norm_finish(OT, okey, hi, g, Rt, 2 + j)
                    attention(heads, dense_blocks, 0.125, fin, PT, ["qT", "kT", "V"])
                elif kind == "C":
                    def fin(hi, g, OT, okey):
                        norm_finish(OT, okey, hi, g, [Rt, uf(o0 + 3072, 512)], 4 + j, esk_col=l * 4 + 2 * j + hi, use_act=True)
                    attention(heads, band_blocks(CB_MC, NT, (-1, 1)), 0.125, fin, PT, ["qT", "kT", "V"])
                else:
                    def fin(hi, g, OT, okey):
                        A_ = ACC[hi]
                        if dil == 1:
                            copy_op("dve", A_[:, g * 512:(g + 1) * 512], OT[:, :], [], [okey, ("ACC", hi)])
                        else:
                            if dil == 4:
                                dst = A_.rearrange("p (u r) -> p r u", r=4)[:, g, :]
                                src = OT[:, :]
                            else:
                                dst = A_.rearrange("p (u r) -> p r u", r=16)[:, 4 * g:4 * g + 4, :]
                                src = OT[:, :].rearrange("p (r u) -> p r u", r=4)
                            op("dve", lambda e: e.tensor_tensor(out=dst, in0=src, in1=dst, op=ALU.add), [], [okey, ("ACC", hi)])
                    attention(heads, band_blocks(CB_MD, NT // dil, (-1, 0, 1)), 0.125, fin, PT, ["qT", "kT", "V"])
                fence()
                return Rt

            for j in range(2):
                gqa_phase("B", j)
            for j in range(2):
                gqa_phase("C", j)
            for j in range(2):
                ACC = [uf(9216, 2048), uf(9216 + 4096, 2048)]
                for g3 in range(3):
                    Rt = gqa_phase("D", j, g3, ACC)
                for hi in range(2):
                    nr = slice(64 * hi, 64 * hi + 64)
                    zr = slice(64 * (1 - hi), 64 * (1 - hi) + 64)
                    A_ = ACC[hi]
                    for g in range(4):
                        cs_ = slice(g * 512, (g + 1) * 512)
                        Rg = [Rt, uf(9216 + 8192, 512)][g % 2]
                        rk = ("RtD", g % 2)
                        op("act", lambda e: e.activation(out=Rg[zr, :], in_=A_[zr, cs_], func=AF.Ln), [("ACC", hi)], [rk])
                        op("act", lambda e: e.activation(out=Rg[nr, :], in_=Rg[zr, :], func=AF.Exp, scale=-1.0), [], [rk])
                        op("dve", lambda e: e.tensor_tensor(out=Rg[nr, :], in0=A_[nr, cs_], in1=Rg[nr, :], op=ALU.mult), [("ACC", hi)], [rk])
                        op("pool", lambda e: e.tensor_tensor(out=YT[nr, 6 + j, cs_], in0=YT[nr, 6 + j, cs_], in1=Rg[nr, :], op=ALU.mult),
                           [rk], [("YT", 6 + j)])
                fence()

            if dbg and s == 0 and l == 0:
                S.dma("sp", lambda e: e.dma_start(out=dbg_d, in_=YT[:, :, :].rearrange("p a b -> p (a b)")),
                      reads=[("YT", p) for p in range(8)] + ["Uown"], writes=["dbg"], res="dbg")

            Macc = uf(0, 4096).rearrange("p (t c) -> p t c", t=4)
            mT = ub(8192, 4096).rearrange("p (k c) -> p k c", k=8)
            Gt = [uf(12288, 512), uf(13312, 512)]
            Tm = [uf(14336, 512), uf(15360, 512)]
            Mb = ub(16384, 1024)
            junkM = ub(18432, 1024)
            op("pool", lambda e: e.memset(ST[:, 0:16], 0.0), [], [("ss", g4) for g4 in range(4)])
            for G in range(4):
                for i in range(4):
                    for c in range(2):
                        wbuf, wkey = wst.get()
                        bbuf, bkey = bst.get()
                        for tt in range(4):
                            t = 4 * G + tt
                            Pg, pgk = proj(wbuf, wkey, 512, xnt_tile(t), [("XNT", t)])
                            rr["st"] ^= 1
                            ub_ = 2 + rr["st"]
                            Pu = PSB[ub_]
                            for pp in range(2):
                                op("pe", (lambda pp_, Pu_=Pu, t_=t, i_=i: (lambda e: e.matmul(
                                    Pu_[:, :], lhsT=YT[:, 2 * i_ + pp_, t_ * 128:(t_ + 1) * 128], rhs=bbuf[:, pp_, :],
                                    start=(pp_ == 0), stop=(pp_ == 1))))(pp), [("YT", 2 * i), ("YT", 2 * i + 1), bkey], [("ps", ub_)])
                            gs = (tt + c) % 2
                            op("act", (lambda Pg_=Pg, gs_=gs: (lambda e: e.activation(out=Gt[gs_], in_=Pg_[:, :], func=AF.Sigmoid)))(),
                               [], [pgk, ("Gt", gs)])
                            mdst = Macc[:, tt, c * 512:(c + 1) * 512]
                            if i == 0:
                                op("dve", (lambda Pu_=Pu, gs_=gs, md=mdst: (lambda e: e.tensor_tensor(out=md, in0=Pu_[:, :], in1=Gt[gs_], op=ALU.mult)))(),
                                   [("Gt", gs)], [("ps", ub_), ("Macc", tt, c)])
                            else:
                                op("dve", (lambda Pu_=Pu, gs_=gs: (lambda e: e.tensor_tensor(out=Tm[gs_], in0=Pu_[:, :], in1=Gt[gs_], op=ALU.mult)))(),
                                   [("Gt", gs)], [("ps", ub_), ("Tm", gs)])
                                op("pool", (lambda gs_=gs, md=mdst: (lambda e: e.tensor_tensor(out=md, in0=md, in1=Tm[gs_], op=ALU.add)))(),
                                   [("Tm", gs)], [("Macc", tt, c)])
                for tt in range(4):
                    op("act", (lambda tt_: (lambda e: e.activation(out=Mb, in_=Macc[:, tt_, :], func=AF.Copy)))(tt),
                       [("Macc", tt, 0), ("Macc", tt, 1)], ["Mb"])
                    TP, tk = transposes([(Mb[:, k * 128:(k + 1) * 128], 128) for k in range(8)], ["Mb"])
                    copy_op("dve", mT[:, :, tt * 128:(tt + 1) * 128], TP[:, :].rearrange("p (k c) -> p k c", k=8), [], [tk, "mT"])
                for c in range(2):
                    wbuf, wkey = wst.get()
                    for tt in range(4):
                        t = 4 * G + tt
                        rr["ot"] ^= 1
                        ob = 4 + rr["ot"]
                        Po = PSB[ob]
                        for k in range(8):
                            op("pe", (lambda k_, Po_=Po, tt_=tt: (lambda e: e.matmul(Po_[:, :], lhsT=mT[:, k_, tt_ * 128:(tt_ + 1) * 128], rhs=wbuf[:, k_, :],
                                                                                   start=(k_ == 0), stop=(k_ == 7))))(k), ["mT", wkey], [("ps", ob)])
                        xs = X[:, t, c * 512:(c + 1) * 512]
                        op("dve", (lambda Po_=Po, xs_=xs: (lambda e: e.tensor_tensor(out=xs_, in0=Po_[:, :], in1=xs_, op=ALU.add)))(),
                           [], [("ps", ob), ("X", t)])
                for tt in range(4):
                    t = 4 * G + tt
                    op("act", lambda e: e.activation(out=junkM, in_=X[:, t, :], func=AF.Square, accum_out=ST[:, t:t + 1]),
                       [("X", t)], ["junkM", ("ss", G)])
            fence()

        def final(s):
            junk = ub(0, 1024)
            gF = uf(1024, 1024)
            OUT = [uf(3072 + 2048 * i, 1024) for i in range(8)]
            ss = ST[:, 0:16]
            rstd = ST[:, 16:32]
            flush_fence()
            S.dma("sp", lambda e: e.dma_start(out=gF, in_=gfin_d), reads=["Uown"], writes=["gF"], res="gF")
            op("act", lambda e: e.activation(out=rstd, in_=ss, func=AF.Ln, scale=1.0 / D, bias=eps_t), [("ss", g4) for g4 in range(4)] + ["CF"], ["rstd"])
            op("act", lambda e: e.activation(out=rstd, in_=rstd, func=AF.Exp, scale=-0.5), [], ["rstd"])
            for t in range(NT):
                o = OUT[t % 8]
                op("dve", (lambda t_, o_: (lambda e: e.scalar_tensor_tensor(out=o_, in0=X[:, t_, :], scalar=ST[:, 16 + t_:17 + t_], in1=gF,
                                                                            op0=ALU.mult, op1=ALU.mult)))(t, o),
                   [("X", t), "rstd", "gF"], [("OUT", t % 8)])
                S.dma("sp", (lambda t_, o_: (lambda e: e.dma_start(out=y_d[s, t_ * 128:(t_ + 1) * 128, :], in_=o_)))(t, o),
                      reads=[("OUT", t % 8), "Uown"], writes=[("y", s, t)], res=("y", t % 8))
                if s + 1 < nseq:
                    S.dma("sp", (lambda t_: (lambda e: e.dma_start(out=X[:, t_, :], in_=x_d[s + 1, t_ * 128:(t_ + 1) * 128, :])))(t),
                          writes=[("X", t)], res=("X", t))
            fence()

        x_load(0)
        fence()
        for s in range(nseq):
            S.epoch = s
            for l in range(depth):
                layer(s, l)
            final(s)
        fin_res = [("y", i) for i in range(8)] + (["dbg"] if dbg else [])
        nsem = S.emit(nc, final_wait_res=fin_res)
    stats = {k: len(v) for k, v in S.q.items()}
    stats["sems"] = nsem
    return nc, stats


_CACHE = {}


def kernel(x_prompt, x_sample, norm_in, w_in, a_q_norm, w_q_up, a_kv_norm, w_kv_up,
           b_q_norm, b_k_norm, c_sink, w_branch, w_out, final_norm):
    f = np.float32
    xs = np.concatenate([np.asarray(x_prompt, f), np.asarray(x_sample, f)], axis=0)
    nseq = xs.shape[0] // NCORES
    if "nc" not in _CACHE:
        _CACHE["nc"] = build(nseq)[0]
    nc = _CACHE["nc"]
    cf, cb = _host_consts()
    pf = _host_params(np.asarray(norm_in, f), np.asarray(a_q_norm, f), np.asarray(a_kv_norm, f),
                      np.asarray(b_q_norm, f), np.asarray(b_k_norm, f), np.asarray(c_sink, f))
    gfin = np.ascontiguousarray(np.broadcast_to(np.asarray(final_norm, f)[None, :], (128, D)))
    shared = {
        "w_in": np.ascontiguousarray(np.asarray(w_in, f)),
        "w_q_up": np.ascontiguousarray(np.asarray(w_q_up, f)),
        "w_kv_up": np.ascontiguousarray(np.asarray(w_kv_up, f)),
        "w_branch": np.ascontiguousarray(np.asarray(w_branch, f).reshape(L, 1024, 1024)),
        "w_out": np.ascontiguousarray(np.asarray(w_out, f)),
        "gfin": gfin, "cf": cf, "cb": cb, "pf": pf,
    }
    in_maps = []
    for c in range(NCORES):
        m = dict(shared)
        m["x"] = np.ascontiguousarray(xs[c * nseq:(c + 1) * nseq])
        in_maps.append(m)
    res = run_bass_kernel_spmd(nc, in_maps, core_ids=list(range(NCORES)))
    ys = np.concatenate([np.asarray(r["y"], f) for r in res.results], axis=0)
    nb = x_prompt.shape[0]
    return (np.ascontiguousarray(ys[:nb]), np.ascontiguousarray(ys[nb:]))
```
